# Optimizing a Trainium2 kernel written in Bass

```python
import math
import jax, jax.numpy as jnp
from jax import lax
import numpy as np

D_MODEL = 2048
BATCH = 2
SEQ = 16384
DEPTH = 4

CHUNK = 64
QBLOCK = 128
N_MEM = 256
ATTN_WIDTH = D_MODEL // 2
SSM_WIDTH = D_MODEL - ATTN_WIDTH
DIFF_HEAD_DIM = 64
DIFF_V_DIM = 2 * DIFF_HEAD_DIM
DIFF_HEADS = ATTN_WIDTH // DIFF_V_DIM
SSM_GROUP = 16
SSM_GROUPS = SSM_WIDTH // SSM_GROUP
SSM_STATE = 64
XATTN_HEADS = 4
XATTN_HEAD_DIM = D_MODEL // XATTN_HEADS
D_FF = ((8 * D_MODEL // 3 + 255) // 256) * 256
IN_WIDTH = 3 * ATTN_WIDTH + SSM_WIDTH
ALPHA = (2 * DEPTH) ** 0.25
BETA = (8 * DEPTH) ** -0.25
LN_EPS = 1e-5
RMS_EPS = 1e-5
NEG_BIG = -1e30

kernel_name = "hymba_diffattn_s5_macaron_deepnorm"


def layer_norm(x, g, b):
    xf = x.astype(jnp.float32)
    mu = jnp.mean(xf, axis=-1, keepdims=True)
    var = jnp.mean(jnp.square(xf - mu), axis=-1, keepdims=True)
    y = (xf - mu) * lax.rsqrt(var + LN_EPS)
    return (y * g.astype(jnp.float32) + b.astype(jnp.float32)).astype(x.dtype)


def swiglu(x, w_gate, w_up, w_down):
    return (jax.nn.silu(x @ w_gate) * (x @ w_up)) @ w_down


def diff_attention(q, k, v, lam, norm_g, lam_init):
    b_, s_, h_, _, dh = q.shape
    nb = s_ // QBLOCK
    q_blocks = q.reshape(b_, nb, QBLOCK, h_, 2, dh).transpose(1, 0, 2, 3, 4, 5)
    key_chunk = jnp.arange(s_) // CHUNK
    scale = dh ** -0.5

    def one_block(args):
        q_blk, i = args
        s = jnp.einsum('bqhmd,bkhmd->bhmqk', q_blk, k,
                       preferred_element_type=jnp.float32) * scale
        q_chunk = (i * QBLOCK + jnp.arange(QBLOCK)) // CHUNK
        mask = key_chunk[None, :] <= q_chunk[:, None]
        s = jnp.where(mask, s, NEG_BIG)
        p = jax.nn.softmax(s, axis=-1)
        w = p[:, :, 0] - lam * p[:, :, 1]
        return jnp.einsum('bhqk,bkhd->bqhd', w.astype(v.dtype), v)

    o = lax.map(one_block, (q_blocks, jnp.arange(nb)))
    o = o.transpose(1, 0, 2, 3, 4).reshape(b_, s_, h_, v.shape[-1]).astype(jnp.float32)
    o = o * lax.rsqrt(jnp.mean(jnp.square(o), axis=-1, keepdims=True) + RMS_EPS)
    o = o * norm_g.astype(jnp.float32) * (1.0 - lam_init)
    return o.reshape(b_, s_, h_ * v.shape[-1]).astype(v.dtype)


def _linear_recurrence_op(e1, e2):
    a1, b1 = e1
    a2, b2 = e2
    return a1 * a2, a2 * b1 + b2


def s5_mixer(u, lam_re, lam_im, log_step, b_re, b_im, c_re, c_im, d_skip, glu_w, glu_b):
    f32 = jnp.float32
    b_, s_, _ = u.shape
    uf = u.astype(f32)
    lam = lax.complex(jnp.minimum(lam_re.astype(f32), -1e-4), lam_im.astype(f32))
    step = jnp.exp(log_step.astype(f32))[:, None]
    lam_dt = lam * step
    a_bar = jnp.exp(lam_dt)
    b_c = lax.complex(b_re.astype(f32), b_im.astype(f32))
    b_bar = ((a_bar - 1.0) / lam)[..., None] * b_c
    c_c = lax.complex(c_re.astype(f32), c_im.astype(f32))
    a_pow = jnp.exp(lam_dt[None] * jnp.arange(1, CHUNK + 1, dtype=f32)[:, None, None])
    a_seq = jnp.broadcast_to(a_bar, (b_, CHUNK, SSM_GROUPS, SSM_STATE))
    u_chunks = uf.reshape(b_, s_ // CHUNK, CHUNK, SSM_GROUPS, SSM_GROUP).transpose(1, 0, 2, 3, 4)

    def chunk_step(h, u_blk):
        bu = jnp.einsum('blgp,gnp->blgn', u_blk.astype(jnp.complex64), b_bar)
        _, h_loc = lax.associative_scan(_linear_recurrence_op, (a_seq, bu), axis=1)
        h_all = h_loc + a_pow[None] * h[:, None]
        y = jnp.einsum('blgn,gpn->blgp', h_all, c_c).real
        return h_all[:, -1], y

    h0 = jnp.zeros((b_, SSM_GROUPS, SSM_STATE), jnp.complex64)
    _, y = lax.scan(chunk_step, h0, u_chunks)
    y = y.transpose(1, 0, 2, 3, 4).reshape(b_, s_, SSM_WIDTH) + d_skip.astype(f32) * uf
    y = jax.nn.gelu(y)
    y = y * jax.nn.sigmoid(y @ glu_w.astype(f32) + glu_b.astype(f32))
    return y.astype(u.dtype)


def cross_attention(x, mem, w_q, w_k, w_v, w_o):
    b_, s_, _ = x.shape
    m_ = mem.shape[1]
    q = (x @ w_q).reshape(b_, s_, XATTN_HEADS, XATTN_HEAD_DIM)
    k = (mem @ w_k).reshape(b_, m_, XATTN_HEADS, XATTN_HEAD_DIM)
    v = (mem @ w_v).reshape(b_, m_, XATTN_HEADS, XATTN_HEAD_DIM)
    s = jnp.einsum('bshd,bmhd->bhsm', q, k, preferred_element_type=jnp.float32) * XATTN_HEAD_DIM ** -0.5
    p = jax.nn.softmax(s, axis=-1).astype(v.dtype)
    o = jnp.einsum('bhsm,bmhd->bshd', p, v).reshape(b_, s_, D_MODEL)
    return o @ w_o


def setup_inputs(seed: int = 0) -> dict:
    key = jax.random.key(seed)
    ks = list(jax.random.split(key, 48))
    it = iter(ks)
    L, D, F = DEPTH, D_MODEL, D_FF
    G, N, P = SSM_GROUPS, SSM_STATE, SSM_GROUP

    def nrm(shape, scale):
        return jax.random.normal(next(it), shape, jnp.float32) * scale

    def gain(shape):
        return 1.0 + nrm(shape, 0.02)

    col_scale = jnp.concatenate([jnp.ones((2 * ATTN_WIDTH,), jnp.float32),
                                 jnp.full((ATTN_WIDTH,), BETA, jnp.float32),
                                 jnp.ones((SSM_WIDTH,), jnp.float32)])
    inp = {}
    inp["x"] = nrm((BATCH, SEQ, D), 1.0)
    inp["mem"] = nrm((BATCH, N_MEM, D), 1.0)
    inp["ffn1_w_gate"] = nrm((L, D, F), D ** -0.5)
    inp["ffn1_w_up"] = nrm((L, D, F), D ** -0.5)
    inp["ffn1_w_down"] = nrm((L, F, D), BETA * F ** -0.5)
    inp["ln1_g"] = gain((L, D))
    inp["ln1_b"] = nrm((L, D), 0.02)
    inp["w_in"] = nrm((L, D, IN_WIDTH), D ** -0.5) * col_scale
    inp["lambda_q1"] = nrm((L, DIFF_HEAD_DIM), 0.1)
    inp["lambda_k1"] = nrm((L, DIFF_HEAD_DIM), 0.1)
    inp["lambda_q2"] = nrm((L, DIFF_HEAD_DIM), 0.1)
    inp["lambda_k2"] = nrm((L, DIFF_HEAD_DIM), 0.1)
    inp["diff_norm_g"] = gain((L, DIFF_V_DIM))
    inp["ssm_lambda_re"] = -0.5 + nrm((L, G, N), 0.01)
    inp["ssm_lambda_im"] = jnp.pi * jnp.arange(N, dtype=jnp.float32)[None, None, :] + nrm((L, G, N), 0.01)
    inp["ssm_log_step"] = jax.random.uniform(next(it), (L, G), jnp.float32,
                                             minval=math.log(1e-3), maxval=math.log(1e-1))
    inp["ssm_b_re"] = nrm((L, G, N, P), (2 * P) ** -0.5)
    inp["ssm_b_im"] = nrm((L, G, N, P), (2 * P) ** -0.5)
    inp["ssm_c_re"] = nrm((L, G, P, N), (2 * N) ** -0.5)
    inp["ssm_c_im"] = nrm((L, G, P, N), (2 * N) ** -0.5)
    inp["ssm_d"] = nrm((L, SSM_WIDTH), 1.0)
    inp["ssm_glu_w"] = nrm((L, SSM_WIDTH, SSM_WIDTH), SSM_WIDTH ** -0.5)
    inp["ssm_glu_b"] = nrm((L, SSM_WIDTH), 0.02)
    inp["w_out"] = nrm((L, D, D), BETA * D ** -0.5)
    inp["ln2_g"] = gain((L, D))
    inp["ln2_b"] = nrm((L, D), 0.02)
    inp["xattn_w_q"] = nrm((L, D, D), D ** -0.5)
    inp["xattn_w_k"] = nrm((L, D, D), D ** -0.5)
    inp["xattn_w_v"] = nrm((L, D, D), BETA * D ** -0.5)
    inp["xattn_w_o"] = nrm((L, D, D), BETA * D ** -0.5)
    inp["ln3_g"] = gain((L, D))
    inp["ln3_b"] = nrm((L, D), 0.02)
    inp["ffn2_w_gate"] = nrm((L, D, F), D ** -0.5)
    inp["ffn2_w_up"] = nrm((L, D, F), D ** -0.5)
    inp["ffn2_w_down"] = nrm((L, F, D), BETA * F ** -0.5)
    inp["ln4_g"] = gain((L, D))
    inp["ln4_b"] = nrm((L, D), 0.02)
    return inp


def reference(x, mem, ffn1_w_gate, ffn1_w_up, ffn1_w_down, ln1_g, ln1_b, w_in,
              lambda_q1, lambda_k1, lambda_q2, lambda_k2, diff_norm_g,
              ssm_lambda_re, ssm_lambda_im, ssm_log_step, ssm_b_re, ssm_b_im, ssm_c_re, ssm_c_im,
              ssm_d, ssm_glu_w, ssm_glu_b, w_out, ln2_g, ln2_b,
              xattn_w_q, xattn_w_k, xattn_w_v, xattn_w_o, ln3_g, ln3_b,
              ffn2_w_gate, ffn2_w_up, ffn2_w_down, ln4_g, ln4_b):
    b_, s_, _ = x.shape
    for l in range(DEPTH):
        lam_init = 0.8 - 0.6 * math.exp(-0.3 * l)
        x = layer_norm(ALPHA * x + 0.5 * swiglu(x, ffn1_w_gate[l], ffn1_w_up[l], ffn1_w_down[l]),
                       ln1_g[l], ln1_b[l])
        h = x @ w_in[l]
        q = h[..., :ATTN_WIDTH].reshape(b_, s_, DIFF_HEADS, 2, DIFF_HEAD_DIM)
        k = h[..., ATTN_WIDTH:2 * ATTN_WIDTH].reshape(b_, s_, DIFF_HEADS, 2, DIFF_HEAD_DIM)
        v = h[..., 2 * ATTN_WIDTH:3 * ATTN_WIDTH].reshape(b_, s_, DIFF_HEADS, DIFF_V_DIM)
        u = h[..., 3 * ATTN_WIDTH:]
        lam = (jnp.exp(jnp.sum(lambda_q1[l].astype(jnp.float32) * lambda_k1[l].astype(jnp.float32)))
               - jnp.exp(jnp.sum(lambda_q2[l].astype(jnp.float32) * lambda_k2[l].astype(jnp.float32)))
               + lam_init)
        attn_out = diff_attention(q, k, v, lam, diff_norm_g[l], lam_init)
        ssm_out = s5_mixer(u, ssm_lambda_re[l], ssm_lambda_im[l], ssm_log_step[l],
                           ssm_b_re[l], ssm_b_im[l], ssm_c_re[l], ssm_c_im[l],
                           ssm_d[l], ssm_glu_w[l], ssm_glu_b[l])
        mixed = jnp.concatenate([attn_out, ssm_out], axis=-1) @ w_out[l]
        x = layer_norm(ALPHA * x + mixed, ln2_g[l], ln2_b[l])
        x = layer_norm(ALPHA * x + cross_attention(x, mem, xattn_w_q[l], xattn_w_k[l],
                                                     xattn_w_v[l], xattn_w_o[l]),
                       ln3_g[l], ln3_b[l])
        x = layer_norm(ALPHA * x + 0.5 * swiglu(x, ffn2_w_gate[l], ffn2_w_up[l], ffn2_w_down[l]),
                       ln4_g[l], ln4_b[l])
    return x
```

```python
import math
import numpy as np
import concourse.bass as bass
import concourse.mybir as mybir
from concourse.bass_utils import run_bass_kernel_spmd

F32 = mybir.dt.float32
BF16 = mybir.dt.bfloat16
AF = mybir.ActivationFunctionType
OP = mybir.AluOpType
AX = mybir.AxisListType

D = 2048
DC = 16
FF = 5632
FC = 44
T = 512
NMEM = 256
DEPTH_FULL = 4
ALPHA = (2 * DEPTH_FULL) ** 0.25
LN_EPS = 1e-5
RMS_EPS = 1e-5
TWO_PI = 2.0 * math.pi
NEGM = -30000.0
WSLOT = 4096
NWSLOT = 3

WK = {
    "gu1": (16, 256, 44), "d1": (22, 128, 32), "win": (16, 256, 16), "glu": (8, 512, 2),
    "wout": (16, 256, 8), "xq": (16, 256, 8), "xk": (16, 256, 8), "xv": (16, 256, 8),
    "xo": (16, 256, 8), "gu2": (16, 256, 44), "d2": (22, 128, 32),
}
WORDER = ["gu1", "d1", "win", "xk", "xv", "glu", "wout", "xq", "xo", "gu2", "d2"]


class Buf:
    __slots__ = ("name", "w", "r")

    def __init__(self, name):
        self.name = name
        self.w = None
        self.r = {}


class Stream:
    def __init__(self, name):
        self.name = name
        self.ops = []
        self.count = 0
        self.known = {}
        self.dma_k = 0


NDMASEM = 8


class Emitter:
    def __init__(self):
        self.s = {n: Stream(n) for n in ("pe", "act", "dve", "pool", "sp")}
        self.cc_count = 0

    def _waits(self, st, reads, writes, extra=()):
        waits = {}

        def need(ev):
            if ev is None:
                return
            sem, val = ev
            if st.known.get(sem, 0) < val and waits.get(sem, 0) < val:
                waits[sem] = val

        for b in reads:
            need(b.w)
        for b in writes:
            need(b.w)
            for sem, val in b.r.items():
                need((sem, val))
        for ev in extra:
            need(ev)
        return waits

    def _finish(self, st, waits, fn, inc, ev, reads, writes):
        for sem, val in waits.items():
            st.known[sem] = val
        st.ops.append((list(waits.items()), fn, inc))
        if ev is not None:
            for b in reads:
                if b.r.get(ev[0], 0) < ev[1]:
                    b.r[ev[0]] = ev[1]
            for b in writes:
                b.w = ev
                b.r = {}

    def op(self, eng, fn, reads=(), writes=()):
        st = self.s[eng]
        waits = self._waits(st, reads, writes)
        if eng == "pe":
            waits.pop("pe", None)
        st.count += 1
        ev = (eng, st.count)
        self._finish(st, waits, fn, (eng, 1), ev, reads, writes)
        return ev

    def dma(self, eng, fn, reads=(), writes=()):
        st = self.s[eng]
        k = st.dma_k
        st.dma_k += 1
        sem = "%sq%d" % (eng, k % NDMASEM)
        extra = [(sem, 16 * (k // NDMASEM))] if k >= NDMASEM else []
        waits = self._waits(st, reads, writes, extra)
        ev = (sem, 16 * (k // NDMASEM + 1))
        self._finish(st, waits, fn, (sem, 16), ev, reads, writes)
        return ev

    def collective(self, fn, reads=(), writes=()):
        st = self.s["pool"]
        waits = self._waits(st, reads, writes)
        self.cc_count += 1
        ev = ("cc", self.cc_count)
        self._finish(st, waits, fn, ("cc", None), ev, reads, writes)
        return ev

    def barrier(self):
        evs = []
        for n, st in self.s.items():
            if st.count:
                evs.append((n, st.count))
            for i in range(min(st.dma_k, NDMASEM)):
                k = st.dma_k - 1 - i
                evs.append(("%sq%d" % (n, k % NDMASEM), 16 * (k // NDMASEM + 1)))
        if self.cc_count:
            evs.append(("cc", self.cc_count))
        for n, st in self.s.items():
            waits = {}
            for sem, val in evs:
                if sem == n:
                    continue
                if st.known.get(sem, 0) < val and waits.get(sem, 0) < val:
                    waits[sem] = val
            for sem, val in waits.items():
                st.known[sem] = val
            if waits:
                st.ops.append((list(waits.items()), None, None))


def build_nc(NB, depth, dbg=None):
    import os
    KSTOP = os.environ.get("KSTOP", "")
    NT = NB * T
    L = depth
    nc = bass.Bass("TRN2", target_bir_lowering=False)
    E = Emitter()

    def dram_in(name, shape, dt=F32):
        return nc.dram_tensor(name, list(shape), dt, kind="ExternalInput").ap()

    def dram_sc(name, shape, dt):
        return nc.dram_tensor(name, list(shape), dt).ap()

    xT = dram_in("xT", [D, NT])
    memT = dram_in("memT", [D, NMEM])
    wsrc = {}
    wbf = {}
    wB = {}
    for k, (Kc, gc, G) in WK.items():
        wsrc[k] = dram_in("w_" + k, [L, G * 128, Kc * gc])
        wbf[k] = dram_sc("wb_" + k, [L, G * 128, Kc * gc], BF16)
        for l in range(L):
            wB[(k, l)] = Buf("wb_%s_%d" % (k, l))
    lnp_d = dram_in("lnp", [128, L * 8 * 16])
    lamv_d = dram_in("lamv", [128, L * 4 * 64])
    dng_d = dram_in("dng", [128, L])
    ssmd_d = dram_in("ssmd", [128, L * 8])
    glub_d = dram_in("glub", [128, L * 8])
    spair_d = dram_in("spair", [128, L * 3 * 32])
    scm_d = dram_in("scm", [128, L * 3 * 512])
    sb_d = dram_in("ssmb", [128, L * 2 * 512])
    sc_d = dram_in("ssmc", [128, L * 2 * 512])
    iota_d = dram_in("iota1", [128, 512])
    kmask_d = dram_in("kmask", [32, 2048])
    qmask_d = dram_in("qmask", [32, 512])
    bmask_d = dram_in("bmask", [128, 4])
    sel_d = dram_in("sel", [128, 4])
    outT = nc.dram_tensor("outT", [D, NT], F32, kind="ExternalOutput").ap()

    xs = dram_sc("xs", [D, NT], F32)
    Qs = dram_sc("Qs", [1024, NT], BF16)
    Us = dram_sc("Us", [1024, NT], BF16)
    As = dram_sc("As", [1024, NT], BF16)
    kin_l = [[dram_sc("kin%d_%d" % (l, h), [128, NT], BF16) for h in range(8)] for l in range(L)]
    kout_l = [[dram_sc("kout%d_%d" % (l, h), [4 * 128, NT], BF16) for h in range(8)] for l in range(L)]
    vin_l = [[dram_sc("vin%d_%d" % (l, j), [T, 1024], BF16) for j in range(NB)] for l in range(L)]
    vout_l = [[dram_sc("vout%d_%d" % (l, j), [4 * T, 1024], BF16) for j in range(NB)] for l in range(L)]
    sin_l = [dram_sc("sin%d" % l, [NB, 8192], F32) for l in range(L)]
    sout_l = [dram_sc("sout%d" % l, [4 * NB, 8192], F32) for l in range(L)]
    tabs = dram_sc("tabs", [32, 128, 1024], F32)
    dbg_x2 = dbg_x3 = dbg_x1 = dbg_cat = None
    if dbg:
        dbg_x1 = nc.dram_tensor("d_x1", [D, NT], F32, kind="ExternalOutput").ap()
        dbg_x2 = nc.dram_tensor("d_x2", [D, NT], F32, kind="ExternalOutput").ap()
        dbg_x3 = nc.dram_tensor("d_x3", [D, NT], F32, kind="ExternalOutput").ap()
        dbg_cat = nc.dram_tensor("d_cat", [D, NT], BF16, kind="ExternalOutput").ap()

    import contextlib
    with contextlib.ExitStack() as ctx:
        def sb(name, shape, dt):
            return ctx.enter_context(nc.sbuf_tensor("s_" + name, list(shape), dt))

        xf = sb("xf", [128, 16, 512], F32)
        xb = sb("xb", [128, 16, 512], BF16)
        hb = sb("hb", [128, 44, 512], BF16)
        wbuf = [sb("wbuf%d" % i, [128, WSLOT], BF16) for i in range(NWSLOT)]
        WB64 = sb("WB64", [128, 8, 2, 2, 128], BF16)
        WC64 = sb("WC64", [128, 32, 2, 64], BF16)
        kmT = sb("kmT", [128, 16, 256], BF16)
        vmem = sb("vmem", [128, 2, 2048], BF16)
        pt = [[sb("pt%d%d" % (i, m), [128, 512], BF16) for m in range(2)] for i in range(2)]
        sg = [sb("sg%d" % i, [128, 512], BF16) for i in range(2)]
        ones = sb("ones", [128, 128], BF16)
        ones32 = sb("ones32", [128, 128], F32)
        kmask = sb("kmask", [32, 2048], BF16)
        qmask = sb("qmask", [32, 512], BF16)
        lnp = sb("lnp", [128, L * 8 * 16], F32)
        iota1 = sb("iota1", [128, 512], F32)
        lamv = sb("lamv", [128, L * 4 * 64], F32)
        dng = sb("dng", [128, L], F32)
        ssmd = sb("ssmd", [128, L * 8], F32)
        glub = sb("glub", [128, L * 8], F32)
        bmask = sb("bmask", [128, 4], F32)
        sel = sb("sel", [128, 4], F32)
        pp = sb("pp", [128, 12, 32], F32)
        hmine = sb("hmine", [128, NB, 64], F32)
        wend = sb("wend", [128, 64], F32)
        sm = sb("sm", [128, 16], F32)
        xpair = sb("xpair", [128, 4, 512], BF16)
        psum = [ctx.enter_context(nc.psum_tensor("ps%d" % i, [128, 512], F32)) for i in range(8)]

        sems = {}
        for n in ["pe", "act", "dve", "pool", "sp", "cc"]:
            sems[n] = ctx.enter_context(nc.semaphore("m_" + n))
        for q in ["pool", "sp"]:
            for i in range(NDMASEM):
                sems["%sq%d" % (q, i)] = ctx.enter_context(nc.semaphore("m_%sq%d" % (q, i)))

        xfB = [Buf("xf%d" % i) for i in range(16)]
        xbB = [Buf("xb%d" % i) for i in range(16)]
        hbB = [Buf("hb%d" % i) for i in range(44)]
        wbufB = [Buf("wbuf%d" % i) for i in range(NWSLOT)]
        psB = [Buf("ps%d" % i) for i in range(8)]
        ptB = [[Buf("pt"), Buf("pt")], [Buf("pt"), Buf("pt")]]
        sgB = [Buf("sg0"), Buf("sg1")]
        miscB = Buf("misc")
        ssmwB = Buf("ssmw")
        kvmB = Buf("kvm")
        ppB = Buf("pp")
        hmB = Buf("hmine")
        wendB = Buf("wend")
        xpB = [Buf("xp%d" % i) for i in range(4)]
        dramB = Buf("dram")

        xf2 = xf[:].rearrange("p c t -> p (c t)")
        xb2 = xb[:].rearrange("p c t -> p (c t)")
        hb2 = hb[:].rearrange("p c t -> p (c t)")
        S = [xf[:, i, :] for i in range(16)]
        SB_ = xfB
        yf = xb2.bitcast(F32).rearrange("p (c t) -> p c t", c=8)
        lnt = hb2[:, 32 * 512:42 * 512].bitcast(F32).rearrange("p (c t) -> p c t", c=5)
        lntB = [Buf("lnt%d" % i) for i in range(5)]

        ps_rr = [0]

        def bank():
            i = ps_rr[0]
            ps_rr[0] = (i + 1) % 8
            return i

        def mm(bi, pairs, reads, start=True, stop=True, out_ap=None):
            o = out_ap if out_ap is not None else psum[bi][:]
            n = len(pairs)

            def fn(pe, pairs=pairs, o=o, n=n, start=start, stop=stop):
                ins = None
                for i, (l_, r_) in enumerate(pairs):
                    ins = pe.matmul(o, l_, r_, start=(start and i == 0), stop=(stop and i == n - 1))
                return ins
            return E.op("pe", fn, reads=reads, writes=[psB[bi]])

        def act(out, in_, func, reads, writes, bias=None, scale=None):
            kw = {}
            if bias is not None:
                kw["bias"] = bias
            if scale is not None:
                kw["scale"] = scale
            return E.op("act", lambda e: e.activation(out=out, in_=in_, func=func, **kw), reads, writes)

        def tt(out, a, b, op, reads, writes):
            return E.op("dve", lambda e: e.tensor_tensor(out, a, b, op), reads, writes)

        def ts(out, a, s1, s2, op0, op1, reads, writes):
            if op1 is None:
                return E.op("dve", lambda e: e.tensor_scalar(out, a, s1, None, op0), reads, writes)
            return E.op("dve", lambda e: e.tensor_scalar(out, a, s1, s2, op0, op1), reads, writes)

        def stt(out, a, sc, b, op0, op1, reads, writes):
            return E.op("dve", lambda e: e.scalar_tensor_tensor(out, a, sc, b, op0, op1), reads, writes)

        def dma(q, out, in_, reads, writes):
            return E.dma(q, lambda e: e.dma_start(out=out, in_=in_), reads, writes)

        dma("sp", lnp[:], lnp_d, [], [miscB])
        dma("sp", iota1[:], iota_d, [], [miscB])
        dma("sp", lamv[:], lamv_d, [], [miscB])
        dma("sp", dng[:], dng_d, [], [miscB])
        dma("sp", ssmd[:], ssmd_d, [], [miscB])
        dma("sp", glub[:], glub_d, [], [miscB])
        dma("sp", bmask[:], bmask_d, [], [miscB])
        dma("sp", sel[:], sel_d, [], [miscB])
        dma("pool", kmask[:], kmask_d, [], [miscB])
        dma("pool", qmask[:], qmask_d, [], [miscB])
        E.op("dve", lambda e: e.memset(ones[:], 1.0), [], [miscB])
        E.op("dve", lambda e: e.memset(ones32[:], 1.0), [], [miscB])

        def convert_layer(l):
            for k in WORDER:
                Kc, gc, G = WK[k]
                rows = G * 128
                step = 1024
                for r0 in range(0, rows, step):
                    r1 = min(rows, r0 + step)
                    dma("pool", wbf[k][l, r0:r1, :], wsrc[k][l, r0:r1, :], [], [wB[(k, l)]])

        wslot = [0]

        def linear_ksplit(k, l, rhs_fn, rhs_bufs, out_fn):
            Kc, gc, G = WK[k]
            for oc in range(G // 2):
                pairs = []
                rds = []
                for half in range(2):
                    g = 2 * oc + half
                    s_ = wslot[0]
                    wslot[0] = (s_ + 1) % NWSLOT
                    wt = wbuf[s_]
                    dma("sp", wt[:, :Kc * gc], wbf[k][l, g * 128:(g + 1) * 128, :], [wB[(k, l)]], [wbufB[s_]])
                    wv = wt[:, :Kc * gc].rearrange("p (k m) -> p k m", k=Kc)
                    pairs += [(wv[:, kc, :], rhs_fn(half * Kc + kc)) for kc in range(Kc)]
                    rds.append(wbufB[s_])
                out_fn(oc, pairs, rds + rhs_bufs)

        def linear(k, l, rhs_fn, rhs_bufs, out_fn, swap=False, groups=None, ntok=None):
            Kc, gc, G = WK[k]
            for g in (groups if groups is not None else range(G)):
                s_ = wslot[0]
                wslot[0] = (s_ + 1) % NWSLOT
                wt = wbuf[s_]
                dma("sp", wt[:, :Kc * gc], wbf[k][l, g * 128:(g + 1) * 128, :], [wB[(k, l)]], [wbufB[s_]])
                wv = wt[:, :Kc * gc].rearrange("p (k m) -> p k m", k=Kc)
                if not swap:
                    for mi in range(gc // 128):
                        oc = g * (gc // 128) + mi
                        pairs = [(wv[:, kc, mi * 128:(mi + 1) * 128], rhs_fn(kc)) for kc in range(Kc)]
                        out_fn(oc, pairs, [wbufB[s_]] + rhs_bufs)
                else:
                    for tsi in range(ntok):
                        pairs = [(rhs_fn(kc, tsi), wv[:, kc, :]) for kc in range(Kc)]
                        out_fn(g, tsi, pairs, [wbufB[s_]] + rhs_bufs)

        def layer_norm(l, idx):
            g0 = (l * 8 + 2 * idx) * 16
            b0 = (l * 8 + 2 * idx + 1) * 16
            act(xb2, xf2, AF.Copy, xfB, xbB)
            act(hb2[:, 0:8192], xf2, AF.Square, xfB, hbB[0:16])
            b1 = bank()
            mm(b1, [(ones[:], xb[:, c, :]) for c in range(16)], xbB + [miscB])
            b2 = bank()
            mm(b2, [(ones[:], hb[:, c, :]) for c in range(16)], hbB[0:16] + [miscB])
            mt, vt, rstd, nmr = lnt[:, 0, :], lnt[:, 1, :], lnt[:, 2, :], lnt[:, 3, :]
            LB = hbB[32:42]
            ts(mt, psum[b1][:], 1.0 / D, None, OP.mult, None, [psB[b1]], LB + [lntB[0]])
            tt(vt, mt, mt, OP.mult, [lntB[0]], LB + [lntB[1]])
            stt(vt, psum[b2][:], 1.0 / D, vt, OP.mult, OP.subtract, [psB[b2], lntB[1]], LB + [lntB[1]])
            ts(vt, vt, LN_EPS, None, OP.add, None, [lntB[1]], LB + [lntB[1]])
            act(nmr, vt, AF.Sqrt, [lntB[1]], LB + [lntB[3]])
            E.op("dve", lambda e: e.reciprocal(rstd, nmr), [lntB[3]], LB + [lntB[2]])
            tt(nmr, rstd, rstd, OP.mult, [lntB[2]], LB + [lntB[3]])
            tt(nmr, nmr, vt, OP.mult, [lntB[3], lntB[1]], LB + [lntB[3]])
            ts(nmr, nmr, -0.5, 1.5, OP.mult, OP.add, [lntB[3]], LB + [lntB[3]])
            tt(rstd, rstd, nmr, OP.mult, [lntB[2], lntB[3]], LB + [lntB[2]])
            stt(nmr, mt, -1.0, rstd, OP.mult, OP.mult, [lntB[0], lntB[2]], LB + [lntB[3]])
            for c in range(16):
                tt(xf[:, c, :], xf[:, c, :], rstd, OP.mult, [xfB[c], lntB[2]], [xfB[c]])
                tt(xf[:, c, :], xf[:, c, :], nmr, OP.add, [xfB[c], lntB[3]], [xfB[c]])
                act(xb[:, c, :], xf[:, c, :], AF.Identity, [xfB[c], miscB], [xbB[c]],
                    bias=lnp[:, b0 + c:b0 + c + 1], scale=lnp[:, g0 + c:g0 + c + 1])
                act(xf[:, c, :], xf[:, c, :], AF.Identity, [xfB[c], miscB], [xfB[c]],
                    bias=lnp[:, b0 + c:b0 + c + 1], scale=lnp[:, g0 + c:g0 + c + 1])

        def ffn(l, kgu, kd):
            pend = {}

            def out_gu(oc, pairs, reads):
                fc, which = oc // 2, oc % 2
                b = bank()
                mm(b, pairs, reads)
                if which == 0:
                    pend[fc] = b
                else:
                    bg = pend.pop(fc)
                    si = fc % 2
                    act(sg[si][:], psum[bg][:], AF.Silu, [psB[bg]], [sgB[si]])
                    tt(hb[:, fc, :], sg[si][:], psum[b][:], OP.mult, [sgB[si], psB[b]], [hbB[fc]])

            linear(kgu, l, lambda kc: xb[:, kc, :], xbB, out_gu)

            def out_d(oc, pairs, reads):
                b = bank()
                mm(b, pairs, reads)
                stt(xf[:, oc, :], psum[b][:], 0.5, xf[:, oc, :], OP.mult, OP.add, [psB[b], xfB[oc]], [xfB[oc]])

            linear_ksplit(kd, l, lambda kc: hb[:, kc, :], hbB, out_d)

        for l in range(L):
            lam_init = 0.8 - 0.6 * math.exp(-0.3 * l)
            kin, kout, vin, vout, sin_, sout = kin_l[l], kout_l[l], vin_l[l], vout_l[l], sin_l[l], sout_l[l]
            if l == 0:
                convert_layer(0)
            if l + 1 < L:
                convert_layer(l + 1)
            xsrc = xT if l == 0 else xs

            lv = lamv[:, l * 256:(l + 1) * 256]
            t0_, t1_ = lnt[:, 4, 0:64], lnt[:, 4, 64:128]
            tB = [lntB[4]] + hbB[40:42]
            tt(t0_, lv[:, 0:64], lv[:, 64:128], OP.mult, [miscB], tB)
            E.op("dve", lambda e: e.reduce_sum(sm[:, 0:1], t0_, axis=AX.X), tB, [miscB])
            tt(t1_, lv[:, 128:192], lv[:, 192:256], OP.mult, [miscB], tB)
            E.op("dve", lambda e: e.reduce_sum(sm[:, 1:2], t1_, axis=AX.X), tB, [miscB])
            act(sm[:, 0:2], sm[:, 0:2], AF.Exp, [miscB], [miscB])
            stt(sm[:, 2:3], sm[:, 1:2], -lam_init, sm[:, 0:1], OP.add, OP.subtract, [miscB], [miscB])
            ts(sm[:, 3:4], dng[:, l:l + 1], 1.0 - lam_init, None, OP.mult, None, [miscB], [miscB])
            neglam = sm[:, 2:3]
            gsc = sm[:, 3:4]

            memb = hb2[:, 0:4096].rearrange("p (c t) -> p c t", c=16)
            dma("pool", memb, memT.rearrange("(c p) t -> p c t", p=128), [], hbB[0:8])

            def out_km(oc, pairs, reads):
                b = bank()
                mm(b, pairs, reads, out_ap=psum[b][:, 0:256])
                act(kmT[:, oc, :], psum[b][:, 0:256], AF.Copy, [psB[b]], [kvmB])
            linear("xk", l, lambda kc: memb[:, kc, :], hbB[0:8], out_km)

            def out_vm(g, tsi, pairs, reads):
                b = bank()
                mm(b, pairs, reads, out_ap=psum[b][:, 0:256])
                act(vmem[:, tsi, g * 256:(g + 1) * 256], psum[b][:, 0:256], AF.Copy, [psB[b]], [kvmB])
            linear("xv", l, lambda kc, tsi: memb[:, kc, tsi * 128:(tsi + 1) * 128], hbB[0:8], out_vm,
                   swap=True, ntok=2)

            def ld(i, src):
                dma("sp", S[i], src, [], [SB_[i]])
            scm_l = scm_d[:, l * 1536:(l + 1) * 1536]
            ld(0, scm_l[:, 0:512]); ld(1, scm_l[:, 512:1024]); ld(2, scm_l[:, 1024:1536])
            ld(3, sb_d[:, (2 * l) * 512:(2 * l + 1) * 512]); ld(4, sb_d[:, (2 * l + 1) * 512:(2 * l + 2) * 512])

            def exp_acc(out, x, xB, outB, t, tB, q, qB):
                cc_ = -4.6
                n = 24
                ts(t, x, -cc_, None, OP.add, None, xB, tB)
                ts(q, t, 1.0 / math.factorial(n), None, OP.mult, None, tB, qB)
                for k in range(n - 1, 0, -1):
                    stt(q, q, 1.0 / math.factorial(k), t, OP.add, OP.mult, qB + tB, qB)
                ts(out, q, 1.0, math.exp(cc_), OP.add, OP.mult, qB, outB)

            def expm1_small(out, x, xB, outB, q, qB):
                n = 7
                ts(q, x, 1.0 / math.factorial(n), None, OP.mult, None, xB, qB)
                for k in range(n - 1, 0, -1):
                    stt(q, q, 1.0 / math.factorial(k), x, OP.add, OP.mult, qB + xB, qB)
                ts(out, q, 1.0, None, OP.mult, None, qB, outB)

            def sin_acc(out, x, xB, outB, ta, taB, tb, tbB, tc, tcB):
                I32 = mybir.dt.int32
                ts(ta, x, 1.0 / TWO_PI, None, OP.mult, None, xB, taB)
                E.op("dve", lambda e: e.tensor_copy(tb.bitcast(I32), ta), taB, tbB)
                E.op("dve", lambda e: e.tensor_copy(ta, tb.bitcast(I32)), tbB, taB)
                stt(ta, ta, -TWO_PI, x, OP.mult, OP.add, taB + xB, taB)
                ts(tb, ta, math.pi, None, OP.is_gt, None, taB, tbB)
                stt(ta, tb, -TWO_PI, ta, OP.mult, OP.add, taB + tbB, taB)
                ts(tb, ta, -math.pi, None, OP.is_lt, None, taB, tbB)
                stt(ta, tb, TWO_PI, ta, OP.mult, OP.add, taB + tbB, taB)
                ts(tb, ta, -1.0, math.pi, OP.mult, OP.add, taB, tbB)
                tt(tb, ta, tb, OP.min, taB + tbB, tbB)
                ts(tc, ta, -1.0, -math.pi, OP.mult, OP.add, taB, tcB)
                tt(ta, tb, tc, OP.max, tbB + tcB, taB)
                tt(tb, ta, ta, OP.mult, taB, tbB)
                n = 7
                cf = [((-1.0) ** k) / math.factorial(2 * k + 1) for k in range(n + 1)]
                ts(tc, tb, cf[n], None, OP.mult, None, tbB, tcB)
                for k in range(n - 1, 0, -1):
                    stt(tc, tc, cf[k], tb, OP.add, OP.mult, tcB + tbB, tcB)
                stt(out, tc, 1.0, ta, OP.add, OP.mult, tcB + taB, outB)

            def sincos(x_ap, xB, sn_ap, snB, cs_ap, csB, ta, taB, tb, tbB):
                I32 = mybir.dt.int32
                ts(ta, x_ap, 1.0 / TWO_PI, None, OP.mult, None, xB, taB)
                E.op("dve", lambda e: e.tensor_copy(tb.bitcast(I32), ta), taB, tbB)
                E.op("dve", lambda e: e.tensor_copy(ta, tb.bitcast(I32)), tbB, taB)
                stt(ta, ta, -TWO_PI, x_ap, OP.mult, OP.add, taB + xB, taB)
                ts(tb, ta, math.pi, None, OP.is_gt, None, taB, tbB)
                stt(ta, tb, -TWO_PI, ta, OP.mult, OP.add, taB + tbB, taB)
                ts(tb, ta, -math.pi, None, OP.is_lt, None, taB, tbB)
                stt(ta, tb, TWO_PI, ta, OP.mult, OP.add, taB + tbB, taB)
                ts(ta, ta, math.pi, -math.pi, OP.min, OP.max, taB, taB)
                act(sn_ap, ta, AF.Sin, taB, snB)
                stt(tb, ta, -1.0, ta, OP.mult, OP.max, taB, tbB)
                act(cs_ap, tb, AF.Sin, tbB + [miscB], csB, bias=hpi_col, scale=-1.0)

            hpi_col = sm[:, 4:5]
            E.op("dve", lambda e: e.memset(sm[:, 4:5], math.pi / 2), [], [miscB])

            def B1(i):
                return [SB_[i]]
            exp_acc(S[2], S[2], B1(2), B1(2), S[9], B1(9), S[10], B1(10))
            ts(S[0], S[0], -1e-4, None, OP.min, None, B1(0), B1(0))
            tt(S[5], S[0], S[2], OP.mult, B1(0) + B1(2), B1(5))
            tt(S[6], S[1], S[2], OP.mult, B1(1) + B1(2), B1(6))
            expm1_small(S[5], S[5], B1(5), B1(5), S[9], B1(9))
            sin_acc(S[7], S[6], B1(6), B1(7), S[9], B1(9), S[10], B1(10), S[11], B1(11))
            ts(S[12], S[6], 0.5, None, OP.mult, None, B1(6), B1(12))
            sin_acc(S[8], S[12], B1(12), B1(8), S[9], B1(9), S[10], B1(10), S[11], B1(11))
            tt(S[8], S[8], S[8], OP.mult, B1(8), B1(8))
            ts(S[8], S[8], -2.0, None, OP.mult, None, B1(8), B1(8))
            ts(S[12], S[8], 1.0, None, OP.add, None, B1(8), B1(12))
            tt(S[9], S[5], S[12], OP.mult, B1(5) + B1(12), B1(9))
            tt(S[9], S[9], S[8], OP.add, B1(9) + B1(8), B1(9))
            stt(S[10], S[5], 1.0, S[7], OP.add, OP.mult, B1(5) + B1(7), B1(10))
            tt(S[11], S[0], S[0], OP.mult, B1(0), B1(11))
            tt(S[12], S[1], S[1], OP.mult, B1(1), B1(12))
            tt(S[11], S[11], S[12], OP.add, B1(11) + B1(12), B1(11))
            E.op("dve", lambda e: e.reciprocal(S[11], S[11]), B1(11), B1(11))
            tt(S[12], S[9], S[0], OP.mult, B1(9) + B1(0), B1(12))
            tt(S[13], S[10], S[1], OP.mult, B1(10) + B1(1), B1(13))
            tt(S[12], S[12], S[13], OP.add, B1(12) + B1(13), B1(12))
            tt(S[12], S[12], S[11], OP.mult, B1(12) + B1(11), B1(12))
            tt(S[13], S[10], S[0], OP.mult, B1(10) + B1(0), B1(13))
            tt(S[14], S[9], S[1], OP.mult, B1(9) + B1(1), B1(14))
            tt(S[13], S[13], S[14], OP.subtract, B1(13) + B1(14), B1(13))
            tt(S[13], S[13], S[11], OP.mult, B1(13) + B1(11), B1(13))
            tt(S[14], S[12], S[3], OP.mult, B1(12) + B1(3), B1(14))
            tt(S[15], S[13], S[4], OP.mult, B1(13) + B1(4), B1(15))
            tt(S[14], S[14], S[15], OP.subtract, B1(14) + B1(15), B1(14))
            tt(S[15], S[12], S[4], OP.mult, B1(12) + B1(4), B1(15))
            tt(S[9], S[13], S[3], OP.mult, B1(13) + B1(3), B1(9))
            tt(S[15], S[15], S[9], OP.add, B1(15) + B1(9), B1(15))
            for jj in range(2):
                for gi in range(2):
                    for reim in range(2):
                        src = S[14 + reim].rearrange("p (c n) -> p c n", c=8)
                        dst = WB64[:, :, jj, reim, gi * 64:(gi + 1) * 64]
                        mcol = bmask[:, jj * 2 + gi:jj * 2 + gi + 1]
                        ts(dst, src, mcol, None, OP.mult, None, B1(14 + reim) + [miscB], [ssmwB])
            E.op("dve", lambda e: e.memset(WC64[:].rearrange("p a b c -> p (a b c)"), 0.0), [], [ssmwB])
            ld(0, sc_d[:, (2 * l) * 512:(2 * l + 1) * 512]); ld(1, sc_d[:, (2 * l + 1) * 512:(2 * l + 2) * 512])
            WCv = WC64[:].rearrange("p (a b) r m -> p a b r m", b=2)
            for reim in range(2):
                cv = S[reim].rearrange("p (a b q) -> p a b q", a=16, b=2)
                for gi in range(2):
                    for jj in range(2):
                        dst = WCv[gi * 64:(gi + 1) * 64, :, jj, reim, jj * 32 + gi * 16:jj * 32 + gi * 16 + 16]
                        src = cv[gi * 64:(gi + 1) * 64, :, jj, :]
                        ts(dst, src, 1.0 if reim == 0 else -1.0, None, OP.mult, None, B1(reim), [ssmwB])
            dma("sp", pp[:, 0:3, :].rearrange("p a b -> p (a b)"), spair_d[:, l * 96:(l + 1) * 96], [], [ppB])
            PB = [ppB]
            exp_acc(pp[:, 2, :], pp[:, 2, :], PB, PB, pp[:, 10, :], PB, pp[:, 11, :], PB)
            ts(pp[:, 0, :], pp[:, 0, :], -1e-4, None, OP.min, None, PB, PB)
            tt(pp[:, 3, :], pp[:, 0, :], pp[:, 2, :], OP.mult, PB, PB)
            tt(pp[:, 4, :], pp[:, 1, :], pp[:, 2, :], OP.mult, PB, PB)
            expm1_small(pp[:, 5, :], pp[:, 3, :], PB, PB, pp[:, 10, :], PB)
            ts(pp[:, 5, :], pp[:, 5, :], 1.0, None, OP.add, None, PB, PB)
            I32 = mybir.dt.int32
            ts(pp[:, 10, :], pp[:, 4, :], 1.0 / TWO_PI, None, OP.mult, None, PB, PB)
            E.op("dve", lambda e: e.tensor_copy(pp[:, 11, :].bitcast(I32), pp[:, 10, :]), PB, PB)
            E.op("dve", lambda e: e.tensor_copy(pp[:, 10, :], pp[:, 11, :].bitcast(I32)), PB, PB)
            stt(pp[:, 4, :], pp[:, 10, :], -TWO_PI, pp[:, 4, :], OP.mult, OP.add, PB, PB)
            ts(pp[:, 10, :], pp[:, 4, :], 0.0, None, OP.is_lt, None, PB, PB)
            stt(pp[:, 4, :], pp[:, 10, :], TWO_PI, pp[:, 4, :], OP.mult, OP.add, PB, PB)
            ts(pp[:, 6, :], pp[:, 5, :], 1.0, None, OP.mult, None, PB, PB)
            for _sq in range(9):
                tt(pp[:, 6, :], pp[:, 6, :], pp[:, 6, :], OP.mult, PB, PB)
            ts(pp[:, 7, :], pp[:, 4, :], 512.0, None, OP.mult, None, PB, PB)
            sin_acc(pp[:, 8, :], pp[:, 7, :], PB, PB, pp[:, 10, :], PB, pp[:, 11, :], PB, pp[:, 3, :], PB)
            ts(pp[:, 9, :], pp[:, 7, :], 0.5, None, OP.mult, None, PB, PB)
            sin_acc(pp[:, 9, :], pp[:, 9, :], PB, PB, pp[:, 10, :], PB, pp[:, 11, :], PB, pp[:, 3, :], PB)
            tt(pp[:, 9, :], pp[:, 9, :], pp[:, 9, :], OP.mult, PB, PB)
            ts(pp[:, 9, :], pp[:, 9, :], -2.0, 1.0, OP.mult, OP.add, PB, PB)
            rcol = pp[:, 5, :]
            thr = pp[:, 4, :]
            for pr in range(32):
                o = 5 * (pr % 2)
                ts(S[o + 0], iota1[:], thr[:, pr:pr + 1], None, OP.mult, None, [miscB, ppB], B1(o + 0))
                sincos(S[o + 0], B1(o + 0), S[o + 1], B1(o + 1), S[o + 2], B1(o + 2),
                       S[o + 3], B1(o + 3), S[o + 4], B1(o + 4))
                dma("sp", tabs[pr, :, 0:512], S[o + 2], B1(o + 2), [dramB])
                dma("sp", tabs[pr, :, 512:1024], S[o + 1], B1(o + 1), [dramB])
            E.barrier()

            for j in range(NB):
                cols = slice(j * T, (j + 1) * T)
                dma("sp", xf[:], xsrc[:, cols].rearrange("(c p) t -> p c t", p=128), [dramB], xfB)
                act(xb2, xf2, AF.Copy, xfB, xbB)
                ts(xf2, xf2, ALPHA, None, OP.mult, None, xfB, xfB)
                ffn(l, "gu1", "d1")
                layer_norm(l, 0)
                dma("sp", xs[:, cols].rearrange("(c p) t -> p c t", p=128), xf[:], xfB, [dramB])

                def out_qku(oc, pairs, reads):
                    b = bank()
                    mm(b, pairs, reads)
                    slot = oc if oc < 16 else oc - 8
                    act(hb[:, slot, :], psum[b][:], AF.Copy, [psB[b]], [hbB[slot]])
                linear("win", l, lambda kc: xb[:, kc, :], xbB, out_qku, groups=[0, 1, 2, 3, 4, 5, 6, 7])
                linear("win", l, lambda kc: xb[:, kc, :], xbB, out_qku, groups=[12, 13, 14, 15])
                vst = hb2[:, 24 * 512:32 * 512].rearrange("p (s d) -> p s d", s=4)

                def out_v(g, tsi, pairs, reads):
                    b = bank()
                    mm(b, pairs, reads, out_ap=psum[b][:, 0:256])
                    gg = g - 8
                    act(vst[:, tsi, gg * 256:(gg + 1) * 256], psum[b][:, 0:256], AF.Copy, [psB[b]], hbB[24:32])
                linear("win", l, lambda kc, tsi: xb[:, kc, tsi * 128:(tsi + 1) * 128], xbB, out_v,
                       swap=True, ntok=4, groups=[8, 9, 10, 11])
                dma("sp", Qs[:, cols].rearrange("(c p) t -> p c t", p=128), hb[:, 0:8, :], hbB[0:8], [dramB])
                for h_ in range(8):
                    dma("sp", kin[h_][:, cols], hb[:, 8 + h_, :], [hbB[8 + h_]], [dramB])
                dma("sp", Us[:, cols].rearrange("(c p) t -> p c t", p=128), hb[:, 16:24, :], hbB[16:24], [dramB])
                dma("sp", vin[j].rearrange("(s p) d -> p s d", p=128), vst, hbB[24:32], [dramB])
            E.barrier()
            if KSTOP == "A":
                break

            for a_, b_ in list(zip(kin, kout)) + list(zip(vin, vout)):
                E.collective(lambda e, a=a_, b=b_: e.collective_compute(
                    "AllGather", OP.bypass, replica_groups=[[0, 1, 2, 3], [4, 5, 6, 7]],
                    ins=[a.opt()], outs=[b.opt()]), [], [dramB])
            if KSTOP == "X":
                E.barrier()
                break
            ut = hb[:, 16:24, :]
            utB = hbB[16:24]
            TBL = [(S[8], S[9]), (S[10], S[11])]
            TBLB = [(SB_[8], SB_[9]), (SB_[10], SB_[11])]

            ssm_rr = [0]

            def ssm_front(j, pr, init_re, init_im, initB):
                cc, j4 = pr // 4, pr % 4
                hf, jj = j4 // 2, j4 % 2
                ti = pr % 2
                cs, sn = TBL[ti]
                csB, snB = TBLB[ti]
                dma("sp", cs, tabs[pr, :, 0:512], [dramB], [csB])
                dma("sp", sn, tabs[pr, :, 512:1024], [dramB], [snB])
                bR = ssm_rr[0] % 6
                bI = (ssm_rr[0] + 1) % 6
                ssm_rr[0] += 2
                hs = slice(64 * hf, 64 * hf + 64)
                mm(bR, [(WB64[hs, cc, jj, 0, :], ut[hs, cc, :])], [ssmwB, utB[cc]])
                mm(bI, [(WB64[hs, cc, jj, 1, :], ut[hs, cc, :])], [ssmwB, utB[cc]])
                o = 4 * ti
                t1, t2, t3, t4 = S[o + 0], S[o + 1], S[o + 2], S[o + 3]
                tB_ = [SB_[o + 0], SB_[o + 1], SB_[o + 2], SB_[o + 3]]
                tt(t1, psum[bR][:], cs, OP.mult, [psB[bR], csB], [tB_[0]])
                tt(t2, psum[bI][:], sn, OP.mult, [psB[bI], snB], [tB_[1]])
                tt(t1, t1, t2, OP.add, [tB_[0], tB_[1]], [tB_[0]])
                tt(t3, psum[bI][:], cs, OP.mult, [psB[bI], csB], [tB_[2]])
                tt(t4, psum[bR][:], sn, OP.mult, [psB[bR], snB], [tB_[3]])
                tt(t3, t3, t4, OP.subtract, [tB_[2], tB_[3]], [tB_[2]])
                rb = rcol[:, pr:pr + 1].broadcast_to([128, 512])
                E.op("dve", lambda e: e.tensor_tensor_scan(t2, rb, t1, init_re, OP.mult, OP.add),
                     [tB_[0], ppB] + initB, [tB_[1]])
                E.op("dve", lambda e: e.tensor_tensor_scan(t4, rb, t3, init_im, OP.mult, OP.add),
                     [tB_[2], ppB] + initB, [tB_[3]])
                return t2, t4, tB_[1], tB_[3], cs, sn, csB, snB, t1, t3, tB_[0], tB_[2]

            for j in range(NB):
                cols = slice(j * T, (j + 1) * T)
                dma("sp", ut, Us[:, cols].rearrange("(c p) t -> p c t", p=128), [dramB], utB)
                for pr in range(32):
                    wr, wi, wrB, wiB = ssm_front(j, pr, 0.0, 0.0, [])[0:4]
                    act(wend[:, 2 * pr:2 * pr + 1], wr[:, 511:512], AF.Copy, [wrB], [wendB])
                    act(wend[:, 2 * pr + 1:2 * pr + 2], wi[:, 511:512], AF.Copy, [wiB], [wendB])
                dma("sp", sin_[j:j + 1, :].rearrange("o (p q) -> (o p) q", p=128), wend[:], [wendB], [dramB])
            E.barrier()
            E.collective(lambda e, a=sin_, b=sout: e.collective_compute(
                "AllGather", OP.bypass, replica_groups=[[0, 1, 2, 3], [4, 5, 6, 7]],
                ins=[a.opt()], outs=[b.opt()]), [], [dramB])
            E.barrier()
            if KSTOP == "S1":
                break
            EA = xf2[:, 12 * 512:16 * 512].rearrange("p (g q) -> p g q", g=32)[:, 0:4 * NB, :]
            EAB = SB_[12:16]
            HALL = xf2[:, 8 * 512:12 * 512].rearrange("p (g q) -> p g q", g=32)[:, 0:4 * NB, :]
            HB_ = SB_[8:12]
            dma("sp", EA, sout.rearrange("g (p q) -> p g q", p=128), [dramB], EAB)
            E.op("dve", lambda e: e.memset(xf2[:, 8 * 512:12 * 512], 0.0), [], HB_)
            tmpa = S[0][:, 0:32]
            tmpb = S[0][:, 32:64]
            tmpc = S[0][:, 64:96]
            TB0 = [SB_[0]]
            for gb in range(4 * NB - 1):
                rr_, j_ = gb % 4, gb // 4
                row = rr_ * NB + j_
                Xr = HALL[:, gb, 0:64:2]
                Xi = HALL[:, gb, 1:64:2]
                Nr = HALL[:, gb + 1, 0:64:2]
                Ni = HALL[:, gb + 1, 1:64:2]
                Wr = EA[:, row, 0:64:2]
                Wi = EA[:, row, 1:64:2]
                tt(tmpa, Xr, pp[:, 6, :], OP.mult, HB_ + [ppB], TB0)
                tt(tmpa, tmpa, Wr, OP.add, TB0 + EAB, TB0)
                tt(tmpb, Xi, pp[:, 6, :], OP.mult, HB_ + [ppB], TB0)
                tt(tmpb, tmpb, Wi, OP.add, TB0 + EAB, TB0)
                tt(Nr, tmpa, pp[:, 9, :], OP.mult, TB0 + [ppB], HB_)
                tt(tmpc, tmpb, pp[:, 8, :], OP.mult, TB0 + [ppB], TB0)
                tt(Nr, Nr, tmpc, OP.subtract, HB_ + TB0, HB_)
                tt(Ni, tmpa, pp[:, 8, :], OP.mult, TB0 + [ppB], HB_)
                tt(tmpc, tmpb, pp[:, 9, :], OP.mult, TB0 + [ppB], TB0)
                tt(Ni, Ni, tmpc, OP.add, HB_ + TB0, HB_)
            for j in range(NB):
                ts(hmine[:, j, :], HALL[:, 4 * j, :], sel[:, 0:1], None, OP.mult, None, HB_ + [miscB], [hmB])
                for rr_ in range(1, 4):
                    stt(hmine[:, j, :], HALL[:, 4 * j + rr_, :], sel[:, rr_:rr_ + 1], hmine[:, j, :],
                        OP.mult, OP.add, HB_ + [miscB, hmB], [hmB])
            E.barrier()

            if KSTOP == "PFX":
                break
            kt = hb2[:, 0:4 * NT].rearrange("p (r t) -> p r t", r=4)
            ktB = hbB[0:32]
            nsr = NT // 128
            vt = xf2.bitcast(BF16)[:, 0:4 * NT].rearrange("p (n d) -> p n d", d=128)
            vtB = xfB
            qt = wbuf[0][:, 0:NT]
            qtB = [wbufB[0]]
            at = xb2.bitcast(F32).rearrange("p (c t) -> p c t", c=8)
            atB = xbB
            for h in range(8):
                dma("sp", kt, kout[h].rearrange("(r p) t -> p r t", p=128), [dramB], ktB)
                vt4 = xf2.bitcast(BF16)[:, 0:4 * NT].rearrange("p (r n d) -> p r n d", r=4, d=128)
                for j_ in range(NB):
                    for rr_ in range(4):
                        dma("sp", vt4[:, rr_, j_ * 4:(j_ + 1) * 4, :],
                            vout[j_][rr_ * T:(rr_ + 1) * T, h * 128:(h + 1) * 128].rearrange("(n p) d -> p n d", p=128),
                            [dramB], vtB)
                dma("sp", qt, Qs[h * 128:(h + 1) * 128, :], [dramB], qtB)
                for jq in range(NB):
                    qcols = slice(jq * T, (jq + 1) * T)
                    kbs = [(rr_, j_, sbi) for j_ in range(jq + 1) for rr_ in range(4) for sbi in range(4)]
                    nk = len(kbs)
                    OB = [4, 5]
                    LBk = [6, 7]

                    def S_mm(i):
                        rr_, j_, sbi = kbs[i]
                        kc0 = j_ * T + sbi * 128
                        for m in range(2):
                            b = (i % 2) * 2 + m
                            ms = slice(64 * m, 64 * m + 64)
                            pairs = [(kt[ms, rr_, kc0:kc0 + 128], qt[ms, qcols])]
                            reads = ktB + qtB
                            if j_ == jq:
                                idx = rr_ * 4 + sbi
                                pairs.append((kmask[0:32, idx * 128:(idx + 1) * 128], qmask[0:32, :]))
                                reads = reads + [miscB]
                            mm(b, pairs, reads)

                    S_mm(0)
                    for i in range(nk):
                        if i + 1 < nk:
                            S_mm(i + 1)
                        rr_, j_, sbi = kbs[i]
                        for m in range(2):
                            b = (i % 2) * 2 + m
                            act(pt[i % 2][m][:], psum[b][:], AF.Exp, [psB[b]], [ptB[i % 2][m]], scale=0.125)
                        vi = rr_ * nsr + j_ * 4 + sbi
                        for m in range(2):
                            mm(OB[m], [(vt[:, vi, :], pt[i % 2][m][:])], vtB + [ptB[i % 2][m]],
                               start=(i == 0), stop=(i == nk - 1))
                            accB = [atB[8 + 2 * m], atB[9 + 2 * m]]
                            if i == 0:
                                E.op("dve", lambda e, o_=at[:, 4 + m, :], p_=pt[i % 2][m][:]: e.tensor_copy(o_, p_),
                                     [ptB[i % 2][m]], accB)
                            else:
                                tt(at[:, 4 + m, :], at[:, 4 + m, :], pt[i % 2][m][:], OP.add,
                                   [ptB[i % 2][m]] + accB, accB)
                    for m in range(2):
                        mm(LBk[m], [(ones32[:], at[:, 4 + m, :])], [miscB, atB[8 + 2 * m], atB[9 + 2 * m]])
                    E.op("dve", lambda e: e.reciprocal(at[:, 0, :], psum[6][:]), [psB[6]], [atB[0], atB[1]])
                    E.op("dve", lambda e: e.reciprocal(at[:, 1, :], psum[7][:]), [psB[7]], [atB[2], atB[3]])
                    tt(at[:, 2, :], psum[4][:], at[:, 0, :], OP.mult, [psB[4], atB[0], atB[1]], [atB[4], atB[5]])
                    tt(at[:, 3, :], psum[5][:], at[:, 1, :], OP.mult, [psB[5], atB[2], atB[3]], [atB[6], atB[7]])
                    stt(at[:, 2, :], at[:, 3, :], neglam, at[:, 2, :], OP.mult, OP.add,
                        [atB[4], atB[5], atB[6], atB[7], miscB], [atB[4], atB[5]])
                    act(sg[0][:], at[:, 2, :], AF.Square, [atB[4], atB[5]], [sgB[0]])
                    mm(0, [(ones[:], sg[0][:])], [miscB, sgB[0]])
                    ts(at[:, 0, :], psum[0][:], 1.0 / 128, RMS_EPS, OP.mult, OP.add, [psB[0]], [atB[0], atB[1]])
                    act(at[:, 0, :], at[:, 0, :], AF.Sqrt, [atB[0], atB[1]], [atB[0], atB[1]])
                    E.op("dve", lambda e: e.reciprocal(at[:, 0, :], at[:, 0, :]), [atB[0], atB[1]], [atB[0], atB[1]])
                    tt(at[:, 2, :], at[:, 2, :], at[:, 0, :], OP.mult, [atB[0], atB[1], atB[4], atB[5]],
                       [atB[4], atB[5]])
                    ts(sg[1][:], at[:, 2, :], gsc, None, OP.mult, None, [atB[4], atB[5], miscB], [sgB[1]])
                    dma("sp", As[h * 128:(h + 1) * 128, qcols], sg[1][:], [sgB[1]], [dramB])
            E.barrier()

            if KSTOP == "ATT":
                break
            cat = hb[:, 0:16, :]
            ygb = hb[:, 24:32, :]
            for j in range(NB):
                cols = slice(j * T, (j + 1) * T)
                dma("sp", ut, Us[:, cols].rearrange("(c p) t -> p c t", p=128), [dramB], utB)
                dma("sp", hb[:, 0:8, :], As[:, cols].rearrange("(c p) t -> p c t", p=128), [dramB], hbB[0:8])
                for cc in range(8):
                    by = 6 + (cc % 2)
                    for hf in range(2):
                        prs = [4 * cc + 2 * hf, 4 * cc + 2 * hf + 1]
                        pairs = []
                        rds = [ssmwB]
                        for q_, pr in enumerate(prs):
                            ire = hmine[:, j, 2 * pr:2 * pr + 1]
                            iim = hmine[:, j, 2 * pr + 1:2 * pr + 2]
                            wr, wi, wrB, wiB, cs, sn, csB, snB, u1, u3, u1B, u3B = ssm_front(j, pr, ire, iim, [hmB])
                            xr = xpair[:, 2 * q_, :]
                            xi = xpair[:, 2 * q_ + 1, :]
                            tt(u1, wr, cs, OP.mult, [wrB, csB], [u1B])
                            tt(u3, wi, sn, OP.mult, [wiB, snB], [u3B])
                            tt(xr, u1, u3, OP.subtract, [u1B, u3B], [xpB[2 * q_]])
                            tt(u1, wr, sn, OP.mult, [wrB, snB], [u1B])
                            tt(u3, wi, cs, OP.mult, [wiB, csB], [u3B])
                            tt(xi, u1, u3, OP.add, [u1B, u3B], [xpB[2 * q_ + 1]])
                            pairs.append((WC64[:, pr, 0, :], xr))
                            pairs.append((WC64[:, pr, 1, :], xi))
                            rds += [xpB[2 * q_], xpB[2 * q_ + 1]]
                        mm(by, pairs, rds, out_ap=psum[by][64 * hf:64 * hf + 64, :])
                    stt(yf[:, cc, :], ut[:, cc, :], ssmd[:, l * 8 + cc:l * 8 + cc + 1], psum[by][:], OP.mult, OP.add,
                        [utB[cc], miscB, psB[by]], [xbB[2 * cc], xbB[2 * cc + 1]])
                    act(yf[:, cc, :], yf[:, cc, :], AF.Gelu_apprx_tanh, [xbB[2 * cc], xbB[2 * cc + 1]],
                        [xbB[2 * cc], xbB[2 * cc + 1]])
                    act(ygb[:, cc, :], yf[:, cc, :], AF.Copy, [xbB[2 * cc], xbB[2 * cc + 1]], [hbB[24 + cc]])

                def out_glu(oc, pairs, reads):
                    b = bank()
                    mm(b, pairs, reads)
                    act(sg[oc % 2][:], psum[b][:], AF.Sigmoid, [psB[b], miscB], [sgB[oc % 2]],
                        bias=glub[:, l * 8 + oc:l * 8 + oc + 1])
                    tt(hb[:, 8 + oc, :], yf[:, oc, :], sg[oc % 2][:], OP.mult,
                       [xbB[2 * oc], xbB[2 * oc + 1], sgB[oc % 2]], [hbB[8 + oc]])
                linear("glu", l, lambda kc: ygb[:, kc, :], hbB[24:32], out_glu)

                dma("sp", xf[:], xs[:, cols].rearrange("(c p) t -> p c t", p=128), [dramB], xfB)
                ts(xf2, xf2, ALPHA, None, OP.mult, None, xfB, xfB)

                def out_res(oc, pairs, reads):
                    b = bank()
                    mm(b, pairs, reads)
                    tt(xf[:, oc, :], psum[b][:], xf[:, oc, :], OP.add, [psB[b], xfB[oc]], [xfB[oc]])
                if dbg:
                    dma("sp", dbg_cat[:, cols].rearrange("(c p) t -> p c t", p=128), hb[:, 0:16, :], hbB[0:16], [dramB])
                    dma("sp", dbg_x1[:, cols].rearrange("(c p) t -> p c t", p=128), xf[:], xfB, [dramB])
                linear("wout", l, lambda kc: hb[:, kc, :], hbB[0:16], out_res)
                layer_norm(l, 1)
                if dbg:
                    dma("sp", dbg_x2[:, cols].rearrange("(c p) t -> p c t", p=128), xf[:], xfB, [dramB])
                ts(xf2, xf2, ALPHA, None, OP.mult, None, xfB, xfB)

                def out_q(oc, pairs, reads):
                    b = bank()
                    mm(b, pairs, reads)
                    act(hb[:, oc, :], psum[b][:], AF.Copy, [psB[b]], [hbB[oc]])
                linear("xq", l, lambda kc: xb[:, kc, :], xbB, out_q)
                xsc = 512.0 ** -0.5
                for hh in range(4):
                    for mh in range(2):
                        b = bank()
                        mm(b, [(kmT[:, 4 * hh + dd, mh * 128:(mh + 1) * 128], hb[:, 4 * hh + dd, :]) for dd in range(4)],
                           [kvmB] + hbB[4 * hh:4 * hh + 4])
                        act(pt[0][mh][:], psum[b][:], AF.Exp, [psB[b]], [ptB[0][mh]], scale=xsc)
                    bl = bank()
                    mm(bl, [(ones[:], pt[0][mh][:]) for mh in range(2)], [miscB, ptB[0][0], ptB[0][1]])
                    rl = lnt[:, 0, :]
                    E.op("dve", lambda e, bl=bl, rl=rl: e.reciprocal(rl, psum[bl][:]), [psB[bl]],
                         hbB[32:34] + [lntB[0]])
                    for dd in range(4):
                        b = bank()
                        mm(b, [(vmem[:, mh, (4 * hh + dd) * 128:(4 * hh + dd + 1) * 128], pt[0][mh][:]) for mh in range(2)],
                           [kvmB, ptB[0][0], ptB[0][1]])
                        tt(hb[:, 16 + 4 * hh + dd, :], psum[b][:], rl, OP.mult, [psB[b], lntB[0]] + hbB[32:34],
                           [hbB[16 + 4 * hh + dd]])
                linear("xo", l, lambda kc: hb[:, 16 + kc, :], hbB[16:32], out_res)
                layer_norm(l, 2)
                if dbg:
                    dma("sp", dbg_x3[:, cols].rearrange("(c p) t -> p c t", p=128), xf[:], xfB, [dramB])
                ts(xf2, xf2, ALPHA, None, OP.mult, None, xfB, xfB)

                ffn(l, "gu2", "d2")
                layer_norm(l, 3)
                dst = outT if l == L - 1 else xs
                dma("sp", dst[:, cols].rearrange("(c p) t -> p c t", p=128), xf[:], xfB, [dramB])
            E.barrier()

        with nc.Block() as block:
            def replay(eng, name):
                for waits, fn, inc in E.s[name].ops:
                    for sem, val in waits:
                        eng.wait_ge(sems[sem], val)
                    if fn is None:
                        continue
                    ins = fn(eng)
                    if inc[1] is None:
                        ins.then_inc(sems[inc[0]])
                    else:
                        ins.then_inc(sems[inc[0]], inc[1])

            @block.tensor
            def _(e):
                replay(e, "pe")

            @block.scalar
            def _(e):
                replay(e, "act")

            @block.vector
            def _(e):
                replay(e, "dve")

            @block.gpsimd
            def _(e):
                replay(e, "pool")

            @block.sync
            def _(e):
                replay(e, "sp")

    return nc


def _gt(W, Kc, gc):
    K, N = W.shape
    G = N // gc
    return np.ascontiguousarray(W.reshape(Kc, 128, G, gc).transpose(2, 1, 0, 3)).reshape(G * 128, Kc * gc)


def _gt_ksplit(W, Kc, gc):
    K, N = W.shape
    G = N // gc
    a = W.reshape(2, Kc, 128, G, gc).transpose(3, 0, 2, 1, 4)
    return np.ascontiguousarray(a).reshape(G * 2 * 128, Kc * gc)


def _interleave_gu(wg, wu):
    K, Fd = wg.shape
    out = np.empty((K, 2 * Fd), np.float32)
    o = out.reshape(K, Fd // 128, 2, 128)
    o[:, :, 0, :] = wg.reshape(K, Fd // 128, 128)
    o[:, :, 1, :] = wu.reshape(K, Fd // 128, 128)
    return out


_CACHE = {}
_DBG = False
_LAST = {}


def kernel(**inp):
    x = np.asarray(inp["x"], np.float32)
    mem = np.asarray(inp["mem"], np.float32)
    Bsz, SEQ, _ = x.shape
    L = inp["w_in"].shape[0]
    NB = SEQ // (4 * T)
    NT = NB * T
    key = (NB, L)
    if key not in _CACHE:
        _CACHE[key] = build_nc(NB, L, dbg=_DBG)
    nc = _CACHE[key]

    f32 = lambda a: np.asarray(a, np.float32)
    shared = {}
    ws = {k: [] for k in WK}
    for l in range(L):
        ws["gu1"].append(_gt(_interleave_gu(f32(inp["ffn1_w_gate"][l]), f32(inp["ffn1_w_up"][l])), 16, 256))
        ws["d1"].append(_gt_ksplit(f32(inp["ffn1_w_down"][l]), 22, 128))
        ws["win"].append(_gt(f32(inp["w_in"][l]), 16, 256))
        ws["glu"].append(_gt(f32(inp["ssm_glu_w"][l]), 8, 512))
        ws["wout"].append(_gt(f32(inp["w_out"][l]), 16, 256))
        ws["xq"].append(_gt(f32(inp["xattn_w_q"][l]), 16, 256))
        ws["xk"].append(_gt(f32(inp["xattn_w_k"][l]), 16, 256))
        ws["xv"].append(_gt(f32(inp["xattn_w_v"][l]), 16, 256))
        ws["xo"].append(_gt(f32(inp["xattn_w_o"][l]), 16, 256))
        ws["gu2"].append(_gt(_interleave_gu(f32(inp["ffn2_w_gate"][l]), f32(inp["ffn2_w_up"][l])), 16, 256))
        ws["d2"].append(_gt_ksplit(f32(inp["ffn2_w_down"][l]), 22, 128))
    for k in WK:
        shared["w_" + k] = np.stack(ws[k], 0)

    def chunkT(v):
        return f32(v).reshape(-1, 128).T

    lnp = np.zeros((128, L, 8, 16), np.float32)
    for l in range(L):
        for i, nm in enumerate(["ln1", "ln2", "ln3", "ln4"]):
            lnp[:, l, 2 * i, :] = chunkT(inp[nm + "_g"][l])
            lnp[:, l, 2 * i + 1, :] = chunkT(inp[nm + "_b"][l])
    shared["lnp"] = lnp.reshape(128, -1)
    lamv = np.zeros((128, L, 4, 64), np.float32)
    for l in range(L):
        for i, nm in enumerate(["lambda_q1", "lambda_k1", "lambda_q2", "lambda_k2"]):
            lamv[:, l, i, :] = f32(inp[nm][l])[None, :]
    shared["lamv"] = lamv.reshape(128, -1)
    shared["dng"] = np.ascontiguousarray(f32(inp["diff_norm_g"]).T)
    shared["ssmd"] = np.concatenate([chunkT(inp["ssm_d"][l]) for l in range(L)], 1)
    shared["glub"] = np.concatenate([chunkT(inp["ssm_glu_b"][l]) for l in range(L)], 1)
    spair = np.zeros((128, L, 3, 32), np.float32)
    scm = np.zeros((128, L, 3, 8, 64), np.float32)
    ssmb = np.zeros((128, L, 2, 8, 64), np.float32)
    ssmc = np.zeros((128, L, 2, 32, 16), np.float32)
    for l in range(L):
        lre, lim, lst = f32(inp["ssm_lambda_re"][l]), f32(inp["ssm_lambda_im"][l]), f32(inp["ssm_log_step"][l])
        lstb = np.broadcast_to(lst[:, None], (64, 64))
        for i, a in enumerate([lre, lim, lstb]):
            spair[:, l, i, :] = a.reshape(32, 2, 64).transpose(1, 2, 0).reshape(128, 32)
            rep = np.repeat(a, 16, axis=0)
            scm[:, l, i] = rep.reshape(8, 128, 64).transpose(1, 0, 2)
        for i, nm in enumerate(["ssm_b_re", "ssm_b_im"]):
            b = f32(inp[nm][l])
            bc = b.transpose(0, 2, 1).reshape(1024, 64)
            ssmb[:, l, i] = bc.reshape(8, 128, 64).transpose(1, 0, 2)
        for i, nm in enumerate(["ssm_c_re", "ssm_c_im"]):
            c = f32(inp[nm][l])
            ssmc[:, l, i] = c.reshape(32, 2, 16, 64).transpose(1, 3, 0, 2).reshape(128, 32, 16)
    shared["spair"] = spair.reshape(128, -1)
    shared["scm"] = scm.reshape(128, -1)
    shared["ssmb"] = ssmb.reshape(128, -1)
    shared["ssmc"] = ssmc.reshape(128, -1)
    shared["iota1"] = np.broadcast_to(np.arange(1, 513, dtype=np.float32)[None, :], (128, 512)).copy()
    km = np.zeros((32, 4, 4, 128), np.float32)
    for rr in range(4):
        for sbi in range(4):
            for kk in range(128):
                km[rr * 8 + sbi * 2 + kk // 64, rr, sbi, kk] = 1.0
    shared["kmask"] = km.reshape(32, 2048)
    bm = np.zeros((128, 4), np.float32)
    for q in range(128):
        jp, gi = q // 32, (q // 16) % 2
        for jj in range(2):
            for g2 in range(2):
                bm[q, jj * 2 + g2] = 1.0 if (jp % 2 == jj and gi == g2) else 0.0
    shared["bmask"] = bm

    in_maps = []
    for c in range(8):
        b, r = c // 4, c % 4
        xb_ = x[b].reshape(NB, 4, T, D)[:, r].reshape(NT, D)
        m = dict(shared)
        m["xT"] = np.ascontiguousarray(xb_.T)
        m["memT"] = np.ascontiguousarray(mem[b].T)
        qm = np.zeros((32, 512), np.float32)
        for cidx in range(32):
            for q in range(512):
                if r * 8 + q // 64 < cidx:
                    qm[cidx, q] = NEGM
        m["qmask"] = qm
        s_ = np.zeros((128, 4), np.float32)
        s_[:, r] = 1.0
        m["sel"] = s_
        in_maps.append(m)

    res = run_bass_kernel_spmd(nc, in_maps, core_ids=list(range(8)))
    if _DBG:
        _LAST["res"] = res
    out = np.empty((Bsz, SEQ, D), np.float32)
    for c in range(8):
        b, r = c // 4, c % 4
        oT = np.asarray(res.results[c]["outT"])
        out[b].reshape(NB, 4, T, D)[:, r] = oT.T.reshape(NB, T, D)
    return out
```

```python
import math
import numpy as np
import concourse.bass as bass
import concourse.mybir as mybir
from concourse.bass_utils import run_bass_kernel_spmd

F32 = mybir.dt.float32
BF16 = mybir.dt.bfloat16
AF = mybir.ActivationFunctionType
OP = mybir.AluOpType
AX = mybir.AxisListType

D = 2048
DC = 16
FF = 5632
FC = 44
T = 512
NMEM = 256
DEPTH_FULL = 4
ALPHA = (2 * DEPTH_FULL) ** 0.25
LN_EPS = 1e-5
RMS_EPS = 1e-5
TWO_PI = 2.0 * math.pi
NEGM = -30000.0
WSLOT = 4096
NWSLOT = 3

WK = {
    "gu1": (16, 256, 44), "d1": (22, 128, 32), "win": (16, 256, 16), "glu": (8, 512, 2),
    "wout": (16, 256, 8), "xq": (16, 256, 8), "xk": (16, 256, 8), "xv": (16, 256, 8),
    "xo": (16, 256, 8), "gu2": (16, 256, 44), "d2": (22, 128, 32),
}
WORDER = ["xk", "xv", "gu1", "d1", "win", "glu", "wout", "xq", "xo", "gu2", "d2"]


class Buf:
    __slots__ = ("name", "w", "r")

    def __init__(self, name):
        self.name = name
        self.w = None
        self.r = {}


class Stream:
    def __init__(self, name):
        self.name = name
        self.ops = []
        self.count = 0
        self.known = {}
        self.dma_k = 0


NDMASEM = 8


class Emitter:
    def __init__(self):
        self.s = {n: Stream(n) for n in ("pe", "act", "dve", "pool", "sp")}
        self.cc_count = 0

    def _waits(self, st, reads, writes, extra=()):
        waits = {}

        def need(ev):
            if ev is None:
                return
            sem, val = ev
            if st.known.get(sem, 0) < val and waits.get(sem, 0) < val:
                waits[sem] = val

        for b in reads:
            need(b.w)
        for b in writes:
            need(b.w)
            for sem, val in b.r.items():
                need((sem, val))
        for ev in extra:
            need(ev)
        return waits

    def _finish(self, st, waits, fn, inc, ev, reads, writes):
        for sem, val in waits.items():
            st.known[sem] = val
        st.ops.append((list(waits.items()), fn, inc))
        if ev is not None:
            for b in reads:
                if b.r.get(ev[0], 0) < ev[1]:
                    b.r[ev[0]] = ev[1]
            for b in writes:
                b.w = ev
                b.r = {}

    def op(self, eng, fn, reads=(), writes=()):
        st = self.s[eng]
        waits = self._waits(st, reads, writes)
        if eng == "pe":
            waits.pop("pe", None)
        st.count += 1
        ev = (eng, st.count)
        self._finish(st, waits, fn, (eng, 1), ev, reads, writes)
        return ev

    def dma(self, eng, fn, reads=(), writes=()):
        st = self.s[eng]
        k = st.dma_k
        st.dma_k += 1
        sem = "%sq%d" % (eng, k % NDMASEM)
        extra = [(sem, 16 * (k // NDMASEM))] if k >= NDMASEM else []
        waits = self._waits(st, reads, writes, extra)
        ev = (sem, 16 * (k // NDMASEM + 1))
        self._finish(st, waits, fn, (sem, 16), ev, reads, writes)
        return ev

    def collective(self, fn, reads=(), writes=()):
        st = self.s["pool"]
        waits = self._waits(st, reads, writes)
        self.cc_count += 1
        ev = ("cc", self.cc_count)
        self._finish(st, waits, fn, ("cc", None), ev, reads, writes)
        return ev

    def barrier(self):
        evs = []
        for n, st in self.s.items():
            if st.count:
                evs.append((n, st.count))
            for i in range(min(st.dma_k, NDMASEM)):
                k = st.dma_k - 1 - i
                evs.append(("%sq%d" % (n, k % NDMASEM), 16 * (k // NDMASEM + 1)))
        if self.cc_count:
            evs.append(("cc", self.cc_count))
        for n, st in self.s.items():
            waits = {}
            for sem, val in evs:
                if sem == n:
                    continue
                if st.known.get(sem, 0) < val and waits.get(sem, 0) < val:
                    waits[sem] = val
            for sem, val in waits.items():
                st.known[sem] = val
            if waits:
                st.ops.append((list(waits.items()), None, None))


def build_nc(NB, depth, dbg=None):
    import os
    KSTOP = os.environ.get("KSTOP", "")
    NT = NB * T
    L = depth
    nc = bass.Bass("TRN2", target_bir_lowering=False)
    E = Emitter()

    def dram_in(name, shape, dt=F32):
        return nc.dram_tensor(name, list(shape), dt, kind="ExternalInput").ap()

    def dram_sc(name, shape, dt):
        return nc.dram_tensor(name, list(shape), dt).ap()

    xT = dram_in("xT", [D, NT])
    memT = dram_in("memT", [D, NMEM])
    wsrc = {}
    wbf = {}
    wB = {}
    for k, (Kc, gc, G) in WK.items():
        wsrc[k] = dram_in("w_" + k, [L, G * 128, Kc * gc])
        wbf[k] = dram_sc("wb_" + k, [L, G * 128, Kc * gc], BF16)
        for l in range(L):
            wB[(k, l)] = Buf("wb_%s_%d" % (k, l))
    lnp_d = dram_in("lnp", [128, L * 8 * 16])
    lamv_d = dram_in("lamv", [128, L * 4 * 64])
    dng_d = dram_in("dng", [128, L])
    ssmd_d = dram_in("ssmd", [128, L * 8])
    glub_d = dram_in("glub", [128, L * 8])
    spair_d = dram_in("spair", [128, L * 3 * 32])
    scm_d = dram_in("scm", [128, L * 3 * 512])
    sb_d = dram_in("ssmb", [128, L * 2 * 512])
    sc_d = dram_in("ssmc", [128, L * 2 * 512])
    iota_d = dram_in("iota1", [128, 512])
    kmask_d = dram_in("kmask", [32, 2048])
    qmask_d = dram_in("qmask", [32, 512])
    bmask_d = dram_in("bmask", [128, 4])
    sel_d = dram_in("sel", [128, 4])
    outT = nc.dram_tensor("outT", [D, NT], F32, kind="ExternalOutput").ap()

    xs = dram_sc("xs", [D, NT], F32)
    Qs = dram_sc("Qs", [1024, NT], BF16)
    Us = dram_sc("Us", [1024, NT], BF16)
    As = dram_sc("As", [1024, NT], BF16)
    kin_l = [[dram_sc("kin%d_%d" % (l, h), [128, NT], BF16) for h in range(8)] for l in range(L)]
    kout_l = [[dram_sc("kout%d_%d" % (l, h), [4 * 128, NT], BF16) for h in range(8)] for l in range(L)]
    vin_l = [[dram_sc("vin%d_%d" % (l, j), [T, 1024], BF16) for j in range(NB)] for l in range(L)]
    vout_l = [[dram_sc("vout%d_%d" % (l, j), [4 * T, 1024], BF16) for j in range(NB)] for l in range(L)]
    sin_l = [dram_sc("sin%d" % l, [NB, 8192], F32) for l in range(L)]
    sout_l = [dram_sc("sout%d" % l, [4 * NB, 8192], F32) for l in range(L)]
    tabs = dram_sc("tabs", [32, 128, 1024], F32)
    dbg_x2 = dbg_x3 = dbg_x1 = dbg_cat = None
    if dbg:
        dbg_x1 = nc.dram_tensor("d_x1", [D, NT], F32, kind="ExternalOutput").ap()
        dbg_x2 = nc.dram_tensor("d_x2", [D, NT], F32, kind="ExternalOutput").ap()
        dbg_x3 = nc.dram_tensor("d_x3", [D, NT], F32, kind="ExternalOutput").ap()
        dbg_cat = nc.dram_tensor("d_cat", [D, NT], BF16, kind="ExternalOutput").ap()

    import contextlib
    with contextlib.ExitStack() as ctx:
        def sb(name, shape, dt):
            return ctx.enter_context(nc.sbuf_tensor("s_" + name, list(shape), dt))

        xf = sb("xf", [128, 16, 512], F32)
        xb = sb("xb", [128, 16, 512], BF16)
        hb = sb("hb", [128, 44, 512], BF16)
        wbuf = [sb("wbuf%d" % i, [128, WSLOT], BF16) for i in range(NWSLOT)]
        WB64 = sb("WB64", [128, 8, 2, 2, 128], BF16)
        WC64 = sb("WC64", [128, 32, 2, 64], BF16)
        kmT = sb("kmT", [128, 16, 256], BF16)
        vmem = sb("vmem", [128, 2, 2048], BF16)
        pt = [[sb("pt%d%d" % (i, m), [128, 512], BF16) for m in range(2)] for i in range(2)]
        sg = [sb("sg%d" % i, [128, 512], BF16) for i in range(2)]
        ones = sb("ones", [128, 128], BF16)
        ones32 = sb("ones32", [128, 128], F32)
        kmask = sb("kmask", [32, 2048], BF16)
        qmask = sb("qmask", [32, 512], BF16)
        lnp = sb("lnp", [128, L * 8 * 16], F32)
        iota1 = sb("iota1", [128, 512], F32)
        lamv = sb("lamv", [128, L * 4 * 64], F32)
        dng = sb("dng", [128, L], F32)
        ssmd = sb("ssmd", [128, L * 8], F32)
        glub = sb("glub", [128, L * 8], F32)
        bmask = sb("bmask", [128, 4], F32)
        sel = sb("sel", [128, 4], F32)
        pp = sb("pp", [128, 12, 32], F32)
        hmine = sb("hmine", [128, NB, 64], F32)
        wend = sb("wend", [128, 64], F32)
        sm = sb("sm", [128, 16], F32)
        xpair = sb("xpair", [128, 4, 512], BF16)
        psum = [ctx.enter_context(nc.psum_tensor("ps%d" % i, [128, 512], F32)) for i in range(8)]

        sems = {}
        for n in ["pe", "act", "dve", "pool", "sp", "cc"]:
            sems[n] = ctx.enter_context(nc.semaphore("m_" + n))
        for q in ["pool", "sp"]:
            for i in range(NDMASEM):
                sems["%sq%d" % (q, i)] = ctx.enter_context(nc.semaphore("m_%sq%d" % (q, i)))

        xfB = [Buf("xf%d" % i) for i in range(16)]
        xbB = [Buf("xb%d" % i) for i in range(16)]
        hbB = [Buf("hb%d" % i) for i in range(44)]
        wbufB = [Buf("wbuf%d" % i) for i in range(NWSLOT)]
        psB = [Buf("ps%d" % i) for i in range(8)]
        ptB = [[Buf("pt"), Buf("pt")], [Buf("pt"), Buf("pt")]]
        sgB = [Buf("sg0"), Buf("sg1")]
        miscB = Buf("misc")
        ssmwB = Buf("ssmw")
        kvmB = Buf("kvm")
        ppB = Buf("pp")
        hmB = Buf("hmine")
        wendB = Buf("wend")
        xpB = [Buf("xp%d" % i) for i in range(4)]
        dramB = Buf("dram")

        xf2 = xf[:].rearrange("p c t -> p (c t)")
        xb2 = xb[:].rearrange("p c t -> p (c t)")
        hb2 = hb[:].rearrange("p c t -> p (c t)")
        S = [xf[:, i, :] for i in range(16)]
        SB_ = xfB
        yf = xb2.bitcast(F32).rearrange("p (c t) -> p c t", c=8)
        lnt = hb2[:, 32 * 512:42 * 512].bitcast(F32).rearrange("p (c t) -> p c t", c=5)
        lntB = [Buf("lnt%d" % i) for i in range(5)]

        ps_rr = [0]

        def bank():
            i = ps_rr[0]
            ps_rr[0] = (i + 1) % 8
            return i

        def mm(bi, pairs, reads, start=True, stop=True, out_ap=None):
            o = out_ap if out_ap is not None else psum[bi][:]
            n = len(pairs)

            def fn(pe, pairs=pairs, o=o, n=n, start=start, stop=stop):
                ins = None
                for i, (l_, r_) in enumerate(pairs):
                    ins = pe.matmul(o, l_, r_, start=(start and i == 0), stop=(stop and i == n - 1))
                return ins
            return E.op("pe", fn, reads=reads, writes=[psB[bi]])

        def act(out, in_, func, reads, writes, bias=None, scale=None):
            kw = {}
            if bias is not None:
                kw["bias"] = bias
            if scale is not None:
                kw["scale"] = scale
            return E.op("act", lambda e: e.activation(out=out, in_=in_, func=func, **kw), reads, writes)

        def tt(out, a, b, op, reads, writes):
            return E.op("dve", lambda e: e.tensor_tensor(out, a, b, op), reads, writes)

        def ts(out, a, s1, s2, op0, op1, reads, writes):
            if op1 is None:
                return E.op("dve", lambda e: e.tensor_scalar(out, a, s1, None, op0), reads, writes)
            return E.op("dve", lambda e: e.tensor_scalar(out, a, s1, s2, op0, op1), reads, writes)

        def stt(out, a, sc, b, op0, op1, reads, writes):
            return E.op("dve", lambda e: e.scalar_tensor_tensor(out, a, sc, b, op0, op1), reads, writes)

        def dma(q, out, in_, reads, writes):
            return E.dma(q, lambda e: e.dma_start(out=out, in_=in_), reads, writes)

        dma("sp", lnp[:], lnp_d, [], [miscB])
        dma("sp", iota1[:], iota_d, [], [miscB])
        dma("sp", lamv[:], lamv_d, [], [miscB])
        dma("sp", dng[:], dng_d, [], [miscB])
        dma("sp", ssmd[:], ssmd_d, [], [miscB])
        dma("sp", glub[:], glub_d, [], [miscB])
        dma("sp", bmask[:], bmask_d, [], [miscB])
        dma("sp", sel[:], sel_d, [], [miscB])
        dma("pool", kmask[:], kmask_d, [], [miscB])
        dma("pool", qmask[:], qmask_d, [], [miscB])
        E.op("dve", lambda e: e.memset(ones[:], 1.0), [], [miscB])
        E.op("dve", lambda e: e.memset(ones32[:], 1.0), [], [miscB])

        def convert_layer(l):
            for k in WORDER:
                Kc, gc, G = WK[k]
                rows = G * 128
                step = 1024
                for r0 in range(0, rows, step):
                    r1 = min(rows, r0 + step)
                    dma("pool", wbf[k][l, r0:r1, :], wsrc[k][l, r0:r1, :], [], [wB[(k, l)]])

        wslot = [0]

        def linear_ksplit(k, l, rhs_fn, rhs_bufs, out_fn):
            Kc, gc, G = WK[k]
            for oc in range(G // 2):
                pairs = []
                rds = []
                for half in range(2):
                    g = 2 * oc + half
                    s_ = wslot[0]
                    wslot[0] = (s_ + 1) % NWSLOT
                    wt = wbuf[s_]
                    dma("sp", wt[:, :Kc * gc], wbf[k][l, g * 128:(g + 1) * 128, :], [wB[(k, l)]], [wbufB[s_]])
                    wv = wt[:, :Kc * gc].rearrange("p (k m) -> p k m", k=Kc)
                    pairs += [(wv[:, kc, :], rhs_fn(half * Kc + kc)) for kc in range(Kc)]
                    rds.append(wbufB[s_])
                out_fn(oc, pairs, rds + rhs_bufs)

        def linear(k, l, rhs_fn, rhs_bufs, out_fn, swap=False, groups=None, ntok=None):
            Kc, gc, G = WK[k]
            for g in (groups if groups is not None else range(G)):
                s_ = wslot[0]
                wslot[0] = (s_ + 1) % NWSLOT
                wt = wbuf[s_]
                dma("sp", wt[:, :Kc * gc], wbf[k][l, g * 128:(g + 1) * 128, :], [wB[(k, l)]], [wbufB[s_]])
                wv = wt[:, :Kc * gc].rearrange("p (k m) -> p k m", k=Kc)
                if not swap:
                    for mi in range(gc // 128):
                        oc = g * (gc // 128) + mi
                        pairs = [(wv[:, kc, mi * 128:(mi + 1) * 128], rhs_fn(kc)) for kc in range(Kc)]
                        out_fn(oc, pairs, [wbufB[s_]] + rhs_bufs)
                else:
                    for tsi in range(ntok):
                        pairs = [(rhs_fn(kc, tsi), wv[:, kc, :]) for kc in range(Kc)]
                        out_fn(g, tsi, pairs, [wbufB[s_]] + rhs_bufs)

        def layer_norm(l, idx):
            g0 = (l * 8 + 2 * idx) * 16
            b0 = (l * 8 + 2 * idx + 1) * 16
            act(xb2, xf2, AF.Copy, xfB, xbB)
            act(hb2[:, 0:8192], xf2, AF.Square, xfB, hbB[0:16])
            b1 = bank()
            mm(b1, [(ones[:], xb[:, c, :]) for c in range(16)], xbB + [miscB])
            b2 = bank()
            mm(b2, [(ones[:], hb[:, c, :]) for c in range(16)], hbB[0:16] + [miscB])
            mt, vt, rstd, nmr = lnt[:, 0, :], lnt[:, 1, :], lnt[:, 2, :], lnt[:, 3, :]
            LB = hbB[32:42]
            ts(mt, psum[b1][:], 1.0 / D, None, OP.mult, None, [psB[b1]], LB + [lntB[0]])
            tt(vt, mt, mt, OP.mult, [lntB[0]], LB + [lntB[1]])
            stt(vt, psum[b2][:], 1.0 / D, vt, OP.mult, OP.subtract, [psB[b2], lntB[1]], LB + [lntB[1]])
            ts(vt, vt, LN_EPS, None, OP.add, None, [lntB[1]], LB + [lntB[1]])
            act(nmr, vt, AF.Sqrt, [lntB[1]], LB + [lntB[3]])
            E.op("dve", lambda e: e.reciprocal(rstd, nmr), [lntB[3]], LB + [lntB[2]])
            tt(nmr, rstd, rstd, OP.mult, [lntB[2]], LB + [lntB[3]])
            tt(nmr, nmr, vt, OP.mult, [lntB[3], lntB[1]], LB + [lntB[3]])
            ts(nmr, nmr, -0.5, 1.5, OP.mult, OP.add, [lntB[3]], LB + [lntB[3]])
            tt(rstd, rstd, nmr, OP.mult, [lntB[2], lntB[3]], LB + [lntB[2]])
            stt(nmr, mt, -1.0, rstd, OP.mult, OP.mult, [lntB[0], lntB[2]], LB + [lntB[3]])
            for c in range(16):
                tt(xf[:, c, :], xf[:, c, :], rstd, OP.mult, [xfB[c], lntB[2]], [xfB[c]])
                tt(xf[:, c, :], xf[:, c, :], nmr, OP.add, [xfB[c], lntB[3]], [xfB[c]])
                act(xb[:, c, :], xf[:, c, :], AF.Identity, [xfB[c], miscB], [xbB[c]],
                    bias=lnp[:, b0 + c:b0 + c + 1], scale=lnp[:, g0 + c:g0 + c + 1])
                act(xf[:, c, :], xf[:, c, :], AF.Identity, [xfB[c], miscB], [xfB[c]],
                    bias=lnp[:, b0 + c:b0 + c + 1], scale=lnp[:, g0 + c:g0 + c + 1])

        def ffn(l, kgu, kd):
            pend = {}

            def out_gu(oc, pairs, reads):
                fc, which = oc // 2, oc % 2
                b = bank()
                mm(b, pairs, reads)
                if which == 0:
                    pend[fc] = b
                else:
                    bg = pend.pop(fc)
                    si = fc % 2
                    act(sg[si][:], psum[bg][:], AF.Silu, [psB[bg]], [sgB[si]])
                    tt(hb[:, fc, :], sg[si][:], psum[b][:], OP.mult, [sgB[si], psB[b]], [hbB[fc]])

            linear(kgu, l, lambda kc: xb[:, kc, :], xbB, out_gu)

            def out_d(oc, pairs, reads):
                b = bank()
                mm(b, pairs, reads)
                stt(xf[:, oc, :], psum[b][:], 0.5, xf[:, oc, :], OP.mult, OP.add, [psB[b], xfB[oc]], [xfB[oc]])

            linear_ksplit(kd, l, lambda kc: hb[:, kc, :], hbB, out_d)

        for l in range(L):
            lam_init = 0.8 - 0.6 * math.exp(-0.3 * l)
            kin, kout, vin, vout, sin_, sout = kin_l[l], kout_l[l], vin_l[l], vout_l[l], sin_l[l], sout_l[l]
            xsrc = xT if l == 0 else xs

            lv = lamv[:, l * 256:(l + 1) * 256]
            t0_, t1_ = lnt[:, 4, 0:64], lnt[:, 4, 64:128]
            tB = [lntB[4]] + hbB[40:42]
            tt(t0_, lv[:, 0:64], lv[:, 64:128], OP.mult, [miscB], tB)
            E.op("dve", lambda e: e.reduce_sum(sm[:, 0:1], t0_, axis=AX.X), tB, [miscB])
            tt(t1_, lv[:, 128:192], lv[:, 192:256], OP.mult, [miscB], tB)
            E.op("dve", lambda e: e.reduce_sum(sm[:, 1:2], t1_, axis=AX.X), tB, [miscB])
            act(sm[:, 0:2], sm[:, 0:2], AF.Exp, [miscB], [miscB])
            stt(sm[:, 2:3], sm[:, 1:2], -lam_init, sm[:, 0:1], OP.add, OP.subtract, [miscB], [miscB])
            ts(sm[:, 3:4], dng[:, l:l + 1], 1.0 - lam_init, None, OP.mult, None, [miscB], [miscB])
            neglam = sm[:, 2:3]
            gsc = sm[:, 3:4]

            memb = hb2[:, 0:4096].rearrange("p (c t) -> p c t", c=16)
            dma("pool", memb, memT.rearrange("(c p) t -> p c t", p=128), [], hbB[0:8])
            if l == 0:
                convert_layer(0)

            def out_km(oc, pairs, reads):
                b = bank()
                mm(b, pairs, reads, out_ap=psum[b][:, 0:256])
                act(kmT[:, oc, :], psum[b][:, 0:256], AF.Copy, [psB[b]], [kvmB])
            linear("xk", l, lambda kc: memb[:, kc, :], hbB[0:8], out_km)

            def out_vm(g, tsi, pairs, reads):
                b = bank()
                mm(b, pairs, reads, out_ap=psum[b][:, 0:256])
                act(vmem[:, tsi, g * 256:(g + 1) * 256], psum[b][:, 0:256], AF.Copy, [psB[b]], [kvmB])
            linear("xv", l, lambda kc, tsi: memb[:, kc, tsi * 128:(tsi + 1) * 128], hbB[0:8], out_vm,
                   swap=True, ntok=2)

            def ld(i, src):
                dma("sp", S[i], src, [], [SB_[i]])
            scm_l = scm_d[:, l * 1536:(l + 1) * 1536]
            ld(0, scm_l[:, 0:512]); ld(1, scm_l[:, 512:1024]); ld(2, scm_l[:, 1024:1536])
            ld(3, sb_d[:, (2 * l) * 512:(2 * l + 1) * 512]); ld(4, sb_d[:, (2 * l + 1) * 512:(2 * l + 2) * 512])

            def exp_acc(out, x, xB, outB, t, tB, q, qB):
                cc_ = -4.6
                n = 24
                ts(t, x, -cc_, None, OP.add, None, xB, tB)
                ts(q, t, 1.0 / math.factorial(n), None, OP.mult, None, tB, qB)
                for k in range(n - 1, 0, -1):
                    stt(q, q, 1.0 / math.factorial(k), t, OP.add, OP.mult, qB + tB, qB)
                ts(out, q, 1.0, math.exp(cc_), OP.add, OP.mult, qB, outB)

            def expm1_small(out, x, xB, outB, q, qB):
                n = 7
                ts(q, x, 1.0 / math.factorial(n), None, OP.mult, None, xB, qB)
                for k in range(n - 1, 0, -1):
                    stt(q, q, 1.0 / math.factorial(k), x, OP.add, OP.mult, qB + xB, qB)
                ts(out, q, 1.0, None, OP.mult, None, qB, outB)

            def sin_acc(out, x, xB, outB, ta, taB, tb, tbB, tc, tcB):
                I32 = mybir.dt.int32
                ts(ta, x, 1.0 / TWO_PI, None, OP.mult, None, xB, taB)
                E.op("dve", lambda e: e.tensor_copy(tb.bitcast(I32), ta), taB, tbB)
                E.op("dve", lambda e: e.tensor_copy(ta, tb.bitcast(I32)), tbB, taB)
                stt(ta, ta, -TWO_PI, x, OP.mult, OP.add, taB + xB, taB)
                ts(tb, ta, math.pi, None, OP.is_gt, None, taB, tbB)
                stt(ta, tb, -TWO_PI, ta, OP.mult, OP.add, taB + tbB, taB)
                ts(tb, ta, -math.pi, None, OP.is_lt, None, taB, tbB)
                stt(ta, tb, TWO_PI, ta, OP.mult, OP.add, taB + tbB, taB)
                ts(tb, ta, -1.0, math.pi, OP.mult, OP.add, taB, tbB)
                tt(tb, ta, tb, OP.min, taB + tbB, tbB)
                ts(tc, ta, -1.0, -math.pi, OP.mult, OP.add, taB, tcB)
                tt(ta, tb, tc, OP.max, tbB + tcB, taB)
                tt(tb, ta, ta, OP.mult, taB, tbB)
                n = 7
                cf = [((-1.0) ** k) / math.factorial(2 * k + 1) for k in range(n + 1)]
                ts(tc, tb, cf[n], None, OP.mult, None, tbB, tcB)
                for k in range(n - 1, 0, -1):
                    stt(tc, tc, cf[k], tb, OP.add, OP.mult, tcB + tbB, tcB)
                stt(out, tc, 1.0, ta, OP.add, OP.mult, tcB + taB, outB)

            def sincos(x_ap, xB, sn_ap, snB, cs_ap, csB, ta, taB, tb, tbB):
                I32 = mybir.dt.int32
                ts(ta, x_ap, 1.0 / TWO_PI, None, OP.mult, None, xB, taB)
                E.op("dve", lambda e: e.tensor_copy(tb.bitcast(I32), ta), taB, tbB)
                E.op("dve", lambda e: e.tensor_copy(ta, tb.bitcast(I32)), tbB, taB)
                stt(ta, ta, -TWO_PI, x_ap, OP.mult, OP.add, taB + xB, taB)
                ts(tb, ta, math.pi, None, OP.is_gt, None, taB, tbB)
                stt(ta, tb, -TWO_PI, ta, OP.mult, OP.add, taB + tbB, taB)
                ts(tb, ta, -math.pi, None, OP.is_lt, None, taB, tbB)
                stt(ta, tb, TWO_PI, ta, OP.mult, OP.add, taB + tbB, taB)
                ts(ta, ta, math.pi, -math.pi, OP.min, OP.max, taB, taB)
                act(sn_ap, ta, AF.Sin, taB, snB)
                stt(tb, ta, -1.0, ta, OP.mult, OP.max, taB, tbB)
                act(cs_ap, tb, AF.Sin, tbB + [miscB], csB, bias=hpi_col, scale=-1.0)

            hpi_col = sm[:, 4:5]
            E.op("dve", lambda e: e.memset(sm[:, 4:5], math.pi / 2), [], [miscB])

            def B1(i):
                return [SB_[i]]
            exp_acc(S[2], S[2], B1(2), B1(2), S[9], B1(9), S[10], B1(10))
            ts(S[0], S[0], -1e-4, None, OP.min, None, B1(0), B1(0))
            tt(S[5], S[0], S[2], OP.mult, B1(0) + B1(2), B1(5))
            tt(S[6], S[1], S[2], OP.mult, B1(1) + B1(2), B1(6))
            expm1_small(S[5], S[5], B1(5), B1(5), S[9], B1(9))
            sin_acc(S[7], S[6], B1(6), B1(7), S[9], B1(9), S[10], B1(10), S[11], B1(11))
            ts(S[12], S[6], 0.5, None, OP.mult, None, B1(6), B1(12))
            sin_acc(S[8], S[12], B1(12), B1(8), S[9], B1(9), S[10], B1(10), S[11], B1(11))
            tt(S[8], S[8], S[8], OP.mult, B1(8), B1(8))
            ts(S[8], S[8], -2.0, None, OP.mult, None, B1(8), B1(8))
            ts(S[12], S[8], 1.0, None, OP.add, None, B1(8), B1(12))
            tt(S[9], S[5], S[12], OP.mult, B1(5) + B1(12), B1(9))
            tt(S[9], S[9], S[8], OP.add, B1(9) + B1(8), B1(9))
            stt(S[10], S[5], 1.0, S[7], OP.add, OP.mult, B1(5) + B1(7), B1(10))
            tt(S[11], S[0], S[0], OP.mult, B1(0), B1(11))
            tt(S[12], S[1], S[1], OP.mult, B1(1), B1(12))
            tt(S[11], S[11], S[12], OP.add, B1(11) + B1(12), B1(11))
            E.op("dve", lambda e: e.reciprocal(S[11], S[11]), B1(11), B1(11))
            tt(S[12], S[9], S[0], OP.mult, B1(9) + B1(0), B1(12))
            tt(S[13], S[10], S[1], OP.mult, B1(10) + B1(1), B1(13))
            tt(S[12], S[12], S[13], OP.add, B1(12) + B1(13), B1(12))
            tt(S[12], S[12], S[11], OP.mult, B1(12) + B1(11), B1(12))
            tt(S[13], S[10], S[0], OP.mult, B1(10) + B1(0), B1(13))
            tt(S[14], S[9], S[1], OP.mult, B1(9) + B1(1), B1(14))
            tt(S[13], S[13], S[14], OP.subtract, B1(13) + B1(14), B1(13))
            tt(S[13], S[13], S[11], OP.mult, B1(13) + B1(11), B1(13))
            tt(S[14], S[12], S[3], OP.mult, B1(12) + B1(3), B1(14))
            tt(S[15], S[13], S[4], OP.mult, B1(13) + B1(4), B1(15))
            tt(S[14], S[14], S[15], OP.subtract, B1(14) + B1(15), B1(14))
            tt(S[15], S[12], S[4], OP.mult, B1(12) + B1(4), B1(15))
            tt(S[9], S[13], S[3], OP.mult, B1(13) + B1(3), B1(9))
            tt(S[15], S[15], S[9], OP.add, B1(15) + B1(9), B1(15))
            for jj in range(2):
                for gi in range(2):
                    for reim in range(2):
                        src = S[14 + reim].rearrange("p (c n) -> p c n", c=8)
                        dst = WB64[:, :, jj, reim, gi * 64:(gi + 1) * 64]
                        mcol = bmask[:, jj * 2 + gi:jj * 2 + gi + 1]
                        ts(dst, src, mcol, None, OP.mult, None, B1(14 + reim) + [miscB], [ssmwB])
            E.op("dve", lambda e: e.memset(WC64[:].rearrange("p a b c -> p (a b c)"), 0.0), [], [ssmwB])
            ld(0, sc_d[:, (2 * l) * 512:(2 * l + 1) * 512]); ld(1, sc_d[:, (2 * l + 1) * 512:(2 * l + 2) * 512])
            WCv = WC64[:].rearrange("p (a b) r m -> p a b r m", b=2)
            for reim in range(2):
                cv = S[reim].rearrange("p (a b q) -> p a b q", a=16, b=2)
                for gi in range(2):
                    for jj in range(2):
                        dst = WCv[gi * 64:(gi + 1) * 64, :, jj, reim, jj * 32 + gi * 16:jj * 32 + gi * 16 + 16]
                        src = cv[gi * 64:(gi + 1) * 64, :, jj, :]
                        ts(dst, src, 1.0 if reim == 0 else -1.0, None, OP.mult, None, B1(reim), [ssmwB])
            dma("sp", pp[:, 0:3, :].rearrange("p a b -> p (a b)"), spair_d[:, l * 96:(l + 1) * 96], [], [ppB])
            PB = [ppB]
            exp_acc(pp[:, 2, :], pp[:, 2, :], PB, PB, pp[:, 10, :], PB, pp[:, 11, :], PB)
            ts(pp[:, 0, :], pp[:, 0, :], -1e-4, None, OP.min, None, PB, PB)
            tt(pp[:, 3, :], pp[:, 0, :], pp[:, 2, :], OP.mult, PB, PB)
            tt(pp[:, 4, :], pp[:, 1, :], pp[:, 2, :], OP.mult, PB, PB)
            expm1_small(pp[:, 5, :], pp[:, 3, :], PB, PB, pp[:, 10, :], PB)
            ts(pp[:, 5, :], pp[:, 5, :], 1.0, None, OP.add, None, PB, PB)
            I32 = mybir.dt.int32
            ts(pp[:, 10, :], pp[:, 4, :], 1.0 / TWO_PI, None, OP.mult, None, PB, PB)
            E.op("dve", lambda e: e.tensor_copy(pp[:, 11, :].bitcast(I32), pp[:, 10, :]), PB, PB)
            E.op("dve", lambda e: e.tensor_copy(pp[:, 10, :], pp[:, 11, :].bitcast(I32)), PB, PB)
            stt(pp[:, 4, :], pp[:, 10, :], -TWO_PI, pp[:, 4, :], OP.mult, OP.add, PB, PB)
            ts(pp[:, 10, :], pp[:, 4, :], 0.0, None, OP.is_lt, None, PB, PB)
            stt(pp[:, 4, :], pp[:, 10, :], TWO_PI, pp[:, 4, :], OP.mult, OP.add, PB, PB)
            ts(pp[:, 6, :], pp[:, 5, :], 1.0, None, OP.mult, None, PB, PB)
            for _sq in range(9):
                tt(pp[:, 6, :], pp[:, 6, :], pp[:, 6, :], OP.mult, PB, PB)
            ts(pp[:, 7, :], pp[:, 4, :], 512.0, None, OP.mult, None, PB, PB)
            sin_acc(pp[:, 8, :], pp[:, 7, :], PB, PB, pp[:, 10, :], PB, pp[:, 11, :], PB, pp[:, 3, :], PB)
            ts(pp[:, 9, :], pp[:, 7, :], 0.5, None, OP.mult, None, PB, PB)
            sin_acc(pp[:, 9, :], pp[:, 9, :], PB, PB, pp[:, 10, :], PB, pp[:, 11, :], PB, pp[:, 3, :], PB)
            tt(pp[:, 9, :], pp[:, 9, :], pp[:, 9, :], OP.mult, PB, PB)
            ts(pp[:, 9, :], pp[:, 9, :], -2.0, 1.0, OP.mult, OP.add, PB, PB)
            rcol = pp[:, 5, :]
            thr = pp[:, 4, :]
            for pr in range(32):
                o = 5 * (pr % 2)
                ts(S[o + 0], iota1[:], thr[:, pr:pr + 1], None, OP.mult, None, [miscB, ppB], B1(o + 0))
                sincos(S[o + 0], B1(o + 0), S[o + 1], B1(o + 1), S[o + 2], B1(o + 2),
                       S[o + 3], B1(o + 3), S[o + 4], B1(o + 4))
                dma("sp", tabs[pr, :, 0:512], S[o + 2], B1(o + 2), [dramB])
                dma("sp", tabs[pr, :, 512:1024], S[o + 1], B1(o + 1), [dramB])
            E.barrier()
            if l + 1 < L:
                convert_layer(l + 1)

            for j in range(NB):
                cols = slice(j * T, (j + 1) * T)
                dma("sp", xf[:], xsrc[:, cols].rearrange("(c p) t -> p c t", p=128), [dramB], xfB)
                act(xb2, xf2, AF.Copy, xfB, xbB)
                ts(xf2, xf2, ALPHA, None, OP.mult, None, xfB, xfB)
                ffn(l, "gu1", "d1")
                layer_norm(l, 0)
                dma("sp", xs[:, cols].rearrange("(c p) t -> p c t", p=128), xf[:], xfB, [dramB])

                def out_qku(oc, pairs, reads):
                    b = bank()
                    mm(b, pairs, reads)
                    slot = oc if oc < 16 else oc - 8
                    act(hb[:, slot, :], psum[b][:], AF.Copy, [psB[b]], [hbB[slot]])
                linear("win", l, lambda kc: xb[:, kc, :], xbB, out_qku, groups=[0, 1, 2, 3, 4, 5, 6, 7])
                linear("win", l, lambda kc: xb[:, kc, :], xbB, out_qku, groups=[12, 13, 14, 15])
                vst = hb2[:, 24 * 512:32 * 512].rearrange("p (s d) -> p s d", s=4)

                def out_v(g, tsi, pairs, reads):
                    b = bank()
                    mm(b, pairs, reads, out_ap=psum[b][:, 0:256])
                    gg = g - 8
                    act(vst[:, tsi, gg * 256:(gg + 1) * 256], psum[b][:, 0:256], AF.Copy, [psB[b]], hbB[24:32])
                linear("win", l, lambda kc, tsi: xb[:, kc, tsi * 128:(tsi + 1) * 128], xbB, out_v,
                       swap=True, ntok=4, groups=[8, 9, 10, 11])
                dma("sp", Qs[:, cols].rearrange("(c p) t -> p c t", p=128), hb[:, 0:8, :], hbB[0:8], [dramB])
                for h_ in range(8):
                    dma("sp", kin[h_][:, cols], hb[:, 8 + h_, :], [hbB[8 + h_]], [dramB])
                dma("sp", Us[:, cols].rearrange("(c p) t -> p c t", p=128), hb[:, 16:24, :], hbB[16:24], [dramB])
                dma("sp", vin[j].rearrange("(s p) d -> p s d", p=128), vst, hbB[24:32], [dramB])
            E.barrier()
            if KSTOP == "A":
                break

            for a_, b_ in list(zip(kin, kout)) + list(zip(vin, vout)):
                E.collective(lambda e, a=a_, b=b_: e.collective_compute(
                    "AllGather", OP.bypass, replica_groups=[[0, 1, 2, 3], [4, 5, 6, 7]],
                    ins=[a.opt()], outs=[b.opt()]), [], [dramB])
            if KSTOP == "X":
                E.barrier()
                break
            ut = hb[:, 16:24, :]
            utB = hbB[16:24]
            TBL = [(S[8], S[9]), (S[10], S[11])]
            TBLB = [(SB_[8], SB_[9]), (SB_[10], SB_[11])]

            ssm_rr = [0]

            def ssm_front(j, pr, init_re, init_im, initB):
                cc, j4 = pr // 4, pr % 4
                hf, jj = j4 // 2, j4 % 2
                ti = pr % 2
                cs, sn = TBL[ti]
                csB, snB = TBLB[ti]
                dma("sp", cs, tabs[pr, :, 0:512], [dramB], [csB])
                dma("sp", sn, tabs[pr, :, 512:1024], [dramB], [snB])
                bR = ssm_rr[0] % 6
                bI = (ssm_rr[0] + 1) % 6
                ssm_rr[0] += 2
                hs = slice(64 * hf, 64 * hf + 64)
                mm(bR, [(WB64[hs, cc, jj, 0, :], ut[hs, cc, :])], [ssmwB, utB[cc]])
                mm(bI, [(WB64[hs, cc, jj, 1, :], ut[hs, cc, :])], [ssmwB, utB[cc]])
                o = 4 * ti
                t1, t2, t3, t4 = S[o + 0], S[o + 1], S[o + 2], S[o + 3]
                tB_ = [SB_[o + 0], SB_[o + 1], SB_[o + 2], SB_[o + 3]]
                tt(t1, psum[bR][:], cs, OP.mult, [psB[bR], csB], [tB_[0]])
                tt(t2, psum[bI][:], sn, OP.mult, [psB[bI], snB], [tB_[1]])
                tt(t1, t1, t2, OP.add, [tB_[0], tB_[1]], [tB_[0]])
                tt(t3, psum[bI][:], cs, OP.mult, [psB[bI], csB], [tB_[2]])
                tt(t4, psum[bR][:], sn, OP.mult, [psB[bR], snB], [tB_[3]])
                tt(t3, t3, t4, OP.subtract, [tB_[2], tB_[3]], [tB_[2]])
                rb = rcol[:, pr:pr + 1].broadcast_to([128, 512])
                E.op("dve", lambda e: e.tensor_tensor_scan(t2, rb, t1, init_re, OP.mult, OP.add),
                     [tB_[0], ppB] + initB, [tB_[1]])
                E.op("dve", lambda e: e.tensor_tensor_scan(t4, rb, t3, init_im, OP.mult, OP.add),
                     [tB_[2], ppB] + initB, [tB_[3]])
                return t2, t4, tB_[1], tB_[3], cs, sn, csB, snB, t1, t3, tB_[0], tB_[2]

            for j in range(NB):
                cols = slice(j * T, (j + 1) * T)
                dma("sp", ut, Us[:, cols].rearrange("(c p) t -> p c t", p=128), [dramB], utB)
                for pr in range(32):
                    wr, wi, wrB, wiB = ssm_front(j, pr, 0.0, 0.0, [])[0:4]
                    act(wend[:, 2 * pr:2 * pr + 1], wr[:, 511:512], AF.Copy, [wrB], [wendB])
                    act(wend[:, 2 * pr + 1:2 * pr + 2], wi[:, 511:512], AF.Copy, [wiB], [wendB])
                dma("sp", sin_[j:j + 1, :].rearrange("o (p q) -> (o p) q", p=128), wend[:], [wendB], [dramB])
            E.barrier()
            E.collective(lambda e, a=sin_, b=sout: e.collective_compute(
                "AllGather", OP.bypass, replica_groups=[[0, 1, 2, 3], [4, 5, 6, 7]],
                ins=[a.opt()], outs=[b.opt()]), [], [dramB])
            E.barrier()
            if KSTOP == "S1":
                break
            EA = xf2[:, 12 * 512:16 * 512].rearrange("p (g q) -> p g q", g=32)[:, 0:4 * NB, :]
            EAB = SB_[12:16]
            HALL = xf2[:, 8 * 512:12 * 512].rearrange("p (g q) -> p g q", g=32)[:, 0:4 * NB, :]
            HB_ = SB_[8:12]
            dma("sp", EA, sout.rearrange("g (p q) -> p g q", p=128), [dramB], EAB)
            E.op("dve", lambda e: e.memset(xf2[:, 8 * 512:12 * 512], 0.0), [], HB_)
            tmpa = S[0][:, 0:32]
            tmpb = S[0][:, 32:64]
            tmpc = S[0][:, 64:96]
            TB0 = [SB_[0]]
            for gb in range(4 * NB - 1):
                rr_, j_ = gb % 4, gb // 4
                row = rr_ * NB + j_
                Xr = HALL[:, gb, 0:64:2]
                Xi = HALL[:, gb, 1:64:2]
                Nr = HALL[:, gb + 1, 0:64:2]
                Ni = HALL[:, gb + 1, 1:64:2]
                Wr = EA[:, row, 0:64:2]
                Wi = EA[:, row, 1:64:2]
                tt(tmpa, Xr, pp[:, 6, :], OP.mult, HB_ + [ppB], TB0)
                tt(tmpa, tmpa, Wr, OP.add, TB0 + EAB, TB0)
                tt(tmpb, Xi, pp[:, 6, :], OP.mult, HB_ + [ppB], TB0)
                tt(tmpb, tmpb, Wi, OP.add, TB0 + EAB, TB0)
                tt(Nr, tmpa, pp[:, 9, :], OP.mult, TB0 + [ppB], HB_)
                tt(tmpc, tmpb, pp[:, 8, :], OP.mult, TB0 + [ppB], TB0)
                tt(Nr, Nr, tmpc, OP.subtract, HB_ + TB0, HB_)
                tt(Ni, tmpa, pp[:, 8, :], OP.mult, TB0 + [ppB], HB_)
                tt(tmpc, tmpb, pp[:, 9, :], OP.mult, TB0 + [ppB], TB0)
                tt(Ni, Ni, tmpc, OP.add, HB_ + TB0, HB_)
            for j in range(NB):
                ts(hmine[:, j, :], HALL[:, 4 * j, :], sel[:, 0:1], None, OP.mult, None, HB_ + [miscB], [hmB])
                for rr_ in range(1, 4):
                    stt(hmine[:, j, :], HALL[:, 4 * j + rr_, :], sel[:, rr_:rr_ + 1], hmine[:, j, :],
                        OP.mult, OP.add, HB_ + [miscB, hmB], [hmB])
            E.barrier()

            if KSTOP == "PFX":
                break
            kt = hb2[:, 0:4 * NT].rearrange("p (r t) -> p r t", r=4)
            ktB = hbB[0:32]
            nsr = NT // 128
            vt = xf2.bitcast(BF16)[:, 0:4 * NT].rearrange("p (n d) -> p n d", d=128)
            vtB = xfB
            qt = wbuf[0][:, 0:NT]
            qtB = [wbufB[0]]
            at = xb2.bitcast(F32).rearrange("p (c t) -> p c t", c=8)
            atB = xbB
            for h in range(8):
                dma("sp", kt, kout[h].rearrange("(r p) t -> p r t", p=128), [dramB], ktB)
                dma("sp", qt, Qs[h * 128:(h + 1) * 128, :], [dramB], qtB)
                vt4 = xf2.bitcast(BF16)[:, 0:4 * NT].rearrange("p (r n d) -> p r n d", r=4, d=128)
                for j_ in range(NB):
                    for rr_ in range(4):
                        dma("sp", vt4[:, rr_, j_ * 4:(j_ + 1) * 4, :],
                            vout[j_][rr_ * T:(rr_ + 1) * T, h * 128:(h + 1) * 128].rearrange("(n p) d -> p n d", p=128),
                            [dramB], vtB)
                for jq in range(NB):
                    qcols = slice(jq * T, (jq + 1) * T)
                    kbs = [(rr_, j_, sbi) for j_ in range(jq + 1) for rr_ in range(4) for sbi in range(4)]
                    nk = len(kbs)
                    OB = [4, 5]
                    LBk = [6, 7]

                    def S_mm(i):
                        rr_, j_, sbi = kbs[i]
                        kc0 = j_ * T + sbi * 128
                        for m in range(2):
                            b = (i % 2) * 2 + m
                            ms = slice(64 * m, 64 * m + 64)
                            pairs = [(kt[ms, rr_, kc0:kc0 + 128], qt[ms, qcols])]
                            reads = ktB + qtB
                            if j_ == jq:
                                idx = rr_ * 4 + sbi
                                pairs.append((kmask[0:32, idx * 128:(idx + 1) * 128], qmask[0:32, :]))
                                reads = reads + [miscB]
                            mm(b, pairs, reads)

                    S_mm(0)
                    for i in range(nk):
                        if i + 1 < nk:
                            S_mm(i + 1)
                        rr_, j_, sbi = kbs[i]
                        for m in range(2):
                            b = (i % 2) * 2 + m
                            act(pt[i % 2][m][:], psum[b][:], AF.Exp, [psB[b]], [ptB[i % 2][m]], scale=0.125)
                        vi = rr_ * nsr + j_ * 4 + sbi
                        for m in range(2):
                            mm(OB[m], [(vt[:, vi, :], pt[i % 2][m][:])], vtB + [ptB[i % 2][m]],
                               start=(i == 0), stop=(i == nk - 1))
                            accB = [atB[8 + 2 * m], atB[9 + 2 * m]]
                            if i == 0:
                                E.op("dve", lambda e, o_=at[:, 4 + m, :], p_=pt[i % 2][m][:]: e.tensor_copy(o_, p_),
                                     [ptB[i % 2][m]], accB)
                            else:
                                tt(at[:, 4 + m, :], at[:, 4 + m, :], pt[i % 2][m][:], OP.add,
                                   [ptB[i % 2][m]] + accB, accB)
                    for m in range(2):
                        mm(LBk[m], [(ones32[:], at[:, 4 + m, :])], [miscB, atB[8 + 2 * m], atB[9 + 2 * m]])
                    E.op("dve", lambda e: e.reciprocal(at[:, 0, :], psum[6][:]), [psB[6]], [atB[0], atB[1]])
                    E.op("dve", lambda e: e.reciprocal(at[:, 1, :], psum[7][:]), [psB[7]], [atB[2], atB[3]])
                    tt(at[:, 2, :], psum[4][:], at[:, 0, :], OP.mult, [psB[4], atB[0], atB[1]], [atB[4], atB[5]])
                    tt(at[:, 3, :], psum[5][:], at[:, 1, :], OP.mult, [psB[5], atB[2], atB[3]], [atB[6], atB[7]])
                    stt(at[:, 2, :], at[:, 3, :], neglam, at[:, 2, :], OP.mult, OP.add,
                        [atB[4], atB[5], atB[6], atB[7], miscB], [atB[4], atB[5]])
                    act(sg[0][:], at[:, 2, :], AF.Square, [atB[4], atB[5]], [sgB[0]])
                    mm(0, [(ones[:], sg[0][:])], [miscB, sgB[0]])
                    ts(at[:, 0, :], psum[0][:], 1.0 / 128, RMS_EPS, OP.mult, OP.add, [psB[0]], [atB[0], atB[1]])
                    act(at[:, 0, :], at[:, 0, :], AF.Sqrt, [atB[0], atB[1]], [atB[0], atB[1]])
                    E.op("dve", lambda e: e.reciprocal(at[:, 0, :], at[:, 0, :]), [atB[0], atB[1]], [atB[0], atB[1]])
                    tt(at[:, 2, :], at[:, 2, :], at[:, 0, :], OP.mult, [atB[0], atB[1], atB[4], atB[5]],
                       [atB[4], atB[5]])
                    ts(sg[1][:], at[:, 2, :], gsc, None, OP.mult, None, [atB[4], atB[5], miscB], [sgB[1]])
                    dma("sp", As[h * 128:(h + 1) * 128, qcols], sg[1][:], [sgB[1]], [dramB])
            E.barrier()

            if KSTOP == "ATT":
                break
            cat = hb[:, 0:16, :]
            ygb = hb[:, 24:32, :]
            for j in range(NB):
                cols = slice(j * T, (j + 1) * T)
                dma("sp", ut, Us[:, cols].rearrange("(c p) t -> p c t", p=128), [dramB], utB)
                dma("sp", hb[:, 0:8, :], As[:, cols].rearrange("(c p) t -> p c t", p=128), [dramB], hbB[0:8])
                for cc in range(8):
                    by = 6 + (cc % 2)
                    for hf in range(2):
                        prs = [4 * cc + 2 * hf, 4 * cc + 2 * hf + 1]
                        pairs = []
                        rds = [ssmwB]
                        for q_, pr in enumerate(prs):
                            ire = hmine[:, j, 2 * pr:2 * pr + 1]
                            iim = hmine[:, j, 2 * pr + 1:2 * pr + 2]
                            wr, wi, wrB, wiB, cs, sn, csB, snB, u1, u3, u1B, u3B = ssm_front(j, pr, ire, iim, [hmB])
                            xr = xpair[:, 2 * q_, :]
                            xi = xpair[:, 2 * q_ + 1, :]
                            tt(u1, wr, cs, OP.mult, [wrB, csB], [u1B])
                            tt(u3, wi, sn, OP.mult, [wiB, snB], [u3B])
                            tt(xr, u1, u3, OP.subtract, [u1B, u3B], [xpB[2 * q_]])
                            tt(u1, wr, sn, OP.mult, [wrB, snB], [u1B])
                            tt(u3, wi, cs, OP.mult, [wiB, csB], [u3B])
                            tt(xi, u1, u3, OP.add, [u1B, u3B], [xpB[2 * q_ + 1]])
                            pairs.append((WC64[:, pr, 0, :], xr))
                            pairs.append((WC64[:, pr, 1, :], xi))
                            rds += [xpB[2 * q_], xpB[2 * q_ + 1]]
                        mm(by, pairs, rds, out_ap=psum[by][64 * hf:64 * hf + 64, :])
                    stt(yf[:, cc, :], ut[:, cc, :], ssmd[:, l * 8 + cc:l * 8 + cc + 1], psum[by][:], OP.mult, OP.add,
                        [utB[cc], miscB, psB[by]], [xbB[2 * cc], xbB[2 * cc + 1]])
                    act(yf[:, cc, :], yf[:, cc, :], AF.Gelu_apprx_tanh, [xbB[2 * cc], xbB[2 * cc + 1]],
                        [xbB[2 * cc], xbB[2 * cc + 1]])
                    act(ygb[:, cc, :], yf[:, cc, :], AF.Copy, [xbB[2 * cc], xbB[2 * cc + 1]], [hbB[24 + cc]])

                def out_glu(oc, pairs, reads):
                    b = bank()
                    mm(b, pairs, reads)
                    act(sg[oc % 2][:], psum[b][:], AF.Sigmoid, [psB[b], miscB], [sgB[oc % 2]],
                        bias=glub[:, l * 8 + oc:l * 8 + oc + 1])
                    tt(hb[:, 8 + oc, :], yf[:, oc, :], sg[oc % 2][:], OP.mult,
                       [xbB[2 * oc], xbB[2 * oc + 1], sgB[oc % 2]], [hbB[8 + oc]])
                linear("glu", l, lambda kc: ygb[:, kc, :], hbB[24:32], out_glu)

                dma("sp", xf[:], xs[:, cols].rearrange("(c p) t -> p c t", p=128), [dramB], xfB)
                ts(xf2, xf2, ALPHA, None, OP.mult, None, xfB, xfB)

                def out_res(oc, pairs, reads):
                    b = bank()
                    mm(b, pairs, reads)
                    tt(xf[:, oc, :], psum[b][:], xf[:, oc, :], OP.add, [psB[b], xfB[oc]], [xfB[oc]])
                if dbg:
                    dma("sp", dbg_cat[:, cols].rearrange("(c p) t -> p c t", p=128), hb[:, 0:16, :], hbB[0:16], [dramB])
                    dma("sp", dbg_x1[:, cols].rearrange("(c p) t -> p c t", p=128), xf[:], xfB, [dramB])
                linear("wout", l, lambda kc: hb[:, kc, :], hbB[0:16], out_res)
                layer_norm(l, 1)
                if dbg:
                    dma("sp", dbg_x2[:, cols].rearrange("(c p) t -> p c t", p=128), xf[:], xfB, [dramB])
                ts(xf2, xf2, ALPHA, None, OP.mult, None, xfB, xfB)

                def out_q(oc, pairs, reads):
                    b = bank()
                    mm(b, pairs, reads)
                    act(hb[:, oc, :], psum[b][:], AF.Copy, [psB[b]], [hbB[oc]])
                linear("xq", l, lambda kc: xb[:, kc, :], xbB, out_q)
                xsc = 512.0 ** -0.5
                for hh in range(4):
                    for mh in range(2):
                        b = bank()
                        mm(b, [(kmT[:, 4 * hh + dd, mh * 128:(mh + 1) * 128], hb[:, 4 * hh + dd, :]) for dd in range(4)],
                           [kvmB] + hbB[4 * hh:4 * hh + 4])
                        act(pt[0][mh][:], psum[b][:], AF.Exp, [psB[b]], [ptB[0][mh]], scale=xsc)
                    bl = bank()
                    mm(bl, [(ones[:], pt[0][mh][:]) for mh in range(2)], [miscB, ptB[0][0], ptB[0][1]])
                    rl = lnt[:, 0, :]
                    E.op("dve", lambda e, bl=bl, rl=rl: e.reciprocal(rl, psum[bl][:]), [psB[bl]],
                         hbB[32:34] + [lntB[0]])
                    for dd in range(4):
                        b = bank()
                        mm(b, [(vmem[:, mh, (4 * hh + dd) * 128:(4 * hh + dd + 1) * 128], pt[0][mh][:]) for mh in range(2)],
                           [kvmB, ptB[0][0], ptB[0][1]])
                        tt(hb[:, 16 + 4 * hh + dd, :], psum[b][:], rl, OP.mult, [psB[b], lntB[0]] + hbB[32:34],
                           [hbB[16 + 4 * hh + dd]])
                linear("xo", l, lambda kc: hb[:, 16 + kc, :], hbB[16:32], out_res)
                layer_norm(l, 2)
                if dbg:
                    dma("sp", dbg_x3[:, cols].rearrange("(c p) t -> p c t", p=128), xf[:], xfB, [dramB])
                ts(xf2, xf2, ALPHA, None, OP.mult, None, xfB, xfB)

                ffn(l, "gu2", "d2")
                layer_norm(l, 3)
                dst = outT if l == L - 1 else xs
                dma("sp", dst[:, cols].rearrange("(c p) t -> p c t", p=128), xf[:], xfB, [dramB])
            E.barrier()

        with nc.Block() as block:
            def replay(eng, name):
                for waits, fn, inc in E.s[name].ops:
                    for sem, val in waits:
                        eng.wait_ge(sems[sem], val)
                    if fn is None:
                        continue
                    ins = fn(eng)
                    if inc[1] is None:
                        ins.then_inc(sems[inc[0]])
                    else:
                        ins.then_inc(sems[inc[0]], inc[1])

            @block.tensor
            def _(e):
                replay(e, "pe")

            @block.scalar
            def _(e):
                replay(e, "act")

            @block.vector
            def _(e):
                replay(e, "dve")

            @block.gpsimd
            def _(e):
                replay(e, "pool")

            @block.sync
            def _(e):
                replay(e, "sp")

    return nc


def _gt(W, Kc, gc):
    K, N = W.shape
    G = N // gc
    return np.ascontiguousarray(W.reshape(Kc, 128, G, gc).transpose(2, 1, 0, 3)).reshape(G * 128, Kc * gc)


def _gt_ksplit(W, Kc, gc):
    K, N = W.shape
    G = N // gc
    a = W.reshape(2, Kc, 128, G, gc).transpose(3, 0, 2, 1, 4)
    return np.ascontiguousarray(a).reshape(G * 2 * 128, Kc * gc)


def _interleave_gu(wg, wu):
    K, Fd = wg.shape
    out = np.empty((K, 2 * Fd), np.float32)
    o = out.reshape(K, Fd // 128, 2, 128)
    o[:, :, 0, :] = wg.reshape(K, Fd // 128, 128)
    o[:, :, 1, :] = wu.reshape(K, Fd // 128, 128)
    return out


_CACHE = {}
_DBG = False
_LAST = {}


def kernel(**inp):
    x = np.asarray(inp["x"], np.float32)
    mem = np.asarray(inp["mem"], np.float32)
    Bsz, SEQ, _ = x.shape
    L = inp["w_in"].shape[0]
    NB = SEQ // (4 * T)
    NT = NB * T
    key = (NB, L)
    if key not in _CACHE:
        _CACHE[key] = build_nc(NB, L, dbg=_DBG)
    nc = _CACHE[key]

    f32 = lambda a: np.asarray(a, np.float32)
    shared = {}
    ws = {k: [] for k in WK}
    for l in range(L):
        ws["gu1"].append(_gt(_interleave_gu(f32(inp["ffn1_w_gate"][l]), f32(inp["ffn1_w_up"][l])), 16, 256))
        ws["d1"].append(_gt_ksplit(f32(inp["ffn1_w_down"][l]), 22, 128))
        ws["win"].append(_gt(f32(inp["w_in"][l]), 16, 256))
        ws["glu"].append(_gt(f32(inp["ssm_glu_w"][l]), 8, 512))
        ws["wout"].append(_gt(f32(inp["w_out"][l]), 16, 256))
        ws["xq"].append(_gt(f32(inp["xattn_w_q"][l]), 16, 256))
        ws["xk"].append(_gt(f32(inp["xattn_w_k"][l]), 16, 256))
        ws["xv"].append(_gt(f32(inp["xattn_w_v"][l]), 16, 256))
        ws["xo"].append(_gt(f32(inp["xattn_w_o"][l]), 16, 256))
        ws["gu2"].append(_gt(_interleave_gu(f32(inp["ffn2_w_gate"][l]), f32(inp["ffn2_w_up"][l])), 16, 256))
        ws["d2"].append(_gt_ksplit(f32(inp["ffn2_w_down"][l]), 22, 128))
    for k in WK:
        shared["w_" + k] = np.stack(ws[k], 0)

    def chunkT(v):
        return f32(v).reshape(-1, 128).T

    lnp = np.zeros((128, L, 8, 16), np.float32)
    for l in range(L):
        for i, nm in enumerate(["ln1", "ln2", "ln3", "ln4"]):
            lnp[:, l, 2 * i, :] = chunkT(inp[nm + "_g"][l])
            lnp[:, l, 2 * i + 1, :] = chunkT(inp[nm + "_b"][l])
    shared["lnp"] = lnp.reshape(128, -1)
    lamv = np.zeros((128, L, 4, 64), np.float32)
    for l in range(L):
        for i, nm in enumerate(["lambda_q1", "lambda_k1", "lambda_q2", "lambda_k2"]):
            lamv[:, l, i, :] = f32(inp[nm][l])[None, :]
    shared["lamv"] = lamv.reshape(128, -1)
    shared["dng"] = np.ascontiguousarray(f32(inp["diff_norm_g"]).T)
    shared["ssmd"] = np.concatenate([chunkT(inp["ssm_d"][l]) for l in range(L)], 1)
    shared["glub"] = np.concatenate([chunkT(inp["ssm_glu_b"][l]) for l in range(L)], 1)
    spair = np.zeros((128, L, 3, 32), np.float32)
    scm = np.zeros((128, L, 3, 8, 64), np.float32)
    ssmb = np.zeros((128, L, 2, 8, 64), np.float32)
    ssmc = np.zeros((128, L, 2, 32, 16), np.float32)
    for l in range(L):
        lre, lim, lst = f32(inp["ssm_lambda_re"][l]), f32(inp["ssm_lambda_im"][l]), f32(inp["ssm_log_step"][l])
        lstb = np.broadcast_to(lst[:, None], (64, 64))
        for i, a in enumerate([lre, lim, lstb]):
            spair[:, l, i, :] = a.reshape(32, 2, 64).transpose(1, 2, 0).reshape(128, 32)
            rep = np.repeat(a, 16, axis=0)
            scm[:, l, i] = rep.reshape(8, 128, 64).transpose(1, 0, 2)
        for i, nm in enumerate(["ssm_b_re", "ssm_b_im"]):
            b = f32(inp[nm][l])
            bc = b.transpose(0, 2, 1).reshape(1024, 64)
            ssmb[:, l, i] = bc.reshape(8, 128, 64).transpose(1, 0, 2)
        for i, nm in enumerate(["ssm_c_re", "ssm_c_im"]):
            c = f32(inp[nm][l])
            ssmc[:, l, i] = c.reshape(32, 2, 16, 64).transpose(1, 3, 0, 2).reshape(128, 32, 16)
    shared["spair"] = spair.reshape(128, -1)
    shared["scm"] = scm.reshape(128, -1)
    shared["ssmb"] = ssmb.reshape(128, -1)
    shared["ssmc"] = ssmc.reshape(128, -1)
    shared["iota1"] = np.broadcast_to(np.arange(1, 513, dtype=np.float32)[None, :], (128, 512)).copy()
    km = np.zeros((32, 4, 4, 128), np.float32)
    for rr in range(4):
        for sbi in range(4):
            for kk in range(128):
                km[rr * 8 + sbi * 2 + kk // 64, rr, sbi, kk] = 1.0
    shared["kmask"] = km.reshape(32, 2048)
    bm = np.zeros((128, 4), np.float32)
    for q in range(128):
        jp, gi = q // 32, (q // 16) % 2
        for jj in range(2):
            for g2 in range(2):
                bm[q, jj * 2 + g2] = 1.0 if (jp % 2 == jj and gi == g2) else 0.0
    shared["bmask"] = bm

    in_maps = []
    for c in range(8):
        b, r = c // 4, c % 4
        xb_ = x[b].reshape(NB, 4, T, D)[:, r].reshape(NT, D)
        m = dict(shared)
        m["xT"] = np.ascontiguousarray(xb_.T)
        m["memT"] = np.ascontiguousarray(mem[b].T)
        qm = np.zeros((32, 512), np.float32)
        for cidx in range(32):
            for q in range(512):
                if r * 8 + q // 64 < cidx:
                    qm[cidx, q] = NEGM
        m["qmask"] = qm
        s_ = np.zeros((128, 4), np.float32)
        s_[:, r] = 1.0
        m["sel"] = s_
        in_maps.append(m)

    res = run_bass_kernel_spmd(nc, in_maps, core_ids=list(range(8)))
    if _DBG:
        _LAST["res"] = res
    out = np.empty((Bsz, SEQ, D), np.float32)
    for c in range(8):
        b, r = c // 4, c % 4
        oT = np.asarray(res.results[c]["outT"])
        out[b].reshape(NB, 4, T, D)[:, r] = oT.T.reshape(NB, T, D)
    return out
```

```python
import math
import numpy as np
import concourse.bass as bass
import concourse.mybir as mybir
from concourse.bass_utils import run_bass_kernel_spmd

F32 = mybir.dt.float32
BF16 = mybir.dt.bfloat16
AF = mybir.ActivationFunctionType
OP = mybir.AluOpType
AX = mybir.AxisListType

D = 2048
DC = 16
FF = 5632
FC = 44
T = 512
NMEM = 256
DEPTH_FULL = 4
ALPHA = (2 * DEPTH_FULL) ** 0.25
LN_EPS = 1e-5
RMS_EPS = 1e-5
TWO_PI = 2.0 * math.pi
NEGM = -30000.0
WSLOT = 4096
NWSLOT = 3

WK = {
    "gu1": (16, 256, 44), "d1": (22, 128, 32), "win": (16, 256, 16), "glu": (8, 512, 2),
    "wout": (16, 256, 8), "xq": (16, 256, 8), "xk": (16, 256, 8), "xv": (16, 256, 8),
    "xo": (16, 256, 8), "gu2": (16, 256, 44), "d2": (22, 128, 32),
}
WORDER = ["xk", "xv", "gu1", "d1", "win", "glu", "wout", "xq", "xo", "gu2", "d2"]


class Buf:
    __slots__ = ("name", "w", "r")

    def __init__(self, name):
        self.name = name
        self.w = None
        self.r = {}


class Stream:
    def __init__(self, name):
        self.name = name
        self.ops = []
        self.count = 0
        self.known = {}
        self.dma_k = 0


NDMASEM = 8


class Emitter:
    def __init__(self):
        self.s = {n: Stream(n) for n in ("pe", "act", "dve", "pool", "sp")}
        self.cc_count = 0

    def _waits(self, st, reads, writes, extra=()):
        waits = {}

        def need(ev):
            if ev is None:
                return
            sem, val = ev
            if st.known.get(sem, 0) < val and waits.get(sem, 0) < val:
                waits[sem] = val

        for b in reads:
            need(b.w)
        for b in writes:
            need(b.w)
            for sem, val in b.r.items():
                need((sem, val))
        for ev in extra:
            need(ev)
        return waits

    def _finish(self, st, waits, fn, inc, ev, reads, writes):
        for sem, val in waits.items():
            st.known[sem] = val
        st.ops.append((list(waits.items()), fn, inc))
        if ev is not None:
            for b in reads:
                if b.r.get(ev[0], 0) < ev[1]:
                    b.r[ev[0]] = ev[1]
            for b in writes:
                b.w = ev
                b.r = {}

    def op(self, eng, fn, reads=(), writes=()):
        st = self.s[eng]
        waits = self._waits(st, reads, writes)
        if eng == "pe":
            waits.pop("pe", None)
        st.count += 1
        ev = (eng, st.count)
        self._finish(st, waits, fn, (eng, 1), ev, reads, writes)
        return ev

    def dma(self, eng, fn, reads=(), writes=()):
        st = self.s[eng]
        k = st.dma_k
        st.dma_k += 1
        sem = "%sq%d" % (eng, k % NDMASEM)
        extra = [(sem, 16 * (k // NDMASEM))] if k >= NDMASEM else []
        waits = self._waits(st, reads, writes, extra)
        ev = (sem, 16 * (k // NDMASEM + 1))
        self._finish(st, waits, fn, (sem, 16), ev, reads, writes)
        return ev

    def collective(self, fn, reads=(), writes=()):
        st = self.s["pool"]
        waits = self._waits(st, reads, writes)
        self.cc_count += 1
        ev = ("cc", self.cc_count)
        self._finish(st, waits, fn, ("cc", None), ev, reads, writes)
        return ev

    def barrier(self):
        evs = []
        for n, st in self.s.items():
            if st.count:
                evs.append((n, st.count))
            for i in range(min(st.dma_k, NDMASEM)):
                k = st.dma_k - 1 - i
                evs.append(("%sq%d" % (n, k % NDMASEM), 16 * (k // NDMASEM + 1)))
        if self.cc_count:
            evs.append(("cc", self.cc_count))
        for n, st in self.s.items():
            waits = {}
            for sem, val in evs:
                if sem == n:
                    continue
                if st.known.get(sem, 0) < val and waits.get(sem, 0) < val:
                    waits[sem] = val
            for sem, val in waits.items():
                st.known[sem] = val
            if waits:
                st.ops.append((list(waits.items()), None, None))


def build_nc(NB, depth, dbg=None):
    import os
    KSTOP = os.environ.get("KSTOP", "")
    NT = NB * T
    L = depth
    nc = bass.Bass("TRN2", target_bir_lowering=False)
    E = Emitter()

    def dram_in(name, shape, dt=F32):
        return nc.dram_tensor(name, list(shape), dt, kind="ExternalInput").ap()

    def dram_sc(name, shape, dt):
        return nc.dram_tensor(name, list(shape), dt).ap()

    xT = dram_in("xT", [D, NT])
    memT = dram_in("memT", [D, NMEM])
    wsrc = {}
    wbf = {}
    wB = {}
    for k, (Kc, gc, G) in WK.items():
        wsrc[k] = dram_in("w_" + k, [L, G * 128, Kc * gc])
        wbf[k] = dram_sc("wb_" + k, [L, G * 128, Kc * gc], BF16)
        for l in range(L):
            wB[(k, l)] = Buf("wb_%s_%d" % (k, l))
    lnp_d = dram_in("lnp", [128, L * 8 * 16])
    lamv_d = dram_in("lamv", [128, L * 4 * 64])
    dng_d = dram_in("dng", [128, L])
    ssmd_d = dram_in("ssmd", [128, L * 8])
    glub_d = dram_in("glub", [128, L * 8])
    spair_d = dram_in("spair", [128, L * 3 * 32])
    scm_d = dram_in("scm", [128, L * 3 * 512])
    sb_d = dram_in("ssmb", [128, L * 2 * 512])
    sc_d = dram_in("ssmc", [128, L * 2 * 512])
    iota_d = dram_in("iota1", [128, 512])
    kmask_d = dram_in("kmask", [32, 2048])
    qmask_d = dram_in("qmask", [32, 512])
    bmask_d = dram_in("bmask", [128, 4])
    sel_d = dram_in("sel", [128, 4])
    outT = nc.dram_tensor("outT", [D, NT], F32, kind="ExternalOutput").ap()

    xs = dram_sc("xs", [D, NT], F32)
    Qs = dram_sc("Qs", [1024, NT], BF16)
    Us = dram_sc("Us", [1024, NT], BF16)
    As = dram_sc("As", [1024, NT], BF16)
    kin_l = [[dram_sc("kin%d_%d" % (l, h), [128, NT], BF16) for h in range(8)] for l in range(L)]
    kout_l = [[dram_sc("kout%d_%d" % (l, h), [4 * 128, NT], BF16) for h in range(8)] for l in range(L)]
    vin_l = [[dram_sc("vin%d_%d" % (l, j), [T, 1024], BF16) for j in range(NB)] for l in range(L)]
    vout_l = [[dram_sc("vout%d_%d" % (l, j), [4 * T, 1024], BF16) for j in range(NB)] for l in range(L)]
    sin_l = [dram_sc("sin%d" % l, [NB, 8192], F32) for l in range(L)]
    sout_l = [dram_sc("sout%d" % l, [4 * NB, 8192], F32) for l in range(L)]
    tabs = dram_sc("tabs", [32, 128, 1024], F32)
    dbg_x2 = dbg_x3 = dbg_x1 = dbg_cat = None
    if dbg:
        dbg_x1 = nc.dram_tensor("d_x1", [D, NT], F32, kind="ExternalOutput").ap()
        dbg_x2 = nc.dram_tensor("d_x2", [D, NT], F32, kind="ExternalOutput").ap()
        dbg_x3 = nc.dram_tensor("d_x3", [D, NT], F32, kind="ExternalOutput").ap()
        dbg_cat = nc.dram_tensor("d_cat", [D, NT], BF16, kind="ExternalOutput").ap()

    import contextlib
    with contextlib.ExitStack() as ctx:
        def sb(name, shape, dt):
            return ctx.enter_context(nc.sbuf_tensor("s_" + name, list(shape), dt))

        xf = sb("xf", [128, 16, 512], F32)
        xb = sb("xb", [128, 16, 512], BF16)
        hb = sb("hb", [128, 44, 512], BF16)
        wbuf = [sb("wbuf%d" % i, [128, WSLOT], BF16) for i in range(NWSLOT)]
        WB64 = sb("WB64", [128, 8, 2, 2, 128], BF16)
        WC64 = sb("WC64", [128, 32, 2, 64], BF16)
        kmT = sb("kmT", [128, 16, 256], BF16)
        vmem = sb("vmem", [128, 2, 2048], BF16)
        pt = [[sb("pt%d%d" % (i, m), [128, 512], BF16) for m in range(2)] for i in range(2)]
        sg = [sb("sg%d" % i, [128, 512], BF16) for i in range(2)]
        ones = sb("ones", [128, 128], BF16)
        ones32 = sb("ones32", [128, 128], F32)
        kmask = sb("kmask", [32, 2048], BF16)
        qmask = sb("qmask", [32, 512], BF16)
        lnp = sb("lnp", [128, L * 8 * 16], F32)
        iota1 = sb("iota1", [128, 512], F32)
        lamv = sb("lamv", [128, L * 4 * 64], F32)
        dng = sb("dng", [128, L], F32)
        ssmd = sb("ssmd", [128, L * 8], F32)
        glub = sb("glub", [128, L * 8], F32)
        bmask = sb("bmask", [128, 4], F32)
        sel = sb("sel", [128, 4], F32)
        pp = sb("pp", [128, 12, 32], F32)
        hmine = sb("hmine", [128, NB, 64], F32)
        wend = sb("wend", [128, 64], F32)
        sm = sb("sm", [128, 16], F32)
        xpair = sb("xpair", [128, 4, 512], BF16)
        psum = [ctx.enter_context(nc.psum_tensor("ps%d" % i, [128, 512], F32)) for i in range(8)]

        sems = {}
        for n in ["pe", "act", "dve", "pool", "sp", "cc"]:
            sems[n] = ctx.enter_context(nc.semaphore("m_" + n))
        for q in ["pool", "sp"]:
            for i in range(NDMASEM):
                sems["%sq%d" % (q, i)] = ctx.enter_context(nc.semaphore("m_%sq%d" % (q, i)))

        xfB = [Buf("xf%d" % i) for i in range(16)]
        xbB = [Buf("xb%d" % i) for i in range(16)]
        hbB = [Buf("hb%d" % i) for i in range(44)]
        wbufB = [Buf("wbuf%d" % i) for i in range(NWSLOT)]
        psB = [Buf("ps%d" % i) for i in range(8)]
        ptB = [[Buf("pt"), Buf("pt")], [Buf("pt"), Buf("pt")]]
        sgB = [Buf("sg0"), Buf("sg1")]
        miscB = Buf("misc")
        ssmwB = Buf("ssmw")
        kvmB = Buf("kvm")
        ppB = Buf("pp")
        hmB = Buf("hmine")
        wendB = Buf("wend")
        xpB = [Buf("xp%d" % i) for i in range(4)]
        dramB = Buf("dram")
        ccB = Buf("cc")

        xf2 = xf[:].rearrange("p c t -> p (c t)")
        xb2 = xb[:].rearrange("p c t -> p (c t)")
        hb2 = hb[:].rearrange("p c t -> p (c t)")
        S = [xf[:, i, :] for i in range(16)]
        SB_ = xfB
        yf = xb2.bitcast(F32).rearrange("p (c t) -> p c t", c=8)
        lnt = hb2[:, 32 * 512:42 * 512].bitcast(F32).rearrange("p (c t) -> p c t", c=5)
        lntB = [Buf("lnt%d" % i) for i in range(5)]

        ps_rr = [0]

        def bank():
            i = ps_rr[0]
            ps_rr[0] = (i + 1) % 8
            return i

        def mm(bi, pairs, reads, start=True, stop=True, out_ap=None):
            o = out_ap if out_ap is not None else psum[bi][:]
            n = len(pairs)

            def fn(pe, pairs=pairs, o=o, n=n, start=start, stop=stop):
                ins = None
                for i, (l_, r_) in enumerate(pairs):
                    ins = pe.matmul(o, l_, r_, start=(start and i == 0), stop=(stop and i == n - 1))
                return ins
            return E.op("pe", fn, reads=reads, writes=[psB[bi]])

        def act(out, in_, func, reads, writes, bias=None, scale=None):
            kw = {}
            if bias is not None:
                kw["bias"] = bias
            if scale is not None:
                kw["scale"] = scale
            return E.op("act", lambda e: e.activation(out=out, in_=in_, func=func, **kw), reads, writes)

        def tt(out, a, b, op, reads, writes):
            return E.op("dve", lambda e: e.tensor_tensor(out, a, b, op), reads, writes)

        def ts(out, a, s1, s2, op0, op1, reads, writes):
            if op1 is None:
                return E.op("dve", lambda e: e.tensor_scalar(out, a, s1, None, op0), reads, writes)
            return E.op("dve", lambda e: e.tensor_scalar(out, a, s1, s2, op0, op1), reads, writes)

        def stt(out, a, sc, b, op0, op1, reads, writes):
            return E.op("dve", lambda e: e.scalar_tensor_tensor(out, a, sc, b, op0, op1), reads, writes)

        def dma(q, out, in_, reads, writes):
            return E.dma(q, lambda e: e.dma_start(out=out, in_=in_), reads, writes)

        dma("sp", lnp[:], lnp_d, [], [miscB])
        dma("sp", iota1[:], iota_d, [], [miscB])
        dma("sp", lamv[:], lamv_d, [], [miscB])
        dma("sp", dng[:], dng_d, [], [miscB])
        dma("sp", ssmd[:], ssmd_d, [], [miscB])
        dma("sp", glub[:], glub_d, [], [miscB])
        dma("sp", bmask[:], bmask_d, [], [miscB])
        dma("sp", sel[:], sel_d, [], [miscB])
        dma("pool", kmask[:], kmask_d, [], [miscB])
        dma("pool", qmask[:], qmask_d, [], [miscB])
        E.op("dve", lambda e: e.memset(ones[:], 1.0), [], [miscB])
        E.op("dve", lambda e: e.memset(ones32[:], 1.0), [], [miscB])

        def convert_layer(l):
            for k in WORDER:
                Kc, gc, G = WK[k]
                rows = G * 128
                step = 1024
                for r0 in range(0, rows, step):
                    r1 = min(rows, r0 + step)
                    dma("pool", wbf[k][l, r0:r1, :], wsrc[k][l, r0:r1, :], [], [wB[(k, l)]])

        wslot = [0]

        def linear_ksplit(k, l, rhs_fn, rhs_bufs, out_fn):
            Kc, gc, G = WK[k]
            for oc in range(G // 2):
                pairs = []
                rds = []
                for half in range(2):
                    g = 2 * oc + half
                    s_ = wslot[0]
                    wslot[0] = (s_ + 1) % NWSLOT
                    wt = wbuf[s_]
                    dma("sp", wt[:, :Kc * gc], wbf[k][l, g * 128:(g + 1) * 128, :], [wB[(k, l)]], [wbufB[s_]])
                    wv = wt[:, :Kc * gc].rearrange("p (k m) -> p k m", k=Kc)
                    pairs += [(wv[:, kc, :], rhs_fn(half * Kc + kc)) for kc in range(Kc)]
                    rds.append(wbufB[s_])
                out_fn(oc, pairs, rds + rhs_bufs)

        def linear(k, l, rhs_fn, rhs_bufs, out_fn, swap=False, groups=None, ntok=None):
            Kc, gc, G = WK[k]
            for g in (groups if groups is not None else range(G)):
                s_ = wslot[0]
                wslot[0] = (s_ + 1) % NWSLOT
                wt = wbuf[s_]
                dma("sp", wt[:, :Kc * gc], wbf[k][l, g * 128:(g + 1) * 128, :], [wB[(k, l)]], [wbufB[s_]])
                wv = wt[:, :Kc * gc].rearrange("p (k m) -> p k m", k=Kc)
                if not swap:
                    for mi in range(gc // 128):
                        oc = g * (gc // 128) + mi
                        pairs = [(wv[:, kc, mi * 128:(mi + 1) * 128], rhs_fn(kc)) for kc in range(Kc)]
                        out_fn(oc, pairs, [wbufB[s_]] + rhs_bufs)
                else:
                    for tsi in range(ntok):
                        pairs = [(rhs_fn(kc, tsi), wv[:, kc, :]) for kc in range(Kc)]
                        out_fn(g, tsi, pairs, [wbufB[s_]] + rhs_bufs)

        def layer_norm(l, idx):
            g0 = (l * 8 + 2 * idx) * 16
            b0 = (l * 8 + 2 * idx + 1) * 16
            act(xb2, xf2, AF.Copy, xfB, xbB)
            act(hb2[:, 0:8192], xf2, AF.Square, xfB, hbB[0:16])
            b1 = bank()
            mm(b1, [(ones[:], xb[:, c, :]) for c in range(16)], xbB + [miscB])
            b2 = bank()
            mm(b2, [(ones[:], hb[:, c, :]) for c in range(16)], hbB[0:16] + [miscB])
            mt, vt, rstd, nmr = lnt[:, 0, :], lnt[:, 1, :], lnt[:, 2, :], lnt[:, 3, :]
            LB = hbB[32:42]
            ts(mt, psum[b1][:], 1.0 / D, None, OP.mult, None, [psB[b1]], LB + [lntB[0]])
            tt(vt, mt, mt, OP.mult, [lntB[0]], LB + [lntB[1]])
            stt(vt, psum[b2][:], 1.0 / D, vt, OP.mult, OP.subtract, [psB[b2], lntB[1]], LB + [lntB[1]])
            ts(vt, vt, LN_EPS, None, OP.add, None, [lntB[1]], LB + [lntB[1]])
            act(nmr, vt, AF.Sqrt, [lntB[1]], LB + [lntB[3]])
            E.op("dve", lambda e: e.reciprocal(rstd, nmr), [lntB[3]], LB + [lntB[2]])
            tt(nmr, rstd, rstd, OP.mult, [lntB[2]], LB + [lntB[3]])
            tt(nmr, nmr, vt, OP.mult, [lntB[3], lntB[1]], LB + [lntB[3]])
            ts(nmr, nmr, -0.5, 1.5, OP.mult, OP.add, [lntB[3]], LB + [lntB[3]])
            tt(rstd, rstd, nmr, OP.mult, [lntB[2], lntB[3]], LB + [lntB[2]])
            stt(nmr, mt, -1.0, rstd, OP.mult, OP.mult, [lntB[0], lntB[2]], LB + [lntB[3]])
            for c in range(16):
                tt(xf[:, c, :], xf[:, c, :], rstd, OP.mult, [xfB[c], lntB[2]], [xfB[c]])
                tt(xf[:, c, :], xf[:, c, :], nmr, OP.add, [xfB[c], lntB[3]], [xfB[c]])
                act(xb[:, c, :], xf[:, c, :], AF.Identity, [xfB[c], miscB], [xbB[c]],
                    bias=lnp[:, b0 + c:b0 + c + 1], scale=lnp[:, g0 + c:g0 + c + 1])
                act(xf[:, c, :], xf[:, c, :], AF.Identity, [xfB[c], miscB], [xfB[c]],
                    bias=lnp[:, b0 + c:b0 + c + 1], scale=lnp[:, g0 + c:g0 + c + 1])

        def ffn(l, kgu, kd):
            pend = {}

            def out_gu(oc, pairs, reads):
                fc, which = oc // 2, oc % 2
                b = bank()
                mm(b, pairs, reads)
                if which == 0:
                    pend[fc] = b
                else:
                    bg = pend.pop(fc)
                    si = fc % 2
                    act(sg[si][:], psum[bg][:], AF.Silu, [psB[bg]], [sgB[si]])
                    tt(hb[:, fc, :], sg[si][:], psum[b][:], OP.mult, [sgB[si], psB[b]], [hbB[fc]])

            linear(kgu, l, lambda kc: xb[:, kc, :], xbB, out_gu)

            def out_d(oc, pairs, reads):
                b = bank()
                mm(b, pairs, reads)
                stt(xf[:, oc, :], psum[b][:], 0.5, xf[:, oc, :], OP.mult, OP.add, [psB[b], xfB[oc]], [xfB[oc]])

            linear_ksplit(kd, l, lambda kc: hb[:, kc, :], hbB, out_d)

        for l in range(L):
            lam_init = 0.8 - 0.6 * math.exp(-0.3 * l)
            kin, kout, vin, vout, sin_, sout = kin_l[l], kout_l[l], vin_l[l], vout_l[l], sin_l[l], sout_l[l]
            xsrc = xT if l == 0 else xs

            lv = lamv[:, l * 256:(l + 1) * 256]
            t0_, t1_ = lnt[:, 4, 0:64], lnt[:, 4, 64:128]
            tB = [lntB[4]] + hbB[40:42]
            tt(t0_, lv[:, 0:64], lv[:, 64:128], OP.mult, [miscB], tB)
            E.op("dve", lambda e: e.reduce_sum(sm[:, 0:1], t0_, axis=AX.X), tB, [miscB])
            tt(t1_, lv[:, 128:192], lv[:, 192:256], OP.mult, [miscB], tB)
            E.op("dve", lambda e: e.reduce_sum(sm[:, 1:2], t1_, axis=AX.X), tB, [miscB])
            act(sm[:, 0:2], sm[:, 0:2], AF.Exp, [miscB], [miscB])
            stt(sm[:, 2:3], sm[:, 1:2], -lam_init, sm[:, 0:1], OP.add, OP.subtract, [miscB], [miscB])
            ts(sm[:, 3:4], dng[:, l:l + 1], 1.0 - lam_init, None, OP.mult, None, [miscB], [miscB])
            neglam = sm[:, 2:3]
            gsc = sm[:, 3:4]

            memb = hb2[:, 0:4096].rearrange("p (c t) -> p c t", c=16)
            dma("pool", memb, memT.rearrange("(c p) t -> p c t", p=128), [], hbB[0:8])
            if l == 0:
                convert_layer(0)

            def out_km(oc, pairs, reads):
                b = bank()
                mm(b, pairs, reads, out_ap=psum[b][:, 0:256])
                act(kmT[:, oc, :], psum[b][:, 0:256], AF.Copy, [psB[b]], [kvmB])
            linear("xk", l, lambda kc: memb[:, kc, :], hbB[0:8], out_km)

            def out_vm(g, tsi, pairs, reads):
                b = bank()
                mm(b, pairs, reads, out_ap=psum[b][:, 0:256])
                act(vmem[:, tsi, g * 256:(g + 1) * 256], psum[b][:, 0:256], AF.Copy, [psB[b]], [kvmB])
            linear("xv", l, lambda kc, tsi: memb[:, kc, tsi * 128:(tsi + 1) * 128], hbB[0:8], out_vm,
                   swap=True, ntok=2)

            def ld(i, src):
                dma("sp", S[i], src, [], [SB_[i]])
            scm_l = scm_d[:, l * 1536:(l + 1) * 1536]
            ld(0, scm_l[:, 0:512]); ld(1, scm_l[:, 512:1024]); ld(2, scm_l[:, 1024:1536])
            ld(3, sb_d[:, (2 * l) * 512:(2 * l + 1) * 512]); ld(4, sb_d[:, (2 * l + 1) * 512:(2 * l + 2) * 512])

            def exp_acc(out, x, xB, outB, t, tB, q, qB):
                cc_ = -4.6
                n = 24
                ts(t, x, -cc_, None, OP.add, None, xB, tB)
                ts(q, t, 1.0 / math.factorial(n), None, OP.mult, None, tB, qB)
                for k in range(n - 1, 0, -1):
                    stt(q, q, 1.0 / math.factorial(k), t, OP.add, OP.mult, qB + tB, qB)
                ts(out, q, 1.0, math.exp(cc_), OP.add, OP.mult, qB, outB)

            def expm1_small(out, x, xB, outB, q, qB):
                n = 7
                ts(q, x, 1.0 / math.factorial(n), None, OP.mult, None, xB, qB)
                for k in range(n - 1, 0, -1):
                    stt(q, q, 1.0 / math.factorial(k), x, OP.add, OP.mult, qB + xB, qB)
                ts(out, q, 1.0, None, OP.mult, None, qB, outB)

            def sin_acc(out, x, xB, outB, ta, taB, tb, tbB, tc, tcB):
                I32 = mybir.dt.int32
                ts(ta, x, 1.0 / TWO_PI, None, OP.mult, None, xB, taB)
                E.op("dve", lambda e: e.tensor_copy(tb.bitcast(I32), ta), taB, tbB)
                E.op("dve", lambda e: e.tensor_copy(ta, tb.bitcast(I32)), tbB, taB)
                stt(ta, ta, -TWO_PI, x, OP.mult, OP.add, taB + xB, taB)
                ts(tb, ta, math.pi, None, OP.is_gt, None, taB, tbB)
                stt(ta, tb, -TWO_PI, ta, OP.mult, OP.add, taB + tbB, taB)
                ts(tb, ta, -math.pi, None, OP.is_lt, None, taB, tbB)
                stt(ta, tb, TWO_PI, ta, OP.mult, OP.add, taB + tbB, taB)
                ts(tb, ta, -1.0, math.pi, OP.mult, OP.add, taB, tbB)
                tt(tb, ta, tb, OP.min, taB + tbB, tbB)
                ts(tc, ta, -1.0, -math.pi, OP.mult, OP.add, taB, tcB)
                tt(ta, tb, tc, OP.max, tbB + tcB, taB)
                tt(tb, ta, ta, OP.mult, taB, tbB)
                n = 7
                cf = [((-1.0) ** k) / math.factorial(2 * k + 1) for k in range(n + 1)]
                ts(tc, tb, cf[n], None, OP.mult, None, tbB, tcB)
                for k in range(n - 1, 0, -1):
                    stt(tc, tc, cf[k], tb, OP.add, OP.mult, tcB + tbB, tcB)
                stt(out, tc, 1.0, ta, OP.add, OP.mult, tcB + taB, outB)

            def sincos(x_ap, xB, sn_ap, snB, cs_ap, csB, ta, taB, tb, tbB):
                I32 = mybir.dt.int32
                ts(ta, x_ap, 1.0 / TWO_PI, None, OP.mult, None, xB, taB)
                E.op("dve", lambda e: e.tensor_copy(tb.bitcast(I32), ta), taB, tbB)
                E.op("dve", lambda e: e.tensor_copy(ta, tb.bitcast(I32)), tbB, taB)
                stt(ta, ta, -TWO_PI, x_ap, OP.mult, OP.add, taB + xB, taB)
                ts(tb, ta, math.pi, None, OP.is_gt, None, taB, tbB)
                stt(ta, tb, -TWO_PI, ta, OP.mult, OP.add, taB + tbB, taB)
                ts(tb, ta, -math.pi, None, OP.is_lt, None, taB, tbB)
                stt(ta, tb, TWO_PI, ta, OP.mult, OP.add, taB + tbB, taB)
                ts(ta, ta, math.pi, -math.pi, OP.min, OP.max, taB, taB)
                act(sn_ap, ta, AF.Sin, taB, snB)
                stt(tb, ta, -1.0, ta, OP.mult, OP.max, taB, tbB)
                act(cs_ap, tb, AF.Sin, tbB + [miscB], csB, bias=hpi_col, scale=-1.0)

            hpi_col = sm[:, 4:5]
            E.op("dve", lambda e: e.memset(sm[:, 4:5], math.pi / 2), [], [miscB])

            def B1(i):
                return [SB_[i]]
            exp_acc(S[2], S[2], B1(2), B1(2), S[9], B1(9), S[10], B1(10))
            ts(S[0], S[0], -1e-4, None, OP.min, None, B1(0), B1(0))
            tt(S[5], S[0], S[2], OP.mult, B1(0) + B1(2), B1(5))
            tt(S[6], S[1], S[2], OP.mult, B1(1) + B1(2), B1(6))
            expm1_small(S[5], S[5], B1(5), B1(5), S[9], B1(9))
            sin_acc(S[7], S[6], B1(6), B1(7), S[9], B1(9), S[10], B1(10), S[11], B1(11))
            ts(S[12], S[6], 0.5, None, OP.mult, None, B1(6), B1(12))
            sin_acc(S[8], S[12], B1(12), B1(8), S[9], B1(9), S[10], B1(10), S[11], B1(11))
            tt(S[8], S[8], S[8], OP.mult, B1(8), B1(8))
            ts(S[8], S[8], -2.0, None, OP.mult, None, B1(8), B1(8))
            ts(S[12], S[8], 1.0, None, OP.add, None, B1(8), B1(12))
            tt(S[9], S[5], S[12], OP.mult, B1(5) + B1(12), B1(9))
            tt(S[9], S[9], S[8], OP.add, B1(9) + B1(8), B1(9))
            stt(S[10], S[5], 1.0, S[7], OP.add, OP.mult, B1(5) + B1(7), B1(10))
            tt(S[11], S[0], S[0], OP.mult, B1(0), B1(11))
            tt(S[12], S[1], S[1], OP.mult, B1(1), B1(12))
            tt(S[11], S[11], S[12], OP.add, B1(11) + B1(12), B1(11))
            E.op("dve", lambda e: e.reciprocal(S[11], S[11]), B1(11), B1(11))
            tt(S[12], S[9], S[0], OP.mult, B1(9) + B1(0), B1(12))
            tt(S[13], S[10], S[1], OP.mult, B1(10) + B1(1), B1(13))
            tt(S[12], S[12], S[13], OP.add, B1(12) + B1(13), B1(12))
            tt(S[12], S[12], S[11], OP.mult, B1(12) + B1(11), B1(12))
            tt(S[13], S[10], S[0], OP.mult, B1(10) + B1(0), B1(13))
            tt(S[14], S[9], S[1], OP.mult, B1(9) + B1(1), B1(14))
            tt(S[13], S[13], S[14], OP.subtract, B1(13) + B1(14), B1(13))
            tt(S[13], S[13], S[11], OP.mult, B1(13) + B1(11), B1(13))
            tt(S[14], S[12], S[3], OP.mult, B1(12) + B1(3), B1(14))
            tt(S[15], S[13], S[4], OP.mult, B1(13) + B1(4), B1(15))
            tt(S[14], S[14], S[15], OP.subtract, B1(14) + B1(15), B1(14))
            tt(S[15], S[12], S[4], OP.mult, B1(12) + B1(4), B1(15))
            tt(S[9], S[13], S[3], OP.mult, B1(13) + B1(3), B1(9))
            tt(S[15], S[15], S[9], OP.add, B1(15) + B1(9), B1(15))
            for jj in range(2):
                for gi in range(2):
                    for reim in range(2):
                        src = S[14 + reim].rearrange("p (c n) -> p c n", c=8)
                        dst = WB64[:, :, jj, reim, gi * 64:(gi + 1) * 64]
                        mcol = bmask[:, jj * 2 + gi:jj * 2 + gi + 1]
                        ts(dst, src, mcol, None, OP.mult, None, B1(14 + reim) + [miscB], [ssmwB])
            E.op("dve", lambda e: e.memset(WC64[:].rearrange("p a b c -> p (a b c)"), 0.0), [], [ssmwB])
            ld(0, sc_d[:, (2 * l) * 512:(2 * l + 1) * 512]); ld(1, sc_d[:, (2 * l + 1) * 512:(2 * l + 2) * 512])
            WCv = WC64[:].rearrange("p (a b) r m -> p a b r m", b=2)
            for reim in range(2):
                cv = S[reim].rearrange("p (a b q) -> p a b q", a=16, b=2)
                for gi in range(2):
                    for jj in range(2):
                        dst = WCv[gi * 64:(gi + 1) * 64, :, jj, reim, jj * 32 + gi * 16:jj * 32 + gi * 16 + 16]
                        src = cv[gi * 64:(gi + 1) * 64, :, jj, :]
                        ts(dst, src, 1.0 if reim == 0 else -1.0, None, OP.mult, None, B1(reim), [ssmwB])
            dma("sp", pp[:, 0:3, :].rearrange("p a b -> p (a b)"), spair_d[:, l * 96:(l + 1) * 96], [], [ppB])
            PB = [ppB]
            exp_acc(pp[:, 2, :], pp[:, 2, :], PB, PB, pp[:, 10, :], PB, pp[:, 11, :], PB)
            ts(pp[:, 0, :], pp[:, 0, :], -1e-4, None, OP.min, None, PB, PB)
            tt(pp[:, 3, :], pp[:, 0, :], pp[:, 2, :], OP.mult, PB, PB)
            tt(pp[:, 4, :], pp[:, 1, :], pp[:, 2, :], OP.mult, PB, PB)
            expm1_small(pp[:, 5, :], pp[:, 3, :], PB, PB, pp[:, 10, :], PB)
            ts(pp[:, 5, :], pp[:, 5, :], 1.0, None, OP.add, None, PB, PB)
            I32 = mybir.dt.int32
            ts(pp[:, 10, :], pp[:, 4, :], 1.0 / TWO_PI, None, OP.mult, None, PB, PB)
            E.op("dve", lambda e: e.tensor_copy(pp[:, 11, :].bitcast(I32), pp[:, 10, :]), PB, PB)
            E.op("dve", lambda e: e.tensor_copy(pp[:, 10, :], pp[:, 11, :].bitcast(I32)), PB, PB)
            stt(pp[:, 4, :], pp[:, 10, :], -TWO_PI, pp[:, 4, :], OP.mult, OP.add, PB, PB)
            ts(pp[:, 10, :], pp[:, 4, :], 0.0, None, OP.is_lt, None, PB, PB)
            stt(pp[:, 4, :], pp[:, 10, :], TWO_PI, pp[:, 4, :], OP.mult, OP.add, PB, PB)
            ts(pp[:, 6, :], pp[:, 5, :], 1.0, None, OP.mult, None, PB, PB)
            for _sq in range(9):
                tt(pp[:, 6, :], pp[:, 6, :], pp[:, 6, :], OP.mult, PB, PB)
            ts(pp[:, 7, :], pp[:, 4, :], 512.0, None, OP.mult, None, PB, PB)
            sin_acc(pp[:, 8, :], pp[:, 7, :], PB, PB, pp[:, 10, :], PB, pp[:, 11, :], PB, pp[:, 3, :], PB)
            ts(pp[:, 9, :], pp[:, 7, :], 0.5, None, OP.mult, None, PB, PB)
            sin_acc(pp[:, 9, :], pp[:, 9, :], PB, PB, pp[:, 10, :], PB, pp[:, 11, :], PB, pp[:, 3, :], PB)
            tt(pp[:, 9, :], pp[:, 9, :], pp[:, 9, :], OP.mult, PB, PB)
            ts(pp[:, 9, :], pp[:, 9, :], -2.0, 1.0, OP.mult, OP.add, PB, PB)
            rcol = pp[:, 5, :]
            thr = pp[:, 4, :]
            for pr in range(32):
                o = 5 * (pr % 2)
                ts(S[o + 0], iota1[:], thr[:, pr:pr + 1], None, OP.mult, None, [miscB, ppB], B1(o + 0))
                sincos(S[o + 0], B1(o + 0), S[o + 1], B1(o + 1), S[o + 2], B1(o + 2),
                       S[o + 3], B1(o + 3), S[o + 4], B1(o + 4))
                dma("sp", tabs[pr, :, 0:512], S[o + 2], B1(o + 2), [])
                dma("sp", tabs[pr, :, 512:1024], S[o + 1], B1(o + 1), [])
            E.barrier()
            if l + 1 < L:
                convert_layer(l + 1)

            for j in range(NB):
                cols = slice(j * T, (j + 1) * T)
                dma("sp", xf[:], xsrc[:, cols].rearrange("(c p) t -> p c t", p=128), [], xfB)
                act(xb2, xf2, AF.Copy, xfB, xbB)
                ts(xf2, xf2, ALPHA, None, OP.mult, None, xfB, xfB)
                ffn(l, "gu1", "d1")
                layer_norm(l, 0)
                dma("sp", xs[:, cols].rearrange("(c p) t -> p c t", p=128), xf[:], xfB, [])

                def out_qku(oc, pairs, reads):
                    b = bank()
                    mm(b, pairs, reads)
                    slot = oc if oc < 16 else oc - 8
                    act(hb[:, slot, :], psum[b][:], AF.Copy, [psB[b]], [hbB[slot]])
                linear("win", l, lambda kc: xb[:, kc, :], xbB, out_qku, groups=[0, 1, 2, 3, 4, 5, 6, 7])
                linear("win", l, lambda kc: xb[:, kc, :], xbB, out_qku, groups=[12, 13, 14, 15])
                vst = hb2[:, 24 * 512:32 * 512].rearrange("p (s d) -> p s d", s=4)

                def out_v(g, tsi, pairs, reads):
                    b = bank()
                    mm(b, pairs, reads, out_ap=psum[b][:, 0:256])
                    gg = g - 8
                    act(vst[:, tsi, gg * 256:(gg + 1) * 256], psum[b][:, 0:256], AF.Copy, [psB[b]], hbB[24:32])
                linear("win", l, lambda kc, tsi: xb[:, kc, tsi * 128:(tsi + 1) * 128], xbB, out_v,
                       swap=True, ntok=4, groups=[8, 9, 10, 11])
                dma("sp", Qs[:, cols].rearrange("(c p) t -> p c t", p=128), hb[:, 0:8, :], hbB[0:8], [])
                for h_ in range(8):
                    dma("sp", kin[h_][:, cols], hb[:, 8 + h_, :], [hbB[8 + h_]], [])
                dma("sp", Us[:, cols].rearrange("(c p) t -> p c t", p=128), hb[:, 16:24, :], hbB[16:24], [])
                dma("sp", vin[j].rearrange("(s p) d -> p s d", p=128), vst, hbB[24:32], [])
            E.barrier()
            if KSTOP == "A":
                break

            for a_, b_ in list(zip(kin, kout)) + list(zip(vin, vout)):
                E.collective(lambda e, a=a_, b=b_: e.collective_compute(
                    "AllGather", OP.bypass, replica_groups=[[0, 1, 2, 3], [4, 5, 6, 7]],
                    ins=[a.opt()], outs=[b.opt()]), [], [ccB])
            if KSTOP == "X":
                E.barrier()
                break
            ut = hb[:, 16:24, :]
            utB = hbB[16:24]
            TBL = [(S[8], S[9]), (S[10], S[11])]
            TBLB = [(SB_[8], SB_[9]), (SB_[10], SB_[11])]

            ssm_rr = [0]

            def ssm_front(j, pr, init_re, init_im, initB):
                cc, j4 = pr // 4, pr % 4
                hf, jj = j4 // 2, j4 % 2
                ti = pr % 2
                cs, sn = TBL[ti]
                csB, snB = TBLB[ti]
                dma("sp", cs, tabs[pr, :, 0:512], [], [csB])
                dma("sp", sn, tabs[pr, :, 512:1024], [], [snB])
                bR = ssm_rr[0] % 6
                bI = (ssm_rr[0] + 1) % 6
                ssm_rr[0] += 2
                hs = slice(64 * hf, 64 * hf + 64)
                mm(bR, [(WB64[hs, cc, jj, 0, :], ut[hs, cc, :])], [ssmwB, utB[cc]])
                mm(bI, [(WB64[hs, cc, jj, 1, :], ut[hs, cc, :])], [ssmwB, utB[cc]])
                o = 4 * ti
                t1, t2, t3, t4 = S[o + 0], S[o + 1], S[o + 2], S[o + 3]
                tB_ = [SB_[o + 0], SB_[o + 1], SB_[o + 2], SB_[o + 3]]
                tt(t1, psum[bR][:], cs, OP.mult, [psB[bR], csB], [tB_[0]])
                tt(t2, psum[bI][:], sn, OP.mult, [psB[bI], snB], [tB_[1]])
                tt(t1, t1, t2, OP.add, [tB_[0], tB_[1]], [tB_[0]])
                tt(t3, psum[bI][:], cs, OP.mult, [psB[bI], csB], [tB_[2]])
                tt(t4, psum[bR][:], sn, OP.mult, [psB[bR], snB], [tB_[3]])
                tt(t3, t3, t4, OP.subtract, [tB_[2], tB_[3]], [tB_[2]])
                rb = rcol[:, pr:pr + 1].broadcast_to([128, 512])
                E.op("dve", lambda e: e.tensor_tensor_scan(t2, rb, t1, init_re, OP.mult, OP.add),
                     [tB_[0], ppB] + initB, [tB_[1]])
                E.op("dve", lambda e: e.tensor_tensor_scan(t4, rb, t3, init_im, OP.mult, OP.add),
                     [tB_[2], ppB] + initB, [tB_[3]])
                return t2, t4, tB_[1], tB_[3], cs, sn, csB, snB, t1, t3, tB_[0], tB_[2]

            for j in range(NB):
                cols = slice(j * T, (j + 1) * T)
                dma("sp", ut, Us[:, cols].rearrange("(c p) t -> p c t", p=128), [], utB)
                for pr in range(32):
                    wr, wi, wrB, wiB = ssm_front(j, pr, 0.0, 0.0, [])[0:4]
                    act(wend[:, 2 * pr:2 * pr + 1], wr[:, 511:512], AF.Copy, [wrB], [wendB])
                    act(wend[:, 2 * pr + 1:2 * pr + 2], wi[:, 511:512], AF.Copy, [wiB], [wendB])
                dma("sp", sin_[j:j + 1, :].rearrange("o (p q) -> (o p) q", p=128), wend[:], [wendB], [])
            E.barrier()
            E.collective(lambda e, a=sin_, b=sout: e.collective_compute(
                "AllGather", OP.bypass, replica_groups=[[0, 1, 2, 3], [4, 5, 6, 7]],
                ins=[a.opt()], outs=[b.opt()]), [], [ccB])
            E.barrier()
            if KSTOP == "S1":
                break
            EA = xf2[:, 12 * 512:16 * 512].rearrange("p (g q) -> p g q", g=32)[:, 0:4 * NB, :]
            EAB = SB_[12:16]
            HALL = xf2[:, 8 * 512:12 * 512].rearrange("p (g q) -> p g q", g=32)[:, 0:4 * NB, :]
            HB_ = SB_[8:12]
            dma("sp", EA, sout.rearrange("g (p q) -> p g q", p=128), [], EAB)
            E.op("dve", lambda e: e.memset(xf2[:, 8 * 512:12 * 512], 0.0), [], HB_)
            tmpa = S[0][:, 0:32]
            tmpb = S[0][:, 32:64]
            tmpc = S[0][:, 64:96]
            TB0 = [SB_[0]]
            for gb in range(4 * NB - 1):
                rr_, j_ = gb % 4, gb // 4
                row = rr_ * NB + j_
                Xr = HALL[:, gb, 0:64:2]
                Xi = HALL[:, gb, 1:64:2]
                Nr = HALL[:, gb + 1, 0:64:2]
                Ni = HALL[:, gb + 1, 1:64:2]
                Wr = EA[:, row, 0:64:2]
                Wi = EA[:, row, 1:64:2]
                tt(tmpa, Xr, pp[:, 6, :], OP.mult, HB_ + [ppB], TB0)
                tt(tmpa, tmpa, Wr, OP.add, TB0 + EAB, TB0)
                tt(tmpb, Xi, pp[:, 6, :], OP.mult, HB_ + [ppB], TB0)
                tt(tmpb, tmpb, Wi, OP.add, TB0 + EAB, TB0)
                tt(Nr, tmpa, pp[:, 9, :], OP.mult, TB0 + [ppB], HB_)
                tt(tmpc, tmpb, pp[:, 8, :], OP.mult, TB0 + [ppB], TB0)
                tt(Nr, Nr, tmpc, OP.subtract, HB_ + TB0, HB_)
                tt(Ni, tmpa, pp[:, 8, :], OP.mult, TB0 + [ppB], HB_)
                tt(tmpc, tmpb, pp[:, 9, :], OP.mult, TB0 + [ppB], TB0)
                tt(Ni, Ni, tmpc, OP.add, HB_ + TB0, HB_)
            for j in range(NB):
                ts(hmine[:, j, :], HALL[:, 4 * j, :], sel[:, 0:1], None, OP.mult, None, HB_ + [miscB], [hmB])
                for rr_ in range(1, 4):
                    stt(hmine[:, j, :], HALL[:, 4 * j + rr_, :], sel[:, rr_:rr_ + 1], hmine[:, j, :],
                        OP.mult, OP.add, HB_ + [miscB, hmB], [hmB])
            E.barrier()

            if KSTOP == "PFX":
                break
            kt = hb2[:, 0:4 * NT].rearrange("p (r t) -> p r t", r=4)
            ktB = hbB[0:32]
            nsr = NT // 128
            vt = xf2.bitcast(BF16)[:, 0:4 * NT].rearrange("p (n d) -> p n d", d=128)
            vtB = xfB
            qt = wbuf[0][:, 0:NT]
            qtB = [wbufB[0]]
            at = xb2.bitcast(F32).rearrange("p (c t) -> p c t", c=8)
            atB = xbB
            for h in range(8):
                dma("sp", kt, kout[h].rearrange("(r p) t -> p r t", p=128), [], ktB)
                dma("sp", qt, Qs[h * 128:(h + 1) * 128, :], [], qtB)
                vt4 = xf2.bitcast(BF16)[:, 0:4 * NT].rearrange("p (r n d) -> p r n d", r=4, d=128)
                for j_ in range(NB):
                    for rr_ in range(4):
                        dma("sp", vt4[:, rr_, j_ * 4:(j_ + 1) * 4, :],
                            vout[j_][rr_ * T:(rr_ + 1) * T, h * 128:(h + 1) * 128].rearrange("(n p) d -> p n d", p=128),
                            [], [xfB[(rr_ * NT + j_ * T) // 1024]])
                for jq in range(NB):
                    qcols = slice(jq * T, (jq + 1) * T)
                    kbs = [(rr_, j_, sbi) for j_ in range(jq + 1) for rr_ in range(4) for sbi in range(4)]
                    nk = len(kbs)
                    OB = [4, 5]
                    LBk = [6, 7]

                    def S_mm(i):
                        rr_, j_, sbi = kbs[i]
                        kc0 = j_ * T + sbi * 128
                        for m in range(2):
                            b = (i % 2) * 2 + m
                            ms = slice(64 * m, 64 * m + 64)
                            pairs = [(kt[ms, rr_, kc0:kc0 + 128], qt[ms, qcols])]
                            reads = ktB + qtB
                            if j_ == jq:
                                idx = rr_ * 4 + sbi
                                pairs.append((kmask[0:32, idx * 128:(idx + 1) * 128], qmask[0:32, :]))
                                reads = reads + [miscB]
                            mm(b, pairs, reads)

                    S_mm(0)
                    for i in range(nk):
                        if i + 1 < nk:
                            S_mm(i + 1)
                        rr_, j_, sbi = kbs[i]
                        for m in range(2):
                            b = (i % 2) * 2 + m
                            act(pt[i % 2][m][:], psum[b][:], AF.Exp, [psB[b]], [ptB[i % 2][m]], scale=0.125)
                        vi = rr_ * nsr + j_ * 4 + sbi
                        for m in range(2):
                            mm(OB[m], [(vt[:, vi, :], pt[i % 2][m][:])], vtB + [ptB[i % 2][m]],
                               start=(i == 0), stop=(i == nk - 1))
                            accB = [atB[8 + 2 * m], atB[9 + 2 * m]]
                            if i == 0:
                                E.op("dve", lambda e, o_=at[:, 4 + m, :], p_=pt[i % 2][m][:]: e.tensor_copy(o_, p_),
                                     [ptB[i % 2][m]], accB)
                            else:
                                tt(at[:, 4 + m, :], at[:, 4 + m, :], pt[i % 2][m][:], OP.add,
                                   [ptB[i % 2][m]] + accB, accB)
                    for m in range(2):
                        mm(LBk[m], [(ones32[:], at[:, 4 + m, :])], [miscB, atB[8 + 2 * m], atB[9 + 2 * m]])
                    E.op("dve", lambda e: e.reciprocal(at[:, 0, :], psum[6][:]), [psB[6]], [atB[0], atB[1]])
                    E.op("dve", lambda e: e.reciprocal(at[:, 1, :], psum[7][:]), [psB[7]], [atB[2], atB[3]])
                    tt(at[:, 2, :], psum[4][:], at[:, 0, :], OP.mult, [psB[4], atB[0], atB[1]], [atB[4], atB[5]])
                    tt(at[:, 3, :], psum[5][:], at[:, 1, :], OP.mult, [psB[5], atB[2], atB[3]], [atB[6], atB[7]])
                    stt(at[:, 2, :], at[:, 3, :], neglam, at[:, 2, :], OP.mult, OP.add,
                        [atB[4], atB[5], atB[6], atB[7], miscB], [atB[4], atB[5]])
                    act(sg[0][:], at[:, 2, :], AF.Square, [atB[4], atB[5]], [sgB[0]])
                    mm(0, [(ones[:], sg[0][:])], [miscB, sgB[0]])
                    ts(at[:, 0, :], psum[0][:], 1.0 / 128, RMS_EPS, OP.mult, OP.add, [psB[0]], [atB[0], atB[1]])
                    act(at[:, 0, :], at[:, 0, :], AF.Sqrt, [atB[0], atB[1]], [atB[0], atB[1]])
                    E.op("dve", lambda e: e.reciprocal(at[:, 0, :], at[:, 0, :]), [atB[0], atB[1]], [atB[0], atB[1]])
                    tt(at[:, 2, :], at[:, 2, :], at[:, 0, :], OP.mult, [atB[0], atB[1], atB[4], atB[5]],
                       [atB[4], atB[5]])
                    ts(sg[1][:], at[:, 2, :], gsc, None, OP.mult, None, [atB[4], atB[5], miscB], [sgB[1]])
                    dma("sp", As[h * 128:(h + 1) * 128, qcols], sg[1][:], [sgB[1]], [])
            E.barrier()

            if KSTOP == "ATT":
                break
            cat = hb[:, 0:16, :]
            ygb = hb[:, 24:32, :]
            for j in range(NB):
                cols = slice(j * T, (j + 1) * T)
                dma("sp", ut, Us[:, cols].rearrange("(c p) t -> p c t", p=128), [], utB)
                dma("sp", hb[:, 0:8, :], As[:, cols].rearrange("(c p) t -> p c t", p=128), [], hbB[0:8])
                for cc in range(8):
                    by = 6 + (cc % 2)
                    for hf in range(2):
                        prs = [4 * cc + 2 * hf, 4 * cc + 2 * hf + 1]
                        pairs = []
                        rds = [ssmwB]
                        for q_, pr in enumerate(prs):
                            ire = hmine[:, j, 2 * pr:2 * pr + 1]
                            iim = hmine[:, j, 2 * pr + 1:2 * pr + 2]
                            wr, wi, wrB, wiB, cs, sn, csB, snB, u1, u3, u1B, u3B = ssm_front(j, pr, ire, iim, [hmB])
                            xr = xpair[:, 2 * q_, :]
                            xi = xpair[:, 2 * q_ + 1, :]
                            tt(u1, wr, cs, OP.mult, [wrB, csB], [u1B])
                            tt(u3, wi, sn, OP.mult, [wiB, snB], [u3B])
                            tt(xr, u1, u3, OP.subtract, [u1B, u3B], [xpB[2 * q_]])
                            tt(u1, wr, sn, OP.mult, [wrB, snB], [u1B])
                            tt(u3, wi, cs, OP.mult, [wiB, csB], [u3B])
                            tt(xi, u1, u3, OP.add, [u1B, u3B], [xpB[2 * q_ + 1]])
                            pairs.append((WC64[:, pr, 0, :], xr))
                            pairs.append((WC64[:, pr, 1, :], xi))
                            rds += [xpB[2 * q_], xpB[2 * q_ + 1]]
                        mm(by, pairs, rds, out_ap=psum[by][64 * hf:64 * hf + 64, :])
                    stt(yf[:, cc, :], ut[:, cc, :], ssmd[:, l * 8 + cc:l * 8 + cc + 1], psum[by][:], OP.mult, OP.add,
                        [utB[cc], miscB, psB[by]], [xbB[2 * cc], xbB[2 * cc + 1]])
                    act(yf[:, cc, :], yf[:, cc, :], AF.Gelu_apprx_tanh, [xbB[2 * cc], xbB[2 * cc + 1]],
                        [xbB[2 * cc], xbB[2 * cc + 1]])
                    act(ygb[:, cc, :], yf[:, cc, :], AF.Copy, [xbB[2 * cc], xbB[2 * cc + 1]], [hbB[24 + cc]])

                def out_glu(oc, pairs, reads):
                    b = bank()
                    mm(b, pairs, reads)
                    act(sg[oc % 2][:], psum[b][:], AF.Sigmoid, [psB[b], miscB], [sgB[oc % 2]],
                        bias=glub[:, l * 8 + oc:l * 8 + oc + 1])
                    tt(hb[:, 8 + oc, :], yf[:, oc, :], sg[oc % 2][:], OP.mult,
                       [xbB[2 * oc], xbB[2 * oc + 1], sgB[oc % 2]], [hbB[8 + oc]])
                linear("glu", l, lambda kc: ygb[:, kc, :], hbB[24:32], out_glu)

                dma("sp", xf[:], xs[:, cols].rearrange("(c p) t -> p c t", p=128), [], xfB)
                ts(xf2, xf2, ALPHA, None, OP.mult, None, xfB, xfB)

                def out_res(oc, pairs, reads):
                    b = bank()
                    mm(b, pairs, reads)
                    tt(xf[:, oc, :], psum[b][:], xf[:, oc, :], OP.add, [psB[b], xfB[oc]], [xfB[oc]])
                if dbg:
                    dma("sp", dbg_cat[:, cols].rearrange("(c p) t -> p c t", p=128), hb[:, 0:16, :], hbB[0:16], [])
                    dma("sp", dbg_x1[:, cols].rearrange("(c p) t -> p c t", p=128), xf[:], xfB, [])
                linear("wout", l, lambda kc: hb[:, kc, :], hbB[0:16], out_res)
                layer_norm(l, 1)
                if dbg:
                    dma("sp", dbg_x2[:, cols].rearrange("(c p) t -> p c t", p=128), xf[:], xfB, [])
                ts(xf2, xf2, ALPHA, None, OP.mult, None, xfB, xfB)

                def out_q(oc, pairs, reads):
                    b = bank()
                    mm(b, pairs, reads)
                    act(hb[:, oc, :], psum[b][:], AF.Copy, [psB[b]], [hbB[oc]])
                linear("xq", l, lambda kc: xb[:, kc, :], xbB, out_q)
                xsc = 512.0 ** -0.5
                for hh in range(4):
                    for mh in range(2):
                        b = bank()
                        mm(b, [(kmT[:, 4 * hh + dd, mh * 128:(mh + 1) * 128], hb[:, 4 * hh + dd, :]) for dd in range(4)],
                           [kvmB] + hbB[4 * hh:4 * hh + 4])
                        act(pt[0][mh][:], psum[b][:], AF.Exp, [psB[b]], [ptB[0][mh]], scale=xsc)
                    bl = bank()
                    mm(bl, [(ones[:], pt[0][mh][:]) for mh in range(2)], [miscB, ptB[0][0], ptB[0][1]])
                    rl = lnt[:, 0, :]
                    E.op("dve", lambda e, bl=bl, rl=rl: e.reciprocal(rl, psum[bl][:]), [psB[bl]],
                         hbB[32:34] + [lntB[0]])
                    for dd in range(4):
                        b = bank()
                        mm(b, [(vmem[:, mh, (4 * hh + dd) * 128:(4 * hh + dd + 1) * 128], pt[0][mh][:]) for mh in range(2)],
                           [kvmB, ptB[0][0], ptB[0][1]])
                        tt(hb[:, 16 + 4 * hh + dd, :], psum[b][:], rl, OP.mult, [psB[b], lntB[0]] + hbB[32:34],
                           [hbB[16 + 4 * hh + dd]])
                linear("xo", l, lambda kc: hb[:, 16 + kc, :], hbB[16:32], out_res)
                layer_norm(l, 2)
                if dbg:
                    dma("sp", dbg_x3[:, cols].rearrange("(c p) t -> p c t", p=128), xf[:], xfB, [])
                ts(xf2, xf2, ALPHA, None, OP.mult, None, xfB, xfB)

                ffn(l, "gu2", "d2")
                layer_norm(l, 3)
                dst = outT if l == L - 1 else xs
                dma("sp", dst[:, cols].rearrange("(c p) t -> p c t", p=128), xf[:], xfB, [])
            E.barrier()

        with nc.Block() as block:
            def replay(eng, name):
                for waits, fn, inc in E.s[name].ops:
                    for sem, val in waits:
                        eng.wait_ge(sems[sem], val)
                    if fn is None:
                        continue
                    ins = fn(eng)
                    if inc[1] is None:
                        ins.then_inc(sems[inc[0]])
                    else:
                        ins.then_inc(sems[inc[0]], inc[1])

            @block.tensor
            def _(e):
                replay(e, "pe")

            @block.scalar
            def _(e):
                replay(e, "act")

            @block.vector
            def _(e):
                replay(e, "dve")

            @block.gpsimd
            def _(e):
                replay(e, "pool")

            @block.sync
            def _(e):
                replay(e, "sp")

    return nc


def _gt(W, Kc, gc):
    K, N = W.shape
    G = N // gc
    return np.ascontiguousarray(W.reshape(Kc, 128, G, gc).transpose(2, 1, 0, 3)).reshape(G * 128, Kc * gc)


def _gt_ksplit(W, Kc, gc):
    K, N = W.shape
    G = N // gc
    a = W.reshape(2, Kc, 128, G, gc).transpose(3, 0, 2, 1, 4)
    return np.ascontiguousarray(a).reshape(G * 2 * 128, Kc * gc)


def _interleave_gu(wg, wu):
    K, Fd = wg.shape
    out = np.empty((K, 2 * Fd), np.float32)
    o = out.reshape(K, Fd // 128, 2, 128)
    o[:, :, 0, :] = wg.reshape(K, Fd // 128, 128)
    o[:, :, 1, :] = wu.reshape(K, Fd // 128, 128)
    return out


_CACHE = {}
_DBG = False
_LAST = {}


def kernel(**inp):
    x = np.asarray(inp["x"], np.float32)
    mem = np.asarray(inp["mem"], np.float32)
    Bsz, SEQ, _ = x.shape
    L = inp["w_in"].shape[0]
    NB = SEQ // (4 * T)
    NT = NB * T
    key = (NB, L)
    if key not in _CACHE:
        _CACHE[key] = build_nc(NB, L, dbg=_DBG)
    nc = _CACHE[key]

    f32 = lambda a: np.asarray(a, np.float32)
    shared = {}
    ws = {k: [] for k in WK}
    for l in range(L):
        ws["gu1"].append(_gt(_interleave_gu(f32(inp["ffn1_w_gate"][l]), f32(inp["ffn1_w_up"][l])), 16, 256))
        ws["d1"].append(_gt_ksplit(f32(inp["ffn1_w_down"][l]), 22, 128))
        ws["win"].append(_gt(f32(inp["w_in"][l]), 16, 256))
        ws["glu"].append(_gt(f32(inp["ssm_glu_w"][l]), 8, 512))
        ws["wout"].append(_gt(f32(inp["w_out"][l]), 16, 256))
        ws["xq"].append(_gt(f32(inp["xattn_w_q"][l]), 16, 256))
        ws["xk"].append(_gt(f32(inp["xattn_w_k"][l]), 16, 256))
        ws["xv"].append(_gt(f32(inp["xattn_w_v"][l]), 16, 256))
        ws["xo"].append(_gt(f32(inp["xattn_w_o"][l]), 16, 256))
        ws["gu2"].append(_gt(_interleave_gu(f32(inp["ffn2_w_gate"][l]), f32(inp["ffn2_w_up"][l])), 16, 256))
        ws["d2"].append(_gt_ksplit(f32(inp["ffn2_w_down"][l]), 22, 128))
    for k in WK:
        shared["w_" + k] = np.stack(ws[k], 0)

    def chunkT(v):
        return f32(v).reshape(-1, 128).T

    lnp = np.zeros((128, L, 8, 16), np.float32)
    for l in range(L):
        for i, nm in enumerate(["ln1", "ln2", "ln3", "ln4"]):
            lnp[:, l, 2 * i, :] = chunkT(inp[nm + "_g"][l])
            lnp[:, l, 2 * i + 1, :] = chunkT(inp[nm + "_b"][l])
    shared["lnp"] = lnp.reshape(128, -1)
    lamv = np.zeros((128, L, 4, 64), np.float32)
    for l in range(L):
        for i, nm in enumerate(["lambda_q1", "lambda_k1", "lambda_q2", "lambda_k2"]):
            lamv[:, l, i, :] = f32(inp[nm][l])[None, :]
    shared["lamv"] = lamv.reshape(128, -1)
    shared["dng"] = np.ascontiguousarray(f32(inp["diff_norm_g"]).T)
    shared["ssmd"] = np.concatenate([chunkT(inp["ssm_d"][l]) for l in range(L)], 1)
    shared["glub"] = np.concatenate([chunkT(inp["ssm_glu_b"][l]) for l in range(L)], 1)
    spair = np.zeros((128, L, 3, 32), np.float32)
    scm = np.zeros((128, L, 3, 8, 64), np.float32)
    ssmb = np.zeros((128, L, 2, 8, 64), np.float32)
    ssmc = np.zeros((128, L, 2, 32, 16), np.float32)
    for l in range(L):
        lre, lim, lst = f32(inp["ssm_lambda_re"][l]), f32(inp["ssm_lambda_im"][l]), f32(inp["ssm_log_step"][l])
        lstb = np.broadcast_to(lst[:, None], (64, 64))
        for i, a in enumerate([lre, lim, lstb]):
            spair[:, l, i, :] = a.reshape(32, 2, 64).transpose(1, 2, 0).reshape(128, 32)
            rep = np.repeat(a, 16, axis=0)
            scm[:, l, i] = rep.reshape(8, 128, 64).transpose(1, 0, 2)
        for i, nm in enumerate(["ssm_b_re", "ssm_b_im"]):
            b = f32(inp[nm][l])
            bc = b.transpose(0, 2, 1).reshape(1024, 64)
            ssmb[:, l, i] = bc.reshape(8, 128, 64).transpose(1, 0, 2)
        for i, nm in enumerate(["ssm_c_re", "ssm_c_im"]):
            c = f32(inp[nm][l])
            ssmc[:, l, i] = c.reshape(32, 2, 16, 64).transpose(1, 3, 0, 2).reshape(128, 32, 16)
    shared["spair"] = spair.reshape(128, -1)
    shared["scm"] = scm.reshape(128, -1)
    shared["ssmb"] = ssmb.reshape(128, -1)
    shared["ssmc"] = ssmc.reshape(128, -1)
    shared["iota1"] = np.broadcast_to(np.arange(1, 513, dtype=np.float32)[None, :], (128, 512)).copy()
    km = np.zeros((32, 4, 4, 128), np.float32)
    for rr in range(4):
        for sbi in range(4):
            for kk in range(128):
                km[rr * 8 + sbi * 2 + kk // 64, rr, sbi, kk] = 1.0
    shared["kmask"] = km.reshape(32, 2048)
    bm = np.zeros((128, 4), np.float32)
    for q in range(128):
        jp, gi = q // 32, (q // 16) % 2
        for jj in range(2):
            for g2 in range(2):
                bm[q, jj * 2 + g2] = 1.0 if (jp % 2 == jj and gi == g2) else 0.0
    shared["bmask"] = bm

    in_maps = []
    for c in range(8):
        b, r = c // 4, c % 4
        xb_ = x[b].reshape(NB, 4, T, D)[:, r].reshape(NT, D)
        m = dict(shared)
        m["xT"] = np.ascontiguousarray(xb_.T)
        m["memT"] = np.ascontiguousarray(mem[b].T)
        qm = np.zeros((32, 512), np.float32)
        for cidx in range(32):
            for q in range(512):
                if r * 8 + q // 64 < cidx:
                    qm[cidx, q] = NEGM
        m["qmask"] = qm
        s_ = np.zeros((128, 4), np.float32)
        s_[:, r] = 1.0
        m["sel"] = s_
        in_maps.append(m)

    res = run_bass_kernel_spmd(nc, in_maps, core_ids=list(range(8)))
    if _DBG:
        _LAST["res"] = res
    out = np.empty((Bsz, SEQ, D), np.float32)
    for c in range(8):
        b, r = c // 4, c % 4
        oT = np.asarray(res.results[c]["outT"])
        out[b].reshape(NB, 4, T, D)[:, r] = oT.T.reshape(NB, T, D)
    return out
```
